# Optimizing a Trainium2 kernel written in Bass

```python
import jax, jax.numpy as jnp
from jax import lax
import numpy as np

D_MODEL = 1024
BATCH = 8
SEQ = 4096
DEPTH = 2

CTX_LEN = 256
GRID_W = 64

HG_HEADS = 4
HG_DK = 64
HG_DV = 64
HG_W = HG_HEADS * HG_DV
MLA_HEADS = 4
MLA_NOPE = 128
MLA_ROPE = 64
MLA_DV = 128
MLA_Q_RANK = 256
MLA_KV_RANK = 128
MLA_QK = MLA_NOPE + MLA_ROPE
MLA_W = MLA_HEADS * MLA_DV
MLA_SCALE = MLA_QK ** -0.5
GLA_HEADS = 4
GLA_DK = 32
GLA_DV = 64
GLA_GATE_RANK = 16
GLA_TAU = 16.0
GLA_W = GLA_HEADS * GLA_DV

MIX_W = HG_W + MLA_W + GLA_W
CHUNK = 64
Q_BLOCK = 128
ROPE_BASE = 10000.0
ROPE_FREQS = MLA_ROPE // 4
NORM_EPS = 1e-6
FORGET_MIN = 1e-6
DEEPNORM_ALPHA = (2 * DEPTH) ** 0.25
DEEPNORM_BETA = (8 * DEPTH) ** -0.25

IN_SPLITS = (HG_HEADS * HG_DK, HG_HEADS * HG_DK, HG_HEADS * HG_DK, HG_W, HG_W,
             MLA_Q_RANK, MLA_KV_RANK, MLA_ROPE, MLA_W,
             GLA_HEADS * GLA_DK, GLA_HEADS * GLA_DK, GLA_W, GLA_GATE_RANK, GLA_GATE_RANK, GLA_W)
IN_W = sum(IN_SPLITS)

kernel_name = "hymba_hgrn2_mla_gla_prefix_trunk"

F32 = jnp.float32


def split_cols(p):
    idx = [int(i) for i in np.cumsum(IN_SPLITS)[:-1]]
    return jnp.split(p, idx, axis=-1)


def layer_norm(x):
    x32 = x.astype(F32)
    mu = jnp.mean(x32, axis=-1, keepdims=True)
    var = jnp.mean(jnp.square(x32 - mu), axis=-1, keepdims=True)
    return ((x32 - mu) * lax.rsqrt(var + NORM_EPS)).astype(x.dtype)


def rms_norm(x, g):
    x32 = x.astype(F32)
    y = x32 * lax.rsqrt(jnp.mean(x32 * x32, axis=-1, keepdims=True) + NORM_EPS)
    return y.astype(x.dtype) * g


def axial_rope_tables(n):
    rows = n // GRID_W
    pos_r = jnp.repeat(jnp.arange(rows, dtype=F32), GRID_W)
    pos_c = jnp.tile(jnp.arange(GRID_W, dtype=F32), rows)
    inv = 1.0 / (ROPE_BASE ** (jnp.arange(ROPE_FREQS, dtype=F32) / ROPE_FREQS))
    ang = jnp.stack([pos_r, pos_c], axis=-1)[:, :, None] * inv
    return jnp.cos(ang), jnp.sin(ang)


def apply_axial_rope(x, cos, sin):
    xa = x.reshape(x.shape[:-1] + (2, 2, ROPE_FREQS))
    x1, x2 = xa[..., 0, :], xa[..., 1, :]
    cos = cos.astype(x.dtype)
    sin = sin.astype(x.dtype)
    out = jnp.stack([x1 * cos - x2 * sin, x2 * cos + x1 * sin], axis=-2)
    return out.reshape(x.shape)


def gated_recurrence(q, k, v, log_g, s0):
    B, L, H, _ = q.shape
    dv = v.shape[-1]
    nc = L // CHUNK

    def chunks(a):
        return a.astype(F32).reshape(B, nc, CHUNK, H, a.shape[-1]).transpose(1, 0, 3, 2, 4)

    lower = jnp.tril(jnp.ones((CHUNK, CHUNK), dtype=bool))[:, :, None]

    def step(s, inp):
        qc, kc, vc, gc = inp
        b = jnp.cumsum(gc, axis=2)
        diff = b[:, :, :, None, :] - b[:, :, None, :, :]
        decay = jnp.where(lower, jnp.exp(jnp.where(lower, diff, 0.0)), 0.0)
        scores = jnp.einsum('bhid,bhjd,bhijd->bhij', qc, kc, decay)
        o = (jnp.einsum('bhij,bhjv->bhiv', scores, vc)
             + jnp.einsum('bhid,bhdv->bhiv', qc * jnp.exp(b), s))
        b_end = b[:, :, -1:, :]
        s = (s * jnp.exp(b_end[:, :, 0, :])[..., None]
             + jnp.einsum('bhjd,bhjv->bhdv', kc * jnp.exp(b_end - b), vc))
        return s, o

    s_fin, o = lax.scan(step, s0.astype(F32), tuple(chunks(a) for a in (q, k, v, log_g)))
    o = o.transpose(1, 0, 3, 2, 4).reshape(B, L, H, dv)
    return o.astype(v.dtype), s_fin


def bidirectional_recurrence(lat, ctx, need_ctx_out):
    q0 = lat[0]
    B, H, dk = q0.shape[0], q0.shape[2], q0.shape[3]
    dv = lat[3].shape[3]
    s0 = jnp.zeros((B, H, dk, dv), F32)

    def scan_dir(inp, direction, s_init):
        q, k_f, k_b, v, g_f, g_b = inp
        k, g = (k_f, g_f) if direction == 0 else (k_b, g_b)
        if direction == 1:
            q, k, v, g = (jnp.flip(a, axis=1) for a in (q, k, v, g))
        o, s = gated_recurrence(q, k, v, g, s_init)
        if direction == 1:
            o = jnp.flip(o, axis=1)
        return o, s

    o_cf, s_cf = scan_dir(ctx, 0, s0)
    o_cb, s_cb = scan_dir(ctx, 1, s0)
    o_lf, _ = scan_dir(lat, 0, s_cf)
    o_lb, _ = scan_dir(lat, 1, s_cb)
    o_ctx = (o_cf + o_cb) if need_ctx_out else None
    return o_lf + o_lb, o_ctx


def hgrn2_inputs(zq, zf_fwd, zf_bwd, zi, lb):
    B, L, _ = zq.shape
    q = jax.nn.silu(zq).reshape(B, L, HG_HEADS, HG_DK)
    ks, gs = [], []
    for d, z in enumerate((zf_fwd, zf_bwd)):
        z32 = z.astype(F32)
        lbd = lb[d]
        f = lbd + (1.0 - lbd) * jax.nn.sigmoid(z32)
        log_f = jnp.log(jnp.maximum(f, FORGET_MIN))
        k = (1.0 - lbd) * jax.nn.sigmoid(-z32)
        ks.append(k.reshape(B, L, HG_HEADS, HG_DK))
        gs.append(log_f.reshape(B, L, HG_HEADS, HG_DK))
    v = zi.reshape(B, L, HG_HEADS, HG_DV)
    return (q, ks[0], ks[1], v, gs[0], gs[1])


def gla_inputs(zq, zk, zv, za_fwd, za_bwd, w_a2, b_a):
    B, L, _ = zq.shape
    q = zq.reshape(B, L, GLA_HEADS, GLA_DK) * (GLA_DK ** -0.5)
    k = zk.reshape(B, L, GLA_HEADS, GLA_DK)
    v = zv.reshape(B, L, GLA_HEADS, GLA_DV)
    gs = [(jax.nn.log_sigmoid((za @ w_a2[d] + b_a[d]).astype(F32)) / GLA_TAU)
          .reshape(B, L, GLA_HEADS, GLA_DK) for d, za in enumerate((za_fwd, za_bwd))]
    return (q, k, k, v, gs[0], gs[1])


def gated_head_norm(o, g_norm, gate):
    B, L, H, dv = o.shape
    return rms_norm(o, g_norm).reshape(B, L, H * dv) * jax.nn.silu(gate)


def mla_qkv(zcq, zckv, zkr, q_norm_g, kv_norm_g, w_uq, w_ukv, rope):
    B, L, _ = zcq.shape
    q = (rms_norm(zcq, q_norm_g) @ w_uq).reshape(B, L, MLA_HEADS, MLA_QK)
    kv = (rms_norm(zckv, kv_norm_g) @ w_ukv).reshape(B, L, MLA_HEADS, MLA_NOPE + MLA_DV)
    q_nope, q_rope = q[..., :MLA_NOPE], q[..., MLA_NOPE:]
    k_nope, v = kv[..., :MLA_NOPE], kv[..., MLA_NOPE:]
    k_rope = zkr
    if rope is not None:
        cos, sin = rope
        q_rope = apply_axial_rope(q_rope, cos[:, None], sin[:, None])
        k_rope = apply_axial_rope(k_rope, cos, sin)
    q = jnp.concatenate([q_nope, q_rope], axis=-1)
    k = jnp.concatenate(
        [k_nope, jnp.broadcast_to(k_rope[:, :, None, :], (B, L, MLA_HEADS, MLA_ROPE))], axis=-1)
    return q, k, v


def softmax_attend(q, k, v):
    s = jnp.einsum('bqhd,bkhd->bhqk', q, k).astype(F32) * MLA_SCALE
    p = jax.nn.softmax(s, axis=-1).astype(v.dtype)
    return jnp.einsum('bhqk,bkhd->bqhd', p, v)


def blockwise_attend(q, k, v):
    B, N, H, dq = q.shape
    nb = N // Q_BLOCK
    qb = q.reshape(B, nb, Q_BLOCK, H, dq).transpose(1, 0, 2, 3, 4)
    o = lax.map(lambda qblk: softmax_attend(qblk, k, v), qb)
    return o.transpose(1, 0, 2, 3, 4).reshape(B, N, H, v.shape[-1])


def trunk_layer(x, xc, mod_lat, mod_ctx, w_in, w_out, ln_g, ln_b, hg_lb, hg_norm_g,
                mla_q_norm_g, mla_kv_norm_g, mla_w_uq, mla_w_ukv, gla_w_a2, gla_b_a,
                gla_norm_g, rope, need_ctx_out):
    B, N, _ = x.shape
    Lc = xc.shape[1]
    shift, scale, gate = jnp.split(mod_lat[:, None, :], 3, axis=-1)
    shift_c, scale_c, gate_c = jnp.split(mod_ctx, 3, axis=-1)
    h = layer_norm(x) * (1.0 + scale) + shift
    hc = layer_norm(xc) * (1.0 + scale_c) + shift_c

    (hq, hff, hfb, hi, hgate, mcq, mckv, mkr, mgate,
     gq, gk, gv, gaf, gab, ggate) = split_cols(h @ w_in)
    (hq_c, hff_c, hfb_c, hi_c, hgate_c, mcq_c, mckv_c, mkr_c, mgate_c,
     gq_c, gk_c, gv_c, gaf_c, gab_c, ggate_c) = split_cols(hc @ w_in)

    hg_lat, hg_ctx = bidirectional_recurrence(
        hgrn2_inputs(hq, hff, hfb, hi, hg_lb),
        hgrn2_inputs(hq_c, hff_c, hfb_c, hi_c, hg_lb), need_ctx_out)
    gla_lat, gla_ctx = bidirectional_recurrence(
        gla_inputs(gq, gk, gv, gaf, gab, gla_w_a2, gla_b_a),
        gla_inputs(gq_c, gk_c, gv_c, gaf_c, gab_c, gla_w_a2, gla_b_a), need_ctx_out)
    q, k, v = mla_qkv(mcq, mckv, mkr, mla_q_norm_g, mla_kv_norm_g, mla_w_uq, mla_w_ukv, rope)
    q_c, k_c, v_c = mla_qkv(mcq_c, mckv_c, mkr_c, mla_q_norm_g, mla_kv_norm_g,
                            mla_w_uq, mla_w_ukv, None)
    mla_lat = blockwise_attend(q, jnp.concatenate([k_c, k], axis=1),
                               jnp.concatenate([v_c, v], axis=1))

    y = jnp.concatenate([gated_head_norm(hg_lat, hg_norm_g, hgate),
                         mla_lat.reshape(B, N, MLA_W) * jax.nn.silu(mgate),
                         gated_head_norm(gla_lat, gla_norm_g, ggate)], axis=-1)
    x_new = layer_norm(DEEPNORM_ALPHA * x + gate * (y @ w_out)) * ln_g + ln_b
    if not need_ctx_out:
        return x_new, xc

    mla_ctx = softmax_attend(q_c, k_c, v_c)
    yc = jnp.concatenate([gated_head_norm(hg_ctx, hg_norm_g, hgate_c),
                          mla_ctx.reshape(B, Lc, MLA_W) * jax.nn.silu(mgate_c),
                          gated_head_norm(gla_ctx, gla_norm_g, ggate_c)], axis=-1)
    xc_new = layer_norm(DEEPNORM_ALPHA * xc + gate_c * (yc @ w_out)) * ln_g + ln_b
    return x_new, xc_new


def setup_inputs(seed: int = 0) -> dict:
    key = jax.random.key(seed)
    ks = jax.random.split(key, 19)

    def nrm(k, shape, s):
        return jax.random.normal(k, shape, F32) * s

    D = D_MODEL
    return {
        "x": nrm(ks[0], (BATCH, SEQ, D), 1.0),
        "c": nrm(ks[1], (BATCH, D), 1.0),
        "ctx": nrm(ks[2], (BATCH, CTX_LEN, D), 1.0),
        "c_ctx": nrm(ks[3], (D,), 1.0),
        "w_mod": nrm(ks[4], (DEPTH, D, 3 * D), 0.5 * D ** -0.5),
        "b_mod": nrm(ks[5], (DEPTH, 3 * D), 0.02),
        "w_in": nrm(ks[6], (DEPTH, D, IN_W), D ** -0.5),
        "w_out": nrm(ks[7], (DEPTH, MIX_W, D), DEEPNORM_BETA * MIX_W ** -0.5),
        "ln_g": 1.0 + nrm(ks[8], (DEPTH, D), 0.02),
        "ln_b": nrm(ks[9], (DEPTH, D), 0.02),
        "hg_lb_logits": nrm(ks[10], (DEPTH, 2, HG_HEADS * HG_DK), 0.5),
        "hg_norm_g": 1.0 + nrm(ks[11], (DEPTH, HG_DV), 0.02),
        "mla_q_norm_g": 1.0 + nrm(ks[12], (DEPTH, MLA_Q_RANK), 0.02),
        "mla_kv_norm_g": 1.0 + nrm(ks[13], (DEPTH, MLA_KV_RANK), 0.02),
        "mla_w_uq": nrm(ks[14], (DEPTH, MLA_Q_RANK, MLA_HEADS * MLA_QK), MLA_Q_RANK ** -0.5),
        "mla_w_ukv": nrm(ks[15], (DEPTH, MLA_KV_RANK, MLA_HEADS * (MLA_NOPE + MLA_DV)),
                          MLA_KV_RANK ** -0.5),
        "gla_w_a2": nrm(ks[16], (DEPTH, 2, GLA_GATE_RANK, GLA_HEADS * GLA_DK),
                         GLA_GATE_RANK ** -0.5),
        "gla_b_a": nrm(ks[17], (DEPTH, 2, GLA_HEADS * GLA_DK), 0.1),
        "gla_norm_g": 1.0 + nrm(ks[18], (DEPTH, GLA_DV), 0.02),
    }


def reference(x, c, ctx, c_ctx, w_mod, b_mod, w_in, w_out, ln_g, ln_b, hg_lb_logits,
              hg_norm_g, mla_q_norm_g, mla_kv_norm_g, mla_w_uq, mla_w_ukv, gla_w_a2,
              gla_b_a, gla_norm_g):
    rope = axial_rope_tables(x.shape[1])
    lb_soft = jax.nn.softmax(hg_lb_logits.astype(F32), axis=0)
    hg_lb = jnp.clip(jnp.cumsum(lb_soft, axis=0) - lb_soft[0:1], 0.0, 1.0)
    silu_c = jax.nn.silu(c)
    silu_cc = jax.nn.silu(c_ctx)
    xc = ctx
    for l in range(DEPTH):
        mod_lat = silu_c @ w_mod[l] + b_mod[l]
        mod_ctx = silu_cc @ w_mod[l] + b_mod[l]
        x, xc = trunk_layer(x, xc, mod_lat, mod_ctx, w_in[l], w_out[l], ln_g[l], ln_b[l],
                            hg_lb[l].astype(x.dtype), hg_norm_g[l], mla_q_norm_g[l],
                            mla_kv_norm_g[l], mla_w_uq[l], mla_w_ukv[l], gla_w_a2[l],
                            gla_b_a[l], gla_norm_g[l], rope, l < DEPTH - 1)
    return x
```

```python
import numpy as np
import ml_dtypes
import concourse.bass as bass
import concourse.mybir as mybir
from concourse.bass_utils import run_bass_kernel_spmd

F32 = mybir.dt.float32
BF16 = mybir.dt.bfloat16
AF = mybir.ActivationFunctionType
ALU = mybir.AluOpType

T = 4352
NT = 34
NCH = 68
D = 1024
LCTX = 256
NLAT = 4096
GROUPS = [(0, 256)] + [(256 + 512 * i, 512) for i in range(8)]
WCOLS = 3104
(C_ZQ, C_ZFF, C_ZFB, C_HG, C_ZCQ, C_ZCKV, C_KR, C_KRS, C_MG, C_GQ, C_GK, C_GG, C_GAF, C_GAB,
 C_HI, C_GV) = (0, 256, 512, 768, 1024, 1280, 1408, 1472, 1536, 2048, 2176, 2304, 2560, 2576,
                2592, 2848)
EPS = 1e-6
ALPHA = 4.0 ** 0.25
MLA_SCALE = 192.0 ** -0.5
NVEC = 22 + 48
SAME_ENGINE_SYNC = True
SEM_CAP = 30000
DMA_POOL = 20


class _Op:
    __slots__ = ("eng", "fn", "deps", "sig", "sem", "val", "waits", "dma", "idx")


class _Rec:
    def __getattr__(self, name):
        def f(*a, **k):
            self.__dict__["call"] = (name, a, k)
        return f


class Sched:
    ENG = ("sync", "scalar", "vector", "gpsimd", "tensor")

    def __init__(self):
        self.ops = []
        self.lastw = {}
        self.readers = {}
        self.last_eng = {}
        self.dmas_since_bar = []

    ALIAS = {}

    def _expand(self, keys):
        out = []
        for k in keys:
            out.extend(self.ALIAS.get(k, (k,)))
        return out

    cur = None
    bufs = None

    def capture(self, name):
        if self.bufs is None:
            self.bufs = {}
        self.cur = name
        if name is not None:
            self.bufs.setdefault(name, [])

    def flush_interleaved(self, names, blk=3):
        self.cur = None
        L = [self.bufs.pop(nm, []) for nm in names]
        L = [x for x in L if x]
        idx = [0] * len(L)
        while any(idx[i] < len(L[i]) for i in range(len(L))):
            cand = [i for i in range(len(L)) if idx[i] < len(L[i])]
            i = min(cand, key=lambda i_: idx[i_] / len(L[i_]))
            for _ in range(blk):
                if idx[i] < len(L[i]):
                    o = L[i][idx[i]]
                    self.add(o[0], o[1], r=o[2], w=o[3], dma=o[4], _raw=True)
                    idx[i] += 1

    def add(self, eng, fn, r=(), w=(), dma=False, _raw=False):
        if not _raw:
            r = self._expand(r)
            w = self._expand(w)
            if fn is not None:
                rec = _Rec()
                fn(rec)
                fn = rec.__dict__["call"]
            if self.cur is not None:
                self.bufs[self.cur].append((eng, fn, r, w, dma))
                return None
        op = _Op()
        op.eng, op.fn, op.dma = eng, fn, dma
        op.sig, op.sem, op.val, op.waits = False, None, 0, []
        op.idx = len(self.ops)
        deps = set()
        for k in r:
            x = self.lastw.get(k)
            if x is not None:
                deps.add(x)
        for k in w:
            x = self.lastw.get(k)
            if x is not None:
                deps.add(x)
            for x in self.readers.get(k, ()):
                deps.add(x)
        op.deps = deps
        for k in r:
            self.readers.setdefault(k, []).append(op)
        for k in w:
            self.lastw[k] = op
            self.readers[k] = []
        self.ops.append(op)
        self.last_eng[eng] = op
        if dma:
            self.dmas_since_bar.append(op)
        return op

    def barrier(self):
        lasts = [o for o in self.last_eng.values()]
        dm = list(self.dmas_since_bar)
        self.dmas_since_bar = []
        for e in self.ENG:
            op = self.add(e, None)
            op.deps = set(lasts) | set(dm)

    def finalize(self, nc, stack):
        ops = self.ops
        pools = {}
        dcount = {}
        last_on_sem = {}
        for op in ops:
            if op.dma:
                n = dcount.get(op.eng, 0)
                dcount[op.eng] = n + 1
                if op.eng not in pools:
                    pools[op.eng] = [stack.enter_context(nc.semaphore(f"d_{op.eng}_{i}"))
                                     for i in range(DMA_POOL)]
                j = n % DMA_POOL
                op.sem = pools[op.eng][j]
                op.val = 16 * (n // DMA_POOL + 1)
                prev = last_on_sem.get((op.eng, j))
                if prev is not None:
                    op.deps.add(prev)
                last_on_sem[(op.eng, j)] = op
                op.sig = True
        for op in ops:
            for d in op.deps:
                if not d.dma:
                    d.sig = True
        cnt = {}
        esems = {}
        for op in ops:
            if op.dma or not op.sig or op.fn is None:
                continue
            c = cnt.get(op.eng, 0)
            cnt[op.eng] = c + 1
            si = c // SEM_CAP
            lst = esems.setdefault(op.eng, [])
            if si >= len(lst):
                lst.append(stack.enter_context(nc.semaphore(f"e_{op.eng}_{si}")))
            op.sem = lst[si]
            op.val = c % SEM_CAP + 1
        waited = {e: {} for e in self.ENG}
        for op in ops:
            wl = waited[op.eng]
            for d in sorted(op.deps, key=lambda o: o.idx):
                if d.fn is None:
                    continue
                if d.eng == op.eng and not d.dma:
                    if op.eng == "tensor":
                        continue
                    if not SAME_ENGINE_SYNC and not op.dma:
                        continue
                key = id(d.sem)
                if wl.get(key, 0) >= d.val:
                    continue
                wl[key] = d.val
                op.waits.append((d.sem, d.val))
        self.per_eng = {e: [o for o in ops if o.eng == e] for e in self.ENG}

    def emit(self, block):
        def mk(name):
            lst = self.per_eng[name]

            def body(e):
                for op in lst:
                    for s, v in op.waits:
                        e.wait_ge(s, v)
                    if op.fn is not None:
                        name_, a_, k_ = op.fn
                        ins = getattr(e, name_)(*a_, **k_)
                        if op.dma:
                            ins.then_inc(op.sem, 16)
                        elif op.sig:
                            ins.then_inc(op.sem, 1)
            return body
        block.sync(mk("sync"))
        block.scalar(mk("scalar"))
        block.vector(mk("vector"))
        block.gpsimd(mk("gpsimd"))
        block.tensor(mk("tensor"))


class Arena:
    def __init__(self, ap_bf16, nbytes):
        self.ap = ap_bf16
        self.nbytes = nbytes
        self.off = 0
        self.peak = 0

    def reset(self):
        self.off = 0

    def alloc(self, shape, dtype, parts=128):
        es = 4 if dtype == F32 else 2
        nel = int(np.prod(shape[1:]))
        nb = (nel * es + 31) // 32 * 32
        assert self.off + nb <= self.nbytes, f"arena overflow {self.off}+{nb}>{self.nbytes}"
        a = self.ap[0:shape[0], self.off // 2:(self.off + nel * es) // 2]
        self.off += nb
        self.peak = max(self.peak, self.off)
        if dtype == F32:
            a = a.bitcast(F32)
        if len(shape) == 3:
            a = a.rearrange("p (a b) -> p a b", a=shape[1], b=shape[2])
        elif len(shape) == 4:
            a = a.rearrange("p (a b c) -> p a b c", a=shape[1], b=shape[2], c=shape[3])
        elif len(shape) == 5:
            a = a.rearrange("p (a b c d) -> p a b c d", a=shape[1], b=shape[2], c=shape[3],
                            d=shape[4])
        return a


class Rot:
    def __init__(self, arena, name, n, shape, dtype):
        self.bufs = [arena.alloc(shape, dtype) for _ in range(n)]
        self.name = name
        self.i = 0

    def get(self):
        j = self.i % len(self.bufs)
        self.i += 1
        return self.bufs[j], (self.name, j)


def build_program(debug=False, nlayers=2, stop_after=None, a_groups=9, a_stage=9):
    nc = bass.Bass("TRN2", target_bir_lowering=False)
    Sched.ALIAS = {}
    for b_ in (2, 3, 4, 6):
        Sched.ALIAS[f"ps{b_}"] = [f"ps{b_}"] + [(f"ps{b_}", s_) for s_ in range(4)]
    for b_ in (0, 1):
        Sched.ALIAS[f"ps{b_}"] = [f"ps{b_}"] + [(f"ps{b_}", s_) for s_ in range(2)]
    from contextlib import ExitStack
    stack = ExitStack()
    S = Sched()

    def din(name, shape, dt=F32):
        return nc.dram_tensor(name, list(shape), dt, kind="ExternalInput").ap()

    def dscr(name, shape, dt):
        return nc.dram_tensor(name, list(shape), dt,
                              kind="ExternalOutput" if debug else "Internal").ap()

    xin = din("xin", [T, D])
    cc_d = din("cc", [128, 8, 2])
    wmod_d = din("w_mod", [2, D, 3 * D])
    bmod_d = din("b_mod", [2, 3 * D])
    win_d = din("w_in", [2, D, WCOLS])
    wout_d = din("w_out", [2, D, D])
    wuq_d = din("w_uq", [2, 256, 1024])
    wukv_d = din("w_ukv", [2, 128, 1024])
    wa2_d = din("w_a2", [2, 2, 16, 128])
    lng_d = din("ln_g", [2, D])
    lnb_d = din("ln_b", [2, D])
    vecs_d = din("vecs", [128, NVEC])
    ropeC_d = din("ropeC", [64, NLAT])
    ropeS_d = din("ropeS", [64, NLAT])
    cf32_d = din("cf32", [128, 6, 128])
    reset_d = din("resetm", [128, 512])
    out_d = nc.dram_tensor("out", [NLAT, D], F32, kind="ExternalOutput").ap()

    x1_d = dscr("x1", [T, D], F32)
    qn_d = dscr("qn_s", [128, 4, T], BF16)
    qr_d = dscr("qr_s", [64, 4, T], BF16)
    kn_d = dscr("kn_s", [128, 4, T], BF16)
    kr_d = dscr("kr_s", [64, T], BF16)
    vm_d = dscr("vm_s", [128, NT, 512], BF16)
    sgm_d = dscr("sgm_s", [128, 4, T], BF16)
    sgr_d = dscr("sgr_s", [128, 4, T], BF16)
    oin_d = dscr("oin_s", [128, 4, T], F32)
    qt_d = dscr("qt_s", [128, 2, 3, T], BF16)
    U_d = dscr("U_s", [128, 2, 3, NCH, 64], F32)
    snap_d = dscr("snap_s", [128, 2, 3, NCH, 64], BF16)

    ARENA_BYTES = 190 * 1024
    arena_t = stack.enter_context(nc.sbuf_tensor("arena", [128, ARENA_BYTES // 2], BF16))
    PB = 12 * 1024
    pers_t = stack.enter_context(nc.sbuf_tensor("pers", [128, PB // 2], BF16))
    pers = Arena(pers_t, PB)
    ar = Arena(arena_t, ARENA_BYTES)
    psb = [stack.enter_context(nc.psum_tensor(f"ps{i}", [128, 512], F32)) for i in range(8)]
    ps = [p[:, :] for p in psb]

    cf32 = pers.alloc([128, 6, 128], F32)
    ident_f, ones_f, blk_f = cf32[:, 0, :], cf32[:, 1, :], cf32[:, 2, :]
    maskf = cf32[:, 3:5, :]
    cbf = pers.alloc([128, 2, 128], BF16)
    ident_b, ones_b = cbf[:, 0, :], cbf[:, 1, :]
    resetm = pers.alloc([128, 512], F32)
    vecs = pers.alloc([128, NVEC], F32)
    cc = pers.alloc([128, 8, 2], F32)
    sc = pers.alloc([128, 8, 2], F32)
    modT = pers.alloc([128, 16, 2], F32)
    lbv = pers.alloc([128, 4, 4], F32)
    A_sb = pers.alloc([128, 3, 2, NCH], F32)
    E_sb = pers.alloc([128, 3, 2, NCH], F32)
    R_sb = pers.alloc([128, 3, 2, NCH], F32)
    st6 = pers.alloc([128, 2, 6], F32)
    mv = pers.alloc([128, 2], F32)
    rstd = pers.alloc([128, 1], F32)
    tmpc = pers.alloc([128, 16], F32)
    negb = pers.alloc([128, 2], F32)

    def V(col):
        return vecs[:, col:col + 1]

    def dma(eng, out, in_, r, w):
        S.add(eng, lambda e: e.dma_start(out=out, in_=in_), r=r, w=w, dma=True)

    dma("sync", cf32, cf32_d, [], ["cf32"])
    dma("sync", resetm, reset_d, [], ["resetm"])
    dma("sync", vecs, vecs_d, [], ["vecs"])
    dma("sync", cc, cc_d, [], ["cc"])
    S.add("vector", lambda e: e.tensor_copy(out=cbf[:, 0, :], in_=cf32[:, 0, :]), r=["cf32"],
          w=["cbf0"])
    S.add("vector", lambda e: e.tensor_copy(out=cbf[:, 1, :], in_=cf32[:, 1, :]), r=["cf32"],
          w=["cbf1"])
    S.add("scalar", lambda e: e.activation(out=sc, in_=cc, func=AF.Silu), r=["cc"], w=["sc"])
    CONST_R = ["cf32", "cbf0", "cbf1", "resetm", "vecs"]

    for l in range(nlayers):
        xsrc = xin if l == 0 else x1_d
        xsrc_key = "xin" if l == 0 else "x1"
        last = (l == nlayers - 1)
        ar.reset()
        wm_rot = Rot(ar, "wm", 2, [128, 8, 512], F32)
        screp = ar.alloc([128, 2, 8, 128], F32)
        gate_ps_key = "ps1"
        gate_bc = pers_gate = None
        GATE_OFF = ARENA_BYTES - 16 * 1024
        gb = arena_t[:, GATE_OFF // 2:(GATE_OFF + 8192) // 2].bitcast(F32).rearrange(
            "p (a b) -> p a b", a=2, b=1024)
        lnbc = arena_t[:, (GATE_OFF + 8192) // 2:(GATE_OFF + 16384) // 2].bitcast(F32).rearrange(
            "p (a b) -> p a b", a=2, b=1024)
        ar.nbytes = GATE_OFF
        bmg = ar.alloc([128, 1024], F32)
        for j in range(2):
            S.add("vector", lambda e, j=j: e.tensor_copy(
                out=screp[:, j, :, :], in_=sc[:, :, j:j + 1].broadcast_to((128, 8, 128))),
                r=["sc"], w=[("screp", j)])
        dma("sync", bmg, bmod_d[l:l + 1, 2048:3072].partition_broadcast(128)
            .rearrange("p a b -> p (a b)"), [], ["bmg"])
        dma("sync", lnbc[:, 0, :], lng_d[l:l + 1, :].partition_broadcast(128)
            .rearrange("p a b -> p (a b)"), [], ["lnbc0"])
        dma("sync", lnbc[:, 1, :], lnb_d[l:l + 1, :].partition_broadcast(128)
            .rearrange("p a b -> p (a b)"), [], ["lnbc1"])
        wmv = wmod_d[l].rearrange("(k p) n -> p k n", p=128)
        for blk in range(6):
            wm, wk = wm_rot.get()
            dma("sync", wm, wmv[:, :, blk * 512:(blk + 1) * 512], [], [wk])
            if blk < 4:
                for j in range(4):
                    ci = blk * 4 + j
                    for k in range(8):
                        S.add("tensor", lambda e, k=k, j=j, ci=ci, wm=wm: e.matmul(
                            ps[0][:, ci * 2:ci * 2 + 2], lhsT=wm[:, k, j * 128:(j + 1) * 128],
                            rhs=sc[:, k, 0:2], start=(k == 0), stop=(k == 7)),
                            r=[wk, "sc"], w=["ps0"])
            else:
                for j in range(2):
                    for k in range(8):
                        S.add("tensor", lambda e, k=k, j=j, wm=wm: e.matmul(
                            ps[1 + j], lhsT=screp[:, j, k, :], rhs=wm[:, k, :],
                            start=(k == 0), stop=(k == 7)),
                            r=[wk, ("screp", j)], w=[f"ps{1 + j}"])
                    hb = blk - 4
                    S.add("vector", lambda e, j=j, hb=hb: e.tensor_tensor(
                        out=gb[:, j, hb * 512:(hb + 1) * 512], in0=ps[1 + j],
                        in1=bmg[:, hb * 512:(hb + 1) * 512], op=ALU.add),
                        r=[f"ps{1 + j}", "bmg"], w=[("gb", j, hb)])
        bm = vecs[:, 22 + 24 * l:22 + 24 * l + 16]
        S.add("vector", lambda e, bm=bm: e.tensor_tensor(
            out=modT, in0=ps[0][:, 0:32].rearrange("p (a b) -> p a b", a=16, b=2),
            in1=bm.unsqueeze(2).broadcast_to((128, 16, 2)), op=ALU.add),
            r=["ps0", "vecs"], w=["modT"])
        S.add("vector", lambda e: e.tensor_scalar(out=modT[:, 8:16, :], in0=modT[:, 8:16, :],
                                                   scalar1=1.0, scalar2=None, op0=ALU.add),
              r=["modT"], w=["modT"])
        if l == 0:
            S.add("vector", lambda e: e.memset(lbv[:, :, 0:1], 0.0), w=["lbv0"])
        else:
            S.add("vector", lambda e: e.tensor_tensor(
                out=tmpc[:, 0:4], in0=vecs[:, 4:8], in1=vecs[:, 0:4], op=ALU.subtract),
                r=["vecs"], w=["tmpc"])
            S.add("scalar", lambda e: e.activation(out=lbv[:, :, 0], in_=tmpc[:, 0:4],
                                                   func=AF.Sigmoid), r=["tmpc"], w=["lbv0"])
        S.add("vector", lambda e: e.tensor_scalar(out=lbv[:, :, 1:2], in0=lbv[:, :, 0:1],
                                                   scalar1=-1.0, scalar2=1.0, op0=ALU.mult,
                                                   op1=ALU.add), r=["lbv0"], w=["lbv1"])
        S.add("vector", lambda e: e.tensor_scalar(out=lbv[:, :, 2:3], in0=lbv[:, :, 1:2],
                                                   scalar1=-1.0, scalar2=None, op0=ALU.mult),
              r=["lbv1"], w=["lbv2"])
        S.add("vector", lambda e: e.reciprocal(out=tmpc[:, 4:8], in_=lbv[:, :, 1]),
              r=["lbv1", "tmpc"], w=["tmpc2"])
        S.add("vector", lambda e: e.tensor_scalar(out=tmpc[:, 8:12], in0=lbv[:, :, 0],
                                                   scalar1=-1.0, scalar2=1e-6, op0=ALU.mult,
                                                   op1=ALU.add), r=["lbv0", "tmpc2"], w=["tmpc3"])
        S.add("vector", lambda e: e.tensor_tensor(out=lbv[:, :, 3], in0=tmpc[:, 8:12],
                                                  in1=tmpc[:, 4:8], op=ALU.mult),
              r=["tmpc3", "tmpc2"], w=["lbv3"])
        LBV_R = ["lbv0", "lbv1", "lbv2", "lbv3"]
        S.add("vector", lambda e: e.tensor_scalar(out=negb, in0=vecs[:, 14 + 2 * l:16 + 2 * l],
                                                   scalar1=-1.0, scalar2=None, op0=ALU.mult),
              r=["vecs"], w=["negb"])
        S.barrier()
        if debug and l == 0:
            dbg_mod = nc.dram_tensor("dbg_mod", [128, 32], F32, kind="ExternalOutput").ap()
            dbg_gb = nc.dram_tensor("dbg_gb", [128, 2, 1024], F32, kind="ExternalOutput").ap()
            dbg_lbv = nc.dram_tensor("dbg_lbv", [128, 16], F32, kind="ExternalOutput").ap()
            dma("sync", dbg_mod, modT.rearrange("p a b -> p (a b)"), ["modT"], ["dbg_mod"])
            dma("sync", dbg_gb, gb, [("gb", j, hb) for j in range(2) for hb in range(2)], ["dbg_gb"])
            dma("sync", dbg_lbv, lbv.rearrange("p a b -> p (a b)"), LBV_R, ["dbg_lbv"])
            S.barrier()
        if stop_after == "mod":
            break

        ar.reset()
        win = ar.alloc([128, 8, WCOLS], BF16)
        wuq = ar.alloc([128, 2, 1024], BF16)
        wukv = ar.alloc([128, 1024], BF16)
        wa2 = ar.alloc([16, 2, 128], BF16)
        winv = win_d[l].rearrange("(k p) n -> p k n", p=128)
        for k in range(8):
            for hf in range(2):
                dma("gpsimd", win[:, k, hf * 1552:(hf + 1) * 1552],
                    winv[:, k, hf * 1552:(hf + 1) * 1552], [], [("win", k, hf)])
        WIN_R = [("win", k, hf) for k in range(8) for hf in range(2)]
        dma("gpsimd", wuq, wuq_d[l].rearrange("(c p) n -> p c n", p=128), [], ["wuq"])
        dma("gpsimd", wukv, wukv_d[l], [], ["wukv"])
        dma("gpsimd", wa2, wa2_d[l].rearrange("d r n -> r d n"), [], ["wa2"])

        xt_rot = Rot(ar, "xt", 2, [128, 1024], F32)
        hTs = [ar.alloc([128, 8, 512], BF16), ar.alloc([128, 8, 512], BF16)]
        f32p = Rot(ar, "f32p", 8, [128, 512], F32)
        ded = {nm: (ar.alloc([128, 512], F32), "ded_" + nm) for nm in ("hq0", "hq1", "gq", "gk", "rC", "rS")}
        sg_rot = Rot(ar, "sgst", 3, [128, 512], BF16)
        f32m = Rot(ar, "f32m", 6, [128, 512], F32)
        qn_rot = Rot(ar, "qn", 2, [128, 512], BF16)
        qr_rot = Rot(ar, "qr", 2, [128, 512], BF16)
        kn_rot = Rot(ar, "kn", 1, [128, 4, 512], BF16)
        krt_rot = Rot(ar, "krt", 1, [128, 512], BF16)
        vm_rot = Rot(ar, "vm", 1, [128, 4, 512], BF16)
        vtm_rot = Rot(ar, "vtm", 1, [128, 4, 512], BF16)
        cqn_rot = Rot(ar, "cqn", 1, [128, 3, 512], BF16)
        za_rot = Rot(ar, "za", 1, [128, 2, 512], BF16)
        qt_rot = Rot(ar, "qt", 1, [128, 2, 3, 512], BF16)
        kt_rot = Rot(ar, "kt", 1, [128, 2, 3, 512], BF16)
        ktm_rot = Rot(ar, "ktm", 2, [128, 2, 3, 128], BF16)
        pTH_rot = Rot(ar, "pTH", 2, [128, 2, 4, 128], BF16)
        pTG_rot = Rot(ar, "pTG", 2, [128, 4, 2, 128], BF16)
        for rot_ in (pTH_rot, pTG_rot):
            for bi, b_ in enumerate(rot_.bufs):
                S.add("gpsimd", lambda e, b_=b_: e.memset(b_.rearrange("p a b c -> p (a b c)"), 0.0),
                      w=[((rot_.name, bi), x_) for x_ in range(4)])
        maskH = ar.alloc([128, 4, 128], F32)
        for dr_ in range(2):
            S.add("vector", lambda e, dr_=dr_: e.tensor_copy(
                out=maskH[:, 2 * dr_:2 * dr_ + 2, :],
                in_=maskf[:, dr_, :].unsqueeze(1).broadcast_to((128, 2, 128))),
                r=["cf32"], w=[("mask4", dr_)])
        maskH_u = maskH.bitcast(mybir.dt.uint32)
        maskG_u = maskf.bitcast(mybir.dt.uint32)
        oin_rot = Rot(ar, "oin", 1, [128, 4, 128], F32)
        U_rot = Rot(ar, "Ut", 1, [128, 2, 3, 128], F32)
        ps5b = ps[5].bitcast(BF16)
        fm_i = [0]

        def fm_bank():
            if S.cur == "A":
                return ps[4], "ps4"
            j = 2 + fm_i[0] % 2
            fm_i[0] += 1
            return ps[j], f"ps{j}"

        _slot_keys = {f"ps{b_}": [(f"ps{b_}", s_) for s_ in range(4)] for b_ in (2, 3, 4, 6)}

        for gi, (g0, n) in enumerate(GROUPS):
            if gi >= a_groups:
                continue
            ntl = n // 128
            nch = n // 64
            ch0 = g0 // 64
            mc = 1 if gi == 0 else 0
            isctx = gi == 0
            if a_stage < 1:
                continue
            hb = gi % 2
            hT = hTs[hb]

            def emit_LN(gi_):
                g0_, n_ = GROUPS[gi_]
                mc_ = 1 if gi_ == 0 else 0
                hb_ = gi_ % 2
                hT_ = hTs[hb_]
                for j in range(n_ // 128):
                    tok0 = g0_ + j * 128
                    xt, xk = xt_rot.get()
                    dma("sync", xt, xsrc[tok0:tok0 + 128, :], [(xsrc_key, tok0 // 128)], [xk])
                    S.add("vector", lambda e: e.bn_stats(out=st6[:, 0, :], in_=xt[:, 0:512]),
                          r=[xk], w=["st6a"])
                    S.add("vector", lambda e: e.bn_stats(out=st6[:, 1, :], in_=xt[:, 512:1024]),
                          r=[xk], w=["st6b"])
                    S.add("vector", lambda e: e.bn_aggr(out=mv, in_=st6.rearrange("p a b -> p (a b)")),
                          r=["st6a", "st6b"], w=["mv"])
                    S.add("scalar", lambda e: e.activation(out=rstd, in_=mv[:, 1:2], func=AF.Ln,
                                                           bias=EPS), r=["mv"], w=["rstd"])
                    S.add("scalar", lambda e: e.activation(out=rstd, in_=rstd, func=AF.Exp,
                                                           scale=-0.5), r=["rstd"], w=["rstd"])
                    S.add("vector", lambda e: e.tensor_scalar(
                        out=xt, in0=xt, scalar1=mv[:, 0:1], scalar2=rstd, op0=ALU.subtract,
                        op1=ALU.mult), r=[xk, "mv", "rstd"], w=[xk])
                    for k in range(8):
                        S.add("tensor", lambda e: e.transpose(
                            out=ps[k // 4][:, (k % 4) * 128:(k % 4 + 1) * 128],
                            in_=xt[:, k * 128:(k + 1) * 128], identity=ident_f),
                            r=[xk, "cf32"], w=[f"ps{k // 4}"])
                    for k in range(8):
                        if k < 4:
                            S.add("scalar", lambda e: e.activation(
                                out=hT_[:, k, j * 128:(j + 1) * 128],
                                in_=ps[k // 4][:, (k % 4) * 128:(k % 4 + 1) * 128], func=AF.Identity,
                                scale=modT[:, 8 + k, mc_:mc_ + 1], bias=modT[:, k, mc_:mc_ + 1]),
                                r=[f"ps{k // 4}", "modT"], w=[("hT", hb_, k, j)])
                        else:
                            S.add("vector", lambda e: e.tensor_scalar(
                                out=hT_[:, k, j * 128:(j + 1) * 128],
                                in0=ps[k // 4][:, (k % 4) * 128:(k % 4 + 1) * 128],
                                scalar1=modT[:, 8 + k, mc_:mc_ + 1], scalar2=modT[:, k, mc_:mc_ + 1],
                                op0=ALU.mult, op1=ALU.add),
                                r=[f"ps{k // 4}", "modT"], w=[("hT", hb_, k, j)])

            if gi == 0:
                emit_LN(0)
            HT_R = [("hT", hb, k, j) for k in range(8) for j in range(ntl)]

            def fm_chain(col0, m):
                bank, bk = fm_bank()
                for k in range(8):
                    S.add("tensor", lambda e, k=k, bank=bank: e.matmul(
                        bank[0:m, 0:n], lhsT=win[:, k, col0:col0 + m], rhs=hT[:, k, 0:n],
                        start=(k == 0), stop=(k == 7)),
                        r=WIN_R + HT_R, w=[bk])
                return bank, bk

            if debug and gi == 0 and l == 0:
                dbg_hT = nc.dram_tensor("dbg_hT", [128, 8, 256], BF16, kind="ExternalOutput").ap()
                dma("sync", dbg_hT, hTs[0][:, :, 0:256], HT_R, ["dbg_hT"])
            if a_stage < 2:
                continue
            vtm, vtmk = vtm_rot.get()
            for j in range(ntl):
                for k in range(8):
                    S.add("tensor", lambda e, k=k, j=j: e.matmul(
                        ps[5], lhsT=hT[:, k, j * 128:(j + 1) * 128], rhs=win[:, k, C_HI:C_HI + 512],
                        start=(k == 0), stop=(k == 7)), r=WIN_R + HT_R, w=["ps5"])
                S.add("scalar", lambda e, j=j, vtm=vtm: e.copy(out=vtm[:, j, :], in_=ps[5]),
                      r=["ps5"], w=[(vtmk, j)])
            VTM_R = [(vtmk, j) for j in range(ntl)]

            hq = []
            for c in range(2):
                bank, bk = fm_chain(C_ZQ + c * 128, 128)
                t, tk = ded["hq%d" % c]
                S.add("scalar", lambda e, t=t, bank=bank: e.activation(
                    out=t[:, 0:n], in_=bank[:, 0:n], func=AF.Silu), r=[bk], w=[tk])
                hq.append((t, tk))
            for (c0, dst, coff) in ((C_HG, sgr_d, 0), (C_GG, sgr_d, 2), (C_MG, sgm_d, 0)):
                for c in range(2 if dst is sgr_d else 4):
                    bank, bk = fm_chain(c0 + c * 128, 128)
                    sgt, sgtk = sg_rot.get()
                    S.add("scalar", lambda e, bank=bank, sgt=sgt: e.activation(
                        out=sgt[:, 0:n], in_=bank[:, 0:n], func=AF.Silu), r=[bk], w=[sgtk])
                    nm = "sgr_d" if dst is sgr_d else "sgm_d"
                    dma("sync", dst[:, coff + c, g0:g0 + n], sgt[:, 0:n], [sgtk], [(nm, gi, coff + c)])
            S.capture("B")
            cqn, cqnk = cqn_rot.get()
            zc = []
            for c in range(3):
                bank, bk = fm_chain(C_ZCQ + c * 128, 128)
                t, tk = f32m.get()
                S.add("vector", lambda e, t=t, bank=bank: e.tensor_copy(out=t[:, 0:n],
                                                                        in_=bank[:, 0:n]),
                      r=[bk], w=[tk])
                zc.append((t, tk))
            for which in range(2):
                idx = [0, 1] if which == 0 else [2]
                rank = 256.0 if which == 0 else 128.0
                sqs = []
                for c in idx:
                    sq, sqk = f32m.get()
                    S.add("scalar", lambda e, sq=sq, c=c: e.activation(
                        out=sq[:, 0:n], in_=zc[c][0][:, 0:n], func=AF.Square),
                        r=[zc[c][1]], w=[sqk])
                    sqs.append((sq, sqk))
                bank, bk = fm_bank()
                for i, (sq, sqk) in enumerate(sqs):
                    S.add("tensor", lambda e, sq=sq, i=i, bank=bank: e.matmul(
                        bank[:, 0:n], lhsT=ones_f, rhs=sq[:, 0:n], start=(i == 0),
                        stop=(i == len(sqs) - 1)), r=[sqk, "cf32"], w=[bk])
                rr, rrk = f32m.get()
                S.add("scalar", lambda e, rr=rr, bank=bank, rank=rank: e.activation(
                    out=rr[:, 0:n], in_=bank[:, 0:n], func=AF.Ln, scale=1.0 / rank, bias=EPS),
                    r=[bk], w=[rrk])
                S.add("scalar", lambda e, rr=rr: e.activation(
                    out=rr[:, 0:n], in_=rr[:, 0:n], func=AF.Exp, scale=-0.5), r=[rrk], w=[rrk])
                for c in idx:
                    gcol = V(8 + l * 2 + c) if which == 0 else V(12 + l)
                    S.add("vector", lambda e, c=c, gcol=gcol, rr=rr, cqn=cqn: e.scalar_tensor_tensor(
                        out=cqn[:, c, 0:n], in0=zc[c][0][:, 0:n], scalar=gcol, in1=rr[:, 0:n],
                        op0=ALU.mult, op1=ALU.mult), r=[zc[c][1], rrk, "vecs"], w=[(cqnk, c)])
            if not isctx:
                rC, rCk = ded["rC"]
                rS, rSk = ded["rS"]
                dma("sync", rC[0:64, 0:n], ropeC_d[:, g0 - LCTX:g0 - LCTX + n], [], [rCk])
                dma("sync", rS[0:64, 0:n], ropeS_d[:, g0 - LCTX:g0 - LCTX + n], [], [rSk])

            def rope_evac(bank_a, bka, bank_b, bkb, out_ap, outk, scale):
                if isctx:
                    S.add("scalar", lambda e: e.activation(out=out_ap, in_=bank_a[0:64, 0:n],
                                                           func=AF.Identity, scale=scale),
                          r=[bka], w=[outk])
                    return
                t1, t1k = f32m.get()
                t2, t2k = f32m.get()
                S.add("vector", lambda e: e.tensor_tensor(out=t1[0:64, 0:n], in0=bank_a[0:64, 0:n],
                                                          in1=rC[0:64, 0:n], op=ALU.mult),
                      r=[bka, rCk], w=[t1k])
                S.add("vector", lambda e: e.tensor_tensor(out=t2[0:64, 0:n], in0=bank_b[0:64, 0:n],
                                                          in1=rS[0:64, 0:n], op=ALU.mult),
                      r=[bkb, rSk], w=[t2k])
                S.add("vector", lambda e: e.scalar_tensor_tensor(
                    out=out_ap, in0=t1[0:64, 0:n], scalar=scale, in1=t2[0:64, 0:n],
                    op0=ALU.mult, op1=ALU.add), r=[t1k, t2k], w=[outk])

            krt, krtk = krt_rot.get()
            bank_a, bka = fm_chain(C_KR, 64)
            if not isctx:
                bank_b, bkb = fm_chain(C_KRS, 64)
            else:
                bank_b, bkb = bank_a, bka
            rope_evac(bank_a, bka, bank_b, bkb, krt[0:64, 0:n], krtk, 1.0)
            dma("sync", kr_d[:, g0:g0 + n], krt[0:64, 0:n], [krtk], [("kr_d", gi)])

            for h in range(4):
                qn, qnk = qn_rot.get()
                qr, qrk = qr_rot.get()
                bank, bk = fm_bank()
                for c in range(2):
                    S.add("tensor", lambda e, c=c, h=h, bank=bank: e.matmul(
                        bank[:, 0:n], lhsT=wuq[:, c, h * 128:(h + 1) * 128], rhs=cqn[:, c, 0:n],
                        start=(c == 0), stop=(c == 1)), r=["wuq", (cqnk, c)], w=[bk])
                S.add("scalar", lambda e, h=h, bank=bank, qn=qn: e.activation(
                    out=qn[:, 0:n], in_=bank[:, 0:n], func=AF.Identity, scale=MLA_SCALE),
                    r=[bk], w=[qnk])
                dma("sync", qn_d[:, h, g0:g0 + n], qn[:, 0:n], [qnk], [("qn_d", gi, h)])
                banka, bka = fm_bank()
                for c in range(2):
                    S.add("tensor", lambda e, c=c, h=h, banka=banka: e.matmul(
                        banka[0:64, 0:n], lhsT=wuq[:, c, 512 + h * 128:512 + h * 128 + 64],
                        rhs=cqn[:, c, 0:n], start=(c == 0), stop=(c == 1)),
                        r=["wuq", (cqnk, c)], w=[bka])
                if not isctx:
                    bankb, bkb = fm_bank()
                    for c in range(2):
                        S.add("tensor", lambda e, c=c, h=h, bankb=bankb: e.matmul(
                            bankb[0:64, 0:n], lhsT=wuq[:, c, 512 + h * 128 + 64:512 + h * 128 + 128],
                            rhs=cqn[:, c, 0:n], start=(c == 0), stop=(c == 1)),
                            r=["wuq", (cqnk, c)], w=[bkb])
                    t1, t1k = f32m.get()
                    t2, t2k = f32m.get()
                    S.add("vector", lambda e, t1=t1, banka=banka: e.scalar_tensor_tensor(
                        out=t1[0:64, 0:n], in0=banka[0:64, 0:n], scalar=MLA_SCALE,
                        in1=rC[0:64, 0:n], op0=ALU.mult, op1=ALU.mult), r=[bka, rCk], w=[t1k])
                    S.add("vector", lambda e, t2=t2, bankb=bankb: e.scalar_tensor_tensor(
                        out=t2[0:64, 0:n], in0=bankb[0:64, 0:n], scalar=MLA_SCALE,
                        in1=rS[0:64, 0:n], op0=ALU.mult, op1=ALU.mult), r=[bkb, rSk], w=[t2k])
                    S.add("gpsimd", lambda e, t1=t1, t2=t2, h=h, qr=qr: e.tensor_tensor(
                        out=qr[0:64, 0:n], in0=t1[0:64, 0:n], in1=t2[0:64, 0:n], op=ALU.add),
                        r=[t1k, t2k], w=[qrk])
                else:
                    S.add("scalar", lambda e, h=h, banka=banka, qr=qr: e.activation(
                        out=qr[0:64, 0:n], in_=banka[0:64, 0:n], func=AF.Identity,
                        scale=MLA_SCALE), r=[bka], w=[qrk])
                dma("sync", qr_d[:, h, g0:g0 + n], qr[0:64, 0:n], [qrk], [("qr_d", gi, h)])
            kn, knk = kn_rot.get()
            for h in range(4):
                bank, bk = fm_bank()
                S.add("tensor", lambda e, h=h, bank=bank: e.matmul(
                    bank[:, 0:n], lhsT=wukv[:, h * 128:(h + 1) * 128], rhs=cqn[:, 2, 0:n],
                    start=True, stop=True), r=["wukv", (cqnk, 2)], w=[bk])
                S.add("vector", lambda e, h=h, bank=bank, kn=kn: e.tensor_copy(
                    out=kn[:, h, 0:n], in_=bank[:, 0:n]), r=[bk], w=[(knk, h)])
            dma("sync", kn_d[:, :, g0:g0 + n], kn[:, :, 0:n], [(knk, h) for h in range(4)],
                [("kn_d", gi)])
            vm, vmk = vm_rot.get()
            for j in range(ntl):
                bank, bk = fm_bank()
                S.add("tensor", lambda e, j=j, bank=bank: e.matmul(
                    bank, lhsT=cqn[:, 2, j * 128:(j + 1) * 128], rhs=wukv[:, 512:1024],
                    start=True, stop=True), r=["wukv", (cqnk, 2)], w=[bk])
                S.add("scalar", lambda e, j=j, bank=bank, vm=vm: e.copy(out=vm[:, j, :], in_=bank),
                      r=[bk], w=[(vmk, j)])
            dma("sync", vm_d[:, g0 // 128:g0 // 128 + ntl, :], vm[:, 0:ntl, :],
                [(vmk, j) for j in range(ntl)], [("vm_d", gi)])

            S.capture("A")
            qt, qtk = qt_rot.get()
            kt, ktk = kt_rot.get()

            def prep(mt, dr, q, qk, k_, kk, g, gk, esc):
                P, Pk = f32p.get()
                S.add("vector", lambda e: e.tensor_tensor_scan(
                    out=P[:, 0:n], data0=resetm[:, 0:n], data1=g[:, 0:n], initial=0.0,
                    op0=ALU.mult, op1=ALU.add), r=[gk, "resetm"], w=[Pk])
                P3 = P[:, 0:n].rearrange("p (c t) -> p c t", t=64)
                if dr == 0:
                    b, bk_ = P, Pk
                    mid = 31
                else:
                    b, bk_ = f32p.get()
                    S.add("vector", lambda e: e.tensor_tensor(out=b[:, 0:n], in0=g[:, 0:n],
                                                              in1=P[:, 0:n], op=ALU.subtract),
                          r=[gk, Pk], w=[bk_])
                    b3_ = b[:, 0:n].rearrange("p (c t) -> p c t", t=64)
                    S.add("vector", lambda e: e.tensor_tensor(
                        out=b3_, in0=b3_, in1=P3[:, :, 63:64].broadcast_to((128, nch, 64)),
                        op=ALU.add), r=[bk_, Pk], w=[bk_])
                    mid = 32
                b3 = b[:, 0:n].rearrange("p (c t) -> p c t", t=64)
                d, dk_ = f32p.get()
                d3 = d[:, 0:n].rearrange("p (c t) -> p c t", t=64)
                S.add("gpsimd", lambda e: e.tensor_tensor(
                    out=d3, in0=b3, in1=b3[:, :, mid:mid + 1].broadcast_to((128, nch, 64)),
                    op=ALU.subtract), r=[bk_], w=[dk_])
                import os as _os2
                _cl = 40.0 if _os2.environ.get("KCLAMP") else 80.0
                S.add("gpsimd", lambda e: e.tensor_scalar(out=d[:, 0:n], in0=d[:, 0:n], scalar1=_cl,
                                                           scalar2=-_cl, op0=ALU.min, op1=ALU.max),
                      r=[dk_], w=[dk_])
                Bc = P3[:, :, 63]
                rc = b3[:, :, mid]
                S.add("scalar", lambda e: e.activation(out=A_sb[:, mt, dr, ch0:ch0 + nch], in_=Bc,
                                                       func=AF.Exp, scale=esc),
                      r=[Pk], w=[("A", mt, dr, gi)])
                S.add("scalar", lambda e: e.activation(out=R_sb[:, mt, dr, ch0:ch0 + nch], in_=rc,
                                                       func=AF.Exp, scale=esc),
                      r=[bk_], w=[("R", mt, dr, gi)])
                S.add("vector", lambda e: e.tensor_tensor(out=tmpc[:, 0:nch], in0=Bc, in1=rc,
                                                          op=ALU.subtract),
                      r=[Pk, bk_, "tmpc", "tmpc2", "tmpc3"], w=["tmpc"])
                S.add("scalar", lambda e: e.activation(out=E_sb[:, mt, dr, ch0:ch0 + nch],
                                                       in_=tmpc[:, 0:nch], func=AF.Exp, scale=esc),
                      r=["tmpc"], w=[("E", mt, dr, gi)])
                e1, e1k = f32p.get()
                e2, e2k = f32p.get()
                S.add("scalar", lambda e: e.activation(out=e1[:, 0:n], in_=d[:, 0:n], func=AF.Exp,
                                                       scale=esc), r=[dk_], w=[e1k])
                S.add("scalar", lambda e: e.activation(out=e2[:, 0:n], in_=d[:, 0:n], func=AF.Exp,
                                                       scale=-esc), r=[dk_], w=[e2k])
                S.add("gpsimd", lambda e: e.tensor_tensor(out=qt[:, dr, mt, 0:n], in0=q[:, 0:n],
                                                          in1=e1[:, 0:n], op=ALU.mult),
                      r=[qk, e1k], w=[(qtk, dr, mt)])
                S.add("gpsimd", lambda e: e.tensor_tensor(out=kt[:, dr, mt, 0:n], in0=k_[:, 0:n],
                                                          in1=e2[:, 0:n], op=ALU.mult),
                      r=[kk, e2k], w=[(ktk, dr, mt)])

            for c in range(2):
                for dr in range(2):
                    bank, bk = fm_chain((C_ZFF if dr == 0 else C_ZFB) + c * 128, 128)
                    sg, sgk = f32p.get()
                    gt, gtk = f32p.get()
                    li = dr * 2 + c
                    S.add("scalar", lambda e, sg=sg, bank=bank: e.activation(
                        out=sg[:, 0:n], in_=bank[:, 0:n], func=AF.Exp, scale=-1.0), r=[bk], w=[sgk])
                    S.add("scalar", lambda e, sg=sg, gt=gt: e.activation(
                        out=gt[:, 0:n], in_=sg[:, 0:n], func=AF.Ln, bias=1.0), r=[sgk], w=[gtk])
                    S.add("scalar", lambda e, sg=sg, gt=gt: e.activation(
                        out=sg[:, 0:n], in_=gt[:, 0:n], func=AF.Exp, scale=-1.0), r=[gtk], w=[sgk])
                    S.add("vector", lambda e, sg=sg, gt=gt, li=li: e.tensor_scalar(
                        out=gt[:, 0:n], in0=sg[:, 0:n], scalar1=lbv[:, li, 3:4], scalar2=None,
                        op0=ALU.max), r=[sgk] + LBV_R, w=[gtk])
                    S.add("scalar", lambda e, gt=gt, li=li: e.activation(
                        out=gt[:, 0:n], in_=gt[:, 0:n], func=AF.Ln, scale=lbv[:, li, 1:2],
                        bias=lbv[:, li, 0:1]), r=[gtk] + LBV_R, w=[gtk])
                    S.add("vector", lambda e, sg=sg, li=li: e.tensor_scalar(
                        out=sg[:, 0:n], in0=sg[:, 0:n], scalar1=lbv[:, li, 2:3],
                        scalar2=lbv[:, li, 1:2], op0=ALU.mult, op1=ALU.add),
                        r=[sgk] + LBV_R, w=[sgk])
                    prep(c, dr, hq[c][0], hq[c][1], sg, sgk, gt, gtk, 1.0)
            bank, bk = fm_chain(C_GQ, 128)
            gq, gqk = ded["gq"]
            S.add("scalar", lambda e, gq=gq, bank=bank: e.activation(
                out=gq[:, 0:n], in_=bank[:, 0:n], func=AF.Identity, scale=32.0 ** -0.5),
                r=[bk], w=[gqk])
            bank, bk = fm_chain(C_GK, 128)
            gk_, gkk = ded["gk"]
            S.add("vector", lambda e, gk_=gk_, bank=bank: e.tensor_copy(out=gk_[:, 0:n],
                                                                        in_=bank[:, 0:n]),
                  r=[bk], w=[gkk])
            za, zak = za_rot.get()
            for dr in range(2):
                bank, bk = fm_chain(C_GAF + 16 * dr, 16)
                S.add("vector", lambda e, dr=dr, bank=bank, za=za: e.tensor_copy(
                    out=za[0:16, dr, 0:n], in_=bank[0:16, 0:n]), r=[bk], w=[(zak, dr)])
                bank2, bk2 = fm_bank()
                S.add("tensor", lambda e, dr=dr, bank2=bank2, za=za: e.matmul(
                    bank2[:, 0:n], lhsT=wa2[0:16, dr, :], rhs=za[0:16, dr, 0:n], start=True,
                    stop=True), r=["wa2", (zak, dr)], w=[bk2])
                gg, ggk = f32p.get()
                S.add("scalar", lambda e, dr=dr, bank2=bank2, gg=gg: e.activation(
                    out=gg[:, 0:n], in_=bank2[:, 0:n], func=AF.Exp, scale=-1.0,
                    bias=negb[:, dr:dr + 1]), r=[bk2, "negb"], w=[ggk])
                S.add("scalar", lambda e, gg=gg: e.activation(out=gg[:, 0:n], in_=gg[:, 0:n],
                                                               func=AF.Ln, bias=1.0), r=[ggk], w=[ggk])
                prep(2, dr, gq, gqk, gk_, gkk, gg, ggk, -1.0 / 16.0)
            QT_R = [(qtk, dr, mt) for dr in range(2) for mt in range(3)]
            KT_R = [(ktk, dr, mt) for dr in range(2) for mt in range(3)]
            if gi + 1 < min(len(GROUPS), a_groups):
                S.capture("L")
                emit_LN(gi + 1)
            S.flush_interleaved(["A", "B", "L"], blk=3)
            dma("sync", qt_d[:, :, :, g0:g0 + n], qt[:, :, :, 0:n], QT_R, [("qt_d", gi)])

            if a_stage < 4:
                continue
            for j in range(ntl):
                tk0 = j * 128
                ktm, ktmk = ktm_rot.get()
                for dr in range(2):
                    for mt in range(3):
                        slot = dr * 3 + mt
                        S.add("tensor", lambda e, dr=dr, mt=mt, slot=slot: e.transpose(
                            out=ps5b[:, slot * 128:(slot + 1) * 128],
                            in_=kt[:, dr, mt, tk0:tk0 + 128], identity=ident_b),
                            r=[(ktk, dr, mt), "cbf0"], w=["ps5"])
                S.add("vector", lambda e, ktm=ktm: e.tensor_copy(
                    out=ktm.rearrange("p a b c -> p (a b c)"), in_=ps5b[:, 0:768]),
                    r=["ps5"], w=[ktmk])
                oin, oink = oin_rot.get()
                Ut, Utk = U_rot.get()
                pTH, pTHk = pTH_rot.get()
                pTG, pTGk = pTG_rot.get()
                GB = (6, 2, 3, 4)

                def hg_st(dr, h):
                    hh, pair = h % 2, h // 2
                    bi = 6 if hh == 0 else 3
                    col = (dr * 2 + pair) * 128
                    S.add("tensor", lambda e: e.matmul(
                        ps[bi][:, col:col + 128], lhsT=kt[64 * hh:64 * hh + 64, dr, pair, tk0:tk0 + 128],
                        rhs=qt[64 * hh:64 * hh + 64, dr, pair, tk0:tk0 + 128], start=True, stop=True,
                        tile_position=(64 * hh, 0)),
                        r=[(ktk, dr, pair), (qtk, dr, pair)], w=[f"ps{bi}"])

                def gla_st(dr, h):
                    bi = GB[h]
                    col = dr * 128
                    S.add("tensor", lambda e: e.matmul(
                        ps[bi][:, col:col + 128], lhsT=kt[32 * h:32 * h + 32, dr, 2, tk0:tk0 + 128],
                        rhs=qt[32 * h:32 * h + 32, dr, 2, tk0:tk0 + 128], start=True, stop=True,
                        tile_position=(32 * h, 0)),
                        r=[(ktk, dr, 2), (qtk, dr, 2)], w=[f"ps{bi}"])

                def hg_evac(hh):
                    bi = 6 if hh == 0 else 3
                    S.add("vector", lambda e: e.copy_predicated(
                        out=pTH[:, hh, :, :].rearrange("p a c -> p (a c)"),
                        mask=maskH_u.rearrange("p a c -> p (a c)"),
                        data=ps[bi]),
                        r=[f"ps{bi}", ("mask4", 0), ("mask4", 1), (pTHk, hh)], w=[(pTHk, hh)])

                def gla_evac(h):
                    bi = GB[h]
                    S.add("vector", lambda e: e.copy_predicated(
                        out=pTG[:, h, :, :].rearrange("p a c -> p (a c)"),
                        mask=maskG_u.rearrange("p a c -> p (a c)"),
                        data=ps[bi][:, 0:256]),
                        r=[f"ps{bi}", "cf32", (pTGk, h)], w=[(pTGk, h)])

                for dr in range(2):
                    for h in range(4):
                        hg_st(dr, h)
                for dr in range(2):
                    gla_st(dr, 1)
                    gla_st(dr, 3)
                hg_evac(0)
                hg_evac(1)
                for dr in range(2):
                    gla_st(dr, 0)
                    gla_st(dr, 2)
                gla_evac(1)
                gla_evac(3)
                gla_evac(0)
                gla_evac(2)
                first_q = [True, True]
                n_oi = [0]
                for mix in range(2):
                    for dr in range(2):
                        for h in range(4):
                            hh = h % 2
                            ptile = mix * 2 + h // 2
                            vcol = mix * 256 + h * 64
                            st = first_q[hh]
                            first_q[hh] = False
                            n_oi[0] += 1
                            if mix == 0:
                                rhs_ap, rk = pTH[:, hh, dr * 2 + h // 2, :], (pTHk, hh)
                            else:
                                rhs_ap, rk = pTG[:, h, dr, :], (pTGk, h)
                            S.add("tensor", lambda e: e.matmul(
                                ps[7][64 * hh:64 * hh + 64, ptile * 128:(ptile + 1) * 128],
                                lhsT=vtm[:, j, vcol:vcol + 64], rhs=rhs_ap, start=st,
                                stop=(n_oi[0] == 16), skip_group_check=True,
                                tile_position=(0, 64 * hh)),
                                r=[rk, (vtmk, j)], w=["ps7"])
                for cc_ in range(2):
                    ub = ps[cc_]
                    ubk = f"ps{cc_}"
                    for dr in range(2):
                        for mt in range(3):
                            col = (dr * 3 + mt) * 64
                            for hx in range(2 if mt < 2 else 4):
                                if mt < 2:
                                    p0, dk, vcol = 64 * hx, 64, (2 * mt + hx) * 64
                                else:
                                    p0, dk, vcol = 32 * hx, 32, 256 + hx * 64
                                S.add("tensor", lambda e: e.matmul(
                                    ub[p0:p0 + dk, col:col + 64],
                                    lhsT=ktm[cc_ * 64:(cc_ + 1) * 64, dr, mt, p0:p0 + dk],
                                    rhs=vtm[cc_ * 64:(cc_ + 1) * 64, j, vcol:vcol + 64],
                                    start=True, stop=True, tile_position=(cc_ * 64, p0)),
                                    r=[ktmk, (vtmk, j)], w=[ubk])
                    cidx = ch0 + 2 * j + cc_
                    for dr in range(2):
                        S.add("vector", lambda e: e.tensor_tensor(
                            out=Ut[:, dr, :, cc_ * 64:(cc_ + 1) * 64],
                            in0=ub[:, dr * 192:(dr + 1) * 192].rearrange("p (a v) -> p a v", a=3, v=64),
                            in1=E_sb[:, :, dr, cidx:cidx + 1].broadcast_to((128, 3, 64)),
                            op=ALU.mult),
                            r=[ubk] + [("E", mt, dr, gi) for mt in range(3)], w=[(Utk, dr, cc_)])
                S.add("scalar", lambda e, oin=oin: e.copy(
                    out=oin.rearrange("p a b -> p (a b)"), in_=ps[7]), r=["ps7"], w=[oink])
                dma("sync", oin_d[:, :, g0 + tk0:g0 + tk0 + 128], oin, [oink], [("oin_d", gi, j)])
                cidx = ch0 + 2 * j
                dma("sync", U_d[:, :, :, cidx:cidx + 2, :],
                    Ut.rearrange("p a b (c v) -> p a b c v", c=2, v=64),
                    [(Utk, a_, b_) for a_ in range(2) for b_ in range(2)], [("U_d", gi, j)])
        S.barrier()
        if stop_after == "A":
            break

        ar.reset()
        U_sb = ar.alloc([128, 2, 3, NCH, 64], F32)
        snap = ar.alloc([128, 2, 3, NCH, 64], BF16)
        Sst = ar.alloc([128, 2, 3, 2, 64], F32)
        for dr in range(2):
            for mt in range(3):
                dma("sync", U_sb[:, dr, mt, :, :], U_d[:, dr, mt, :, :],
                    [("U_d", gi, j) for gi, (g0, n) in enumerate(GROUPS) for j in range(n // 128)],
                    [("U_sb", dr, mt)])
        S.add("vector", lambda e: e.memset(Sst.rearrange("p a b c d -> p (a b c d)"), 0.0),
              w=[("Sst", dr, mt, pp) for dr in range(2) for mt in range(3) for pp in range(2)])
        order = [list(range(NCH)), [3, 2, 1, 0] + list(range(NCH - 1, 3, -1))]
        AER = [(nm, mt, dr, gi) for nm in ("A", "E", "R") for mt in range(3) for dr in range(2)
               for gi in range(len(GROUPS))]
        for step in range(NCH):
            pp = step % 2
            for dr in range(2):
                c = order[dr][step]
                for mt in range(3):
                    S.add("scalar", lambda e, dr=dr, mt=mt, c=c, pp=pp: e.activation(
                        out=snap[:, dr, mt, c, :], in_=Sst[:, dr, mt, pp, :], func=AF.Identity,
                        scale=R_sb[:, mt, dr, c:c + 1]),
                        r=[("Sst", dr, mt, pp)] + (AER if step == 0 else []),
                        w=[("snap", dr, mt, c)])
                    if step < NCH - 1:
                        S.add("vector", lambda e, dr=dr, mt=mt, c=c, pp=pp: e.scalar_tensor_tensor(
                            out=Sst[:, dr, mt, 1 - pp, :], in0=Sst[:, dr, mt, pp, :],
                            scalar=A_sb[:, mt, dr, c:c + 1], in1=U_sb[:, dr, mt, c, :],
                            op0=ALU.mult, op1=ALU.add),
                            r=[("Sst", dr, mt, pp), ("U_sb", dr, mt)] + (AER if step == 0 else []),
                            w=[("Sst", dr, mt, 1 - pp)])
        dma("sync", snap_d, snap, [("snap", dr, mt, c) for dr in range(2) for mt in range(3)
                                   for c in range(NCH)], ["snap_d"])
        S.barrier()
        if stop_after == "S":
            break

        ar.reset()
        KnT = ar.alloc([128, 4, T], BF16)
        KrT = ar.alloc([128, T], BF16)
        Vsb = ar.alloc([128, NT, 512], BF16)
        wout = ar.alloc([128, 8, 1024], BF16)
        for h in range(4):
            dma("sync", KnT[:, h, :], kn_d[:, h, :], [("kn_d", gi) for gi in range(9)],
                [("KnT", h)])
        dma("sync", KrT[0:64, :], kr_d, [("kr_d", gi) for gi in range(9)], ["KrT"])
        for q4 in range(2):
            dma("sync", Vsb[:, q4 * 17:(q4 + 1) * 17, :], vm_d[:, q4 * 17:(q4 + 1) * 17, :],
                [("vm_d", gi) for gi in range(9)], [("Vsb", q4)])
        woutv = wout_d[l].rearrange("(k p) n -> p k n", p=128)
        for k in range(8):
            dma("gpsimd", wout[:, k, :], woutv[:, k, :], [], [("wout", k)])
        WOUT_R = [("wout", k) for k in range(8)]
        KV_R = [("KnT", h) for h in range(4)] + ["KrT", ("Vsb", 0), ("Vsb", 1)]
        qn_rot = Rot(ar, "cqn", 1, [128, 4, 512], BF16)
        qr_rot = Rot(ar, "cqr", 1, [128, 4, 512], BF16)
        sgm_rot = Rot(ar, "csgm", 1, [128, 4, 512], BF16)
        sgr_rot = Rot(ar, "csgr", 1, [128, 4, 512], BF16)
        oin_rot = Rot(ar, "coin", 1, [128, 4, 512], F32)
        qt_rot = Rot(ar, "cqt", 1, [128, 2, 3, 512], BF16)
        sn_rot = Rot(ar, "csn", 1, [128, 2, 3, 8, 64], BF16)
        pT_rot = Rot(ar, "cpT", 4, [128, 512], BF16)
        yT_rot = Rot(ar, "yT", 2, [128, 8, 512], BF16)
        f32p = Rot(ar, "cf32p", 4, [128, 512], F32)
        acc_rot = Rot(ar, "cacc", 2, [128, 512], F32)
        xt_rot = Rot(ar, "cxt", 1, [128, 1024], F32)
        zt_rot = Rot(ar, "czt", 2, [128, 1024], F32)
        sbank_i = [0]
        obank_i = [0]

        def outproj_steps(gi, g0, n, yT, yTk):
            mc = 1 if gi == 0 else 0
            YT_R = [(yTk, k) for k in range(8)]
            steps = []
            for j in range(n // 128):
                def step(j=j):
                    tok0 = g0 + j * 128
                    xt, xk = xt_rot.get()
                    zt, zk = zt_rot.get()
                    dma("sync", xt, xsrc[tok0:tok0 + 128, :], [(xsrc_key, tok0 // 128)], [xk])
                    for hf in range(2):
                        ob = ps[3 if hf == 0 else 7]
                        obk = "ps3" if hf == 0 else "ps7"
                        for k in range(8):
                            S.add("tensor", lambda e, k=k: e.matmul(
                                ob, lhsT=yT[:, k, j * 128:(j + 1) * 128],
                                rhs=wout[:, k, hf * 512:(hf + 1) * 512], start=(k == 0),
                                stop=(k == 7)), r=YT_R + WOUT_R, w=[obk])
                        S.add("vector", lambda e: e.tensor_tensor(
                            out=zt[:, hf * 512:(hf + 1) * 512], in0=ob,
                            in1=gb[:, mc, hf * 512:(hf + 1) * 512], op=ALU.mult),
                            r=[obk, ("gb", mc, hf)], w=[(zk, hf)])
                        S.add("vector", lambda e: e.scalar_tensor_tensor(
                            out=zt[:, hf * 512:(hf + 1) * 512], in0=xt[:, hf * 512:(hf + 1) * 512],
                            scalar=ALPHA, in1=zt[:, hf * 512:(hf + 1) * 512], op0=ALU.mult,
                            op1=ALU.add), r=[(zk, hf), xk], w=[(zk, hf)])
                    S.add("vector", lambda e: e.bn_stats(out=st6[:, 0, :], in_=zt[:, 0:512]),
                          r=[(zk, 0)], w=["st6a"])
                    S.add("vector", lambda e: e.bn_stats(out=st6[:, 1, :], in_=zt[:, 512:1024]),
                          r=[(zk, 1)], w=["st6b"])
                    S.add("vector", lambda e: e.bn_aggr(out=mv, in_=st6.rearrange("p a b -> p (a b)")),
                          r=["st6a", "st6b"], w=["mv"])
                    S.add("scalar", lambda e: e.activation(out=rstd, in_=mv[:, 1:2], func=AF.Ln,
                                                           bias=EPS), r=["mv"], w=["rstd"])
                    S.add("scalar", lambda e: e.activation(out=rstd, in_=rstd, func=AF.Exp,
                                                           scale=-0.5), r=["rstd"], w=["rstd"])
                    S.add("vector", lambda e: e.tensor_scalar(
                        out=zt, in0=zt, scalar1=mv[:, 0:1], scalar2=rstd, op0=ALU.subtract,
                        op1=ALU.mult), r=[(zk, 0), (zk, 1), "mv", "rstd"], w=[(zk, 0), (zk, 1)])
                    S.add("gpsimd", lambda e: e.tensor_tensor(
                        out=zt, in0=zt, in1=lnbc[:, 0, :], op=ALU.mult),
                        r=[(zk, 0), (zk, 1), "lnbc0"], w=[(zk, 0), (zk, 1)])
                    S.add("gpsimd", lambda e: e.tensor_tensor(
                        out=zt, in0=zt, in1=lnbc[:, 1, :], op=ALU.add),
                        r=[(zk, 0), (zk, 1), "lnbc1"], w=[(zk, 0), (zk, 1)])
                    if last:
                        dma("sync", out_d[tok0 - LCTX:tok0 - LCTX + 128, :], zt, [(zk, 0), (zk, 1)],
                            [("out", tok0 // 128)])
                    else:
                        dma("sync", x1_d[tok0:tok0 + 128, :], zt, [(zk, 0), (zk, 1)],
                            [("x1", tok0 // 128)])
                steps.append(step)
            return steps

        pending_tail = []
        for gi, (g0, n) in enumerate(GROUPS):
            isctx = gi == 0
            if isctx and last:
                continue
            ntl = n // 128
            nch = n // 64
            ch0 = g0 // 64
            kt_lo, kt_hi = (0, 2) if isctx else (0, NT)
            qn, qnk = qn_rot.get()
            qr, qrk = qr_rot.get()
            sgm, sgmk = sgm_rot.get()
            sgr, sgrk = sgr_rot.get()
            oin, oink = oin_rot.get()
            qt, qtk = qt_rot.get()
            sn, snk = sn_rot.get()
            yT, yTk = yT_rot.get()
            dma("sync", qn[:, :, 0:n], qn_d[:, :, g0:g0 + n], [("qn_d", gi, h_) for h_ in range(4)], [qnk])
            dma("sync", qr[0:64, :, 0:n], qr_d[:, :, g0:g0 + n], [("qr_d", gi, h_) for h_ in range(4)], [qrk])
            dma("sync", qt[:, :, :, 0:n], qt_d[:, :, :, g0:g0 + n], [("qt_d", gi)], [qtk])
            for dr in range(2):
                dma("sync", sn[:, dr, :, 0:nch, :], snap_d[:, dr, :, ch0:ch0 + nch, :],
                    ["snap_d"], [(snk, dr)])
            dma("sync", oin[:, :, 0:n], oin_d[:, :, g0:g0 + n],
                [("oin_d", gi, j) for j in range(ntl)], [oink])
            dma("sync", sgr[:, :, 0:n], sgr_d[:, :, g0:g0 + n], [("sgr_d", gi, c_) for c_ in range(4)], [sgrk])
            dma("sync", sgm[:, :, 0:n], sgm_d[:, :, g0:g0 + n], [("sgm_d", gi, c_) for c_ in range(4)], [sgmk])

            deferred = []
            for ptile in range(4):
                def ro_mm(ptile=ptile):
                    rb = (3, 7)
                    firsts = [True, True]
                    cnt = [0, 0]
                    for cidx in range(nch):
                        for dr in range(2):
                            for hh in range(2):
                                if ptile < 2:
                                    mt, p0, dk = ptile, 64 * hh, 64
                                else:
                                    mt, p0, dk = 2, 32 * (2 * (ptile - 2) + hh), 32
                                st = firsts[hh]
                                firsts[hh] = False
                                cnt[hh] += 1
                                S.add("tensor", lambda e: e.matmul(
                                    ps[rb[hh]][64 * hh:64 * hh + 64, cidx * 64:(cidx + 1) * 64],
                                    lhsT=sn[p0:p0 + dk, dr, mt, cidx, :],
                                    rhs=qt[p0:p0 + dk, dr, mt, cidx * 64:(cidx + 1) * 64],
                                    start=st, stop=(cnt[hh] == nch * 2), skip_group_check=True,
                                    tile_position=(p0, 64 * hh)),
                                    r=[(snk, dr), qtk], w=[f"ps{rb[hh]}"])
                    o, ok_ = f32p.get()
                    for hh in range(2):
                        S.add("vector", lambda e: e.tensor_tensor(
                            out=o[64 * hh:64 * hh + 64, 0:n], in0=ps[rb[hh]][64 * hh:64 * hh + 64, 0:n],
                            in1=oin[64 * hh:64 * hh + 64, ptile, 0:n], op=ALU.add),
                            r=[f"ps{rb[hh]}", oink] + ([ok_] if hh == 1 else []), w=[ok_])
                    sq, sqk = f32p.get()
                    S.add("scalar", lambda e: e.activation(out=sq[:, 0:n], in_=o[:, 0:n],
                                                           func=AF.Square), r=[ok_], w=[sqk])

                    def ro_ss():
                        S.add("tensor", lambda e: e.matmul(
                            ps[3][:, 0:n], lhsT=blk_f, rhs=sq[:, 0:n], start=True, stop=True),
                            r=[sqk, "cf32"], w=["ps3"])
                        S.add("scalar", lambda e: e.activation(
                            out=sq[:, 0:n], in_=ps[3][:, 0:n], func=AF.Ln, scale=1.0 / 64.0, bias=EPS),
                            r=["ps3"], w=[sqk])
                        S.add("scalar", lambda e: e.activation(out=sq[:, 0:n], in_=sq[:, 0:n],
                                                               func=AF.Exp, scale=-0.5),
                              r=[sqk], w=[sqk])
                        gcol = V(18 + l) if ptile < 2 else V(20 + l)
                        S.add("vector", lambda e: e.scalar_tensor_tensor(
                            out=o[:, 0:n], in0=o[:, 0:n], scalar=gcol, in1=sq[:, 0:n], op0=ALU.mult,
                            op1=ALU.mult), r=[ok_, sqk, "vecs"], w=[ok_])
                        ychunk = ptile if ptile < 2 else 4 + ptile
                        S.add("gpsimd", lambda e: e.tensor_tensor(
                            out=yT[:, ychunk, 0:n], in0=o[:, 0:n], in1=sgr[:, ptile, 0:n],
                            op=ALU.mult), r=[ok_, sgrk], w=[(yTk, ychunk)])
                    return ro_ss
                deferred.append(ro_mm)

            LA = 2
            nk = kt_hi - kt_lo
            iters = [(h, ki, kt_lo + ki) for h in range(4) for ki in range(nk)]
            hbanks = {}
            for h in range(4):
                par = obank_i[0] % 2
                obank_i[0] += 1
                acc, acck = acc_rot.get()
                hbanks[h] = (ps[4 + par], f"ps{4 + par}", acc, acck)
            pend = {}

            def emit_scores(it):
                h, ki, ktile = it
                sb = ps[sbank_i[0] % 3]
                sbk = f"ps{sbank_i[0] % 3}"
                sbank_i[0] += 1
                S.add("tensor", lambda e: e.matmul(
                    sb[:, 0:n], lhsT=KnT[:, h, ktile * 128:(ktile + 1) * 128], rhs=qn[:, h, 0:n],
                    start=True, stop=False), r=[("KnT", h), qnk], w=[sbk])
                S.add("tensor", lambda e: e.matmul(
                    sb[:, 0:n], lhsT=KrT[0:64, ktile * 128:(ktile + 1) * 128],
                    rhs=qr[0:64, h, 0:n], start=False, stop=True), r=["KrT", qrk], w=[sbk])
                pT, pTk = pT_rot.get()
                S.add("scalar", lambda e: e.activation(
                    out=pT[:, 0:n], in_=sb[:, 0:n], func=AF.Exp), r=[sbk], w=[pTk])
                pend[it] = (pT, pTk)

            def emit_pv(it):
                h, ki, ktile = it
                pT, pTk = pend.pop(it)
                ob, obk, acc, acck = hbanks[h]
                S.add("tensor", lambda e: e.matmul(
                    ob[:, 0:n], lhsT=Vsb[:, ktile, h * 128:(h + 1) * 128], rhs=pT[:, 0:n],
                    start=(ki == 0), stop=(ki == nk - 1)),
                    r=[pTk, ("Vsb", ktile // 17)], w=[obk])
                S.add("tensor", lambda e: e.matmul(
                    ps[6][:, 0:n], lhsT=ones_b, rhs=pT[:, 0:n], start=(ki == 0),
                    stop=(ki == nk - 1)), r=[pTk, "cbf1"], w=["ps6"])
                if ki == nk - 1:
                    S.add("vector", lambda e: e.tensor_copy(out=acc[:, 0:n], in_=ps[6][:, 0:n]),
                          r=["ps6"], w=[acck])
                    rd, rdk = f32p.get()
                    S.add("vector", lambda e: e.reciprocal(out=rd[:, 0:n], in_=acc[:, 0:n]),
                          r=[acck], w=[rdk])
                    S.add("vector", lambda e: e.tensor_tensor(
                        out=rd[:, 0:n], in0=ob[:, 0:n], in1=rd[:, 0:n], op=ALU.mult),
                        r=[obk, rdk], w=[rdk])
                    S.add("gpsimd", lambda e: e.tensor_tensor(
                        out=yT[:, 2 + h, 0:n], in0=rd[:, 0:n], in1=sgm[:, h, 0:n], op=ALU.mult),
                        r=[rdk, sgmk], w=[(yTk, 2 + h)])

            extra = {}
            nit = len(iters)
            ro_pos = [0, 3, 6, 9]
            ss_list = []
            if nit < 40:
                for ro in deferred:
                    ro()()
            else:
                for k_, ro in enumerate(deferred):
                    extra.setdefault(ro_pos[k_], []).append(("ro", ro))
            tail_steps = pending_tail
            pending_tail = []
            if tail_steps:
                gap = max(1, (nit - 30) // len(tail_steps))
                for k_, stp in enumerate(tail_steps):
                    extra.setdefault(min(24 + k_ * gap, nit - 1), []).append(("tail", stp))
            for i in range(nit + LA):
                if i < nit:
                    emit_scores(iters[i])
                if i >= LA:
                    emit_pv(iters[i - LA])
                for kind, fn_ in list(extra.get(i, [])):
                    if kind == "ro":
                        extra.setdefault(i + 5, []).append(("ss", fn_()))
                    else:
                        fn_()
            for ss in ss_list:
                ss()
            pending_tail = outproj_steps(gi, g0, n, yT, yTk)
        for stp in pending_tail:
            stp()
        S.barrier()

    S.finalize(nc, stack)
    blk = stack.enter_context(nc.Block())
    S.emit(blk)
    stack.close()
    return nc


def _perm_w_in():
    o = dict(hq=0, hff=256, hfb=512, hi=768, hgate=1024, mcq=1280, mckv=1536, mkr=1664,
             mgate=1728, gq=2240, gk=2368, gv=2496, gaf=2752, gab=2768, ggate=2784)
    r = np.arange
    p = np.arange(64)
    a, hf, f = p // 32, (p // 16) % 2, p % 16
    partner = a * 32 + (1 - hf) * 16 + f
    idx = np.concatenate([
        r(o["hq"], o["hq"] + 256), r(o["hff"], o["hff"] + 256), r(o["hfb"], o["hfb"] + 256),
        r(o["hgate"], o["hgate"] + 256), r(o["mcq"], o["mcq"] + 256), r(o["mckv"], o["mckv"] + 128),
        r(o["mkr"], o["mkr"] + 64), o["mkr"] + partner, r(o["mgate"], o["mgate"] + 512),
        r(o["gq"], o["gq"] + 128), r(o["gk"], o["gk"] + 128), r(o["ggate"], o["ggate"] + 256),
        r(o["gaf"], o["gaf"] + 16), r(o["gab"], o["gab"] + 16), r(o["hi"], o["hi"] + 256),
        r(o["gv"], o["gv"] + 256)])
    assert idx.size == WCOLS
    return idx, partner


def _consts():
    pos = np.arange(NLAT)
    pos_r = (pos // 64).astype(np.float32)
    pos_c = (pos % 64).astype(np.float32)
    inv = (1.0 / (np.float32(10000.0) ** (np.arange(16, dtype=np.float32) / np.float32(16)))
           ).astype(np.float32)
    p = np.arange(64)
    a, hf, f = p // 32, (p // 16) % 2, p % 16
    posa = np.where(a[:, None] == 0, pos_r[None, :], pos_c[None, :]).astype(np.float32)
    ang = (posa * inv[f][:, None]).astype(np.float32)
    ropeC = np.cos(ang).astype(np.float32)
    sgn = np.where(hf == 0, -1.0, 1.0).astype(np.float32)
    ropeS = (np.sin(ang).astype(np.float32) * sgn[:, None]).astype(np.float32)
    cf = np.zeros((128, 6, 128), np.float32)
    i = np.arange(128)
    cf[:, 0, :] = np.eye(128, dtype=np.float32)
    cf[:, 1, :] = 1.0
    cf[:, 2, :] = (i[:, None] // 64 == i[None, :] // 64)
    same = (i[:, None] // 64 == i[None, :] // 64)
    cf[:, 3, :] = same & (i[:, None] <= i[None, :])
    cf[:, 4, :] = same & (i[:, None] >= i[None, :])
    resetm = np.ones((128, 512), np.float32)
    resetm[:, ::64] = 0.0
    return ropeC, ropeS, cf, resetm


_NC_CACHE = {}


def _prep_shared(inp):
    idx, partner = _perm_w_in()
    w_in_p = np.ascontiguousarray(inp["w_in"][:, :, idx])
    wuq = inp["mla_w_uq"].reshape(2, 256, 4, 192)
    rope = wuq[..., 128:]
    w_uq_p = np.concatenate(
        [wuq[..., :128].reshape(2, 256, 512),
         np.concatenate([rope, rope[..., partner]], axis=-1).reshape(2, 256, 512)], axis=-1)
    wukv = inp["mla_w_ukv"].reshape(2, 128, 4, 256)
    w_ukv_p = np.concatenate([wukv[..., :128].reshape(2, 128, 512),
                              wukv[..., 128:].reshape(2, 128, 512)], axis=-1)
    vecs = np.zeros((128, NVEC), np.float32)
    p = np.arange(128)
    lbl = inp["hg_lb_logits"]
    for l in range(2):
        for d in range(2):
            for c in range(2):
                vecs[:, l * 4 + d * 2 + c] = lbl[l, d, c * 128 + p]
        for c in range(2):
            vecs[:, 8 + l * 2 + c] = inp["mla_q_norm_g"][l, c * 128 + p]
        vecs[:, 12 + l] = inp["mla_kv_norm_g"][l, p]
        for d in range(2):
            vecs[:, 14 + l * 2 + d] = inp["gla_b_a"][l, d, p]
        vecs[:, 18 + l] = inp["hg_norm_g"][l, p % 64]
        vecs[:, 20 + l] = inp["gla_norm_g"][l, p % 64]
        for ci in range(24):
            vecs[:, 22 + 24 * l + ci] = inp["b_mod"][l, ci * 128 + p]
    ropeC, ropeS, cf, resetm = _consts()
    return dict(w_mod=np.ascontiguousarray(inp["w_mod"]), b_mod=np.ascontiguousarray(inp["b_mod"]),
                w_in=w_in_p, w_out=np.ascontiguousarray(inp["w_out"]),
                w_uq=np.ascontiguousarray(w_uq_p), w_ukv=np.ascontiguousarray(w_ukv_p),
                w_a2=np.ascontiguousarray(inp["gla_w_a2"]), ln_g=np.ascontiguousarray(inp["ln_g"]),
                ln_b=np.ascontiguousarray(inp["ln_b"]), vecs=vecs, ropeC=ropeC, ropeS=ropeS,
                cf32=cf, resetm=resetm)


def _per_core(inp, b):
    xin = np.ascontiguousarray(np.concatenate([inp["ctx"][b], inp["x"][b]], axis=0))
    cc = np.stack([inp["c"][b].reshape(8, 128).T, inp["c_ctx"].reshape(8, 128).T], axis=-1)
    return dict(xin=xin, cc=np.ascontiguousarray(cc.astype(np.float32)))


def kernel(**inputs):
    inp = {k: np.asarray(v, dtype=np.float32) for k, v in inputs.items()}
    if "nc" not in _NC_CACHE:
        _NC_CACHE["nc"] = build_program()
    nc = _NC_CACHE["nc"]
    shared = _prep_shared(inp)
    in_maps = []
    for b in range(8):
        m = dict(shared)
        m.update(_per_core(inp, b))
        in_maps.append(m)
    res = run_bass_kernel_spmd(nc, in_maps, core_ids=list(range(8)))
    out = np.stack([np.asarray(r["out"], dtype=np.float32) for r in res.results], axis=0)
    return out
```

```python
import numpy as np
import ml_dtypes
import concourse.bass as bass
import concourse.mybir as mybir
from concourse.bass_utils import run_bass_kernel_spmd

F32 = mybir.dt.float32
BF16 = mybir.dt.bfloat16
AF = mybir.ActivationFunctionType
ALU = mybir.AluOpType

T = 4352
NT = 34
NCH = 68
D = 1024
LCTX = 256
NLAT = 4096
GROUPS = [(0, 256)] + [(256 + 512 * i, 512) for i in range(8)]
WCOLS = 3104
(C_ZQ, C_ZFF, C_ZFB, C_HG, C_ZCQ, C_ZCKV, C_KR, C_KRS, C_MG, C_GQ, C_GK, C_GG, C_GAF, C_GAB,
 C_HI, C_GV) = (0, 256, 512, 768, 1024, 1280, 1408, 1472, 1536, 2048, 2176, 2304, 2560, 2576,
                2592, 2848)
EPS = 1e-6
ALPHA = 4.0 ** 0.25
MLA_SCALE = 192.0 ** -0.5
NVEC = 22 + 48
SAME_ENGINE_SYNC = True
SEM_CAP = 30000
DMA_POOL = 20


class _Op:
    __slots__ = ("eng", "fn", "deps", "sig", "sem", "val", "waits", "dma", "idx")


class _Rec:
    def __getattr__(self, name):
        def f(*a, **k):
            self.__dict__["call"] = (name, a, k)
        return f


class Sched:
    ENG = ("sync", "scalar", "vector", "gpsimd", "tensor")

    def __init__(self):
        self.ops = []
        self.lastw = {}
        self.readers = {}
        self.last_eng = {}
        self.dmas_since_bar = []

    ALIAS = {}

    def _expand(self, keys):
        out = []
        for k in keys:
            out.extend(self.ALIAS.get(k, (k,)))
        return out

    cur = None
    bufs = None

    def capture(self, name):
        if self.bufs is None:
            self.bufs = {}
        self.cur = name
        if name is not None:
            self.bufs.setdefault(name, [])

    def flush_interleaved(self, names, blk=3):
        self.cur = None
        L = [self.bufs.pop(nm, []) for nm in names]
        L = [x for x in L if x]
        idx = [0] * len(L)
        while any(idx[i] < len(L[i]) for i in range(len(L))):
            cand = [i for i in range(len(L)) if idx[i] < len(L[i])]
            i = min(cand, key=lambda i_: idx[i_] / len(L[i_]))
            for _ in range(blk):
                if idx[i] < len(L[i]):
                    o = L[i][idx[i]]
                    self.add(o[0], o[1], r=o[2], w=o[3], dma=o[4], _raw=True)
                    idx[i] += 1

    def add(self, eng, fn, r=(), w=(), dma=False, _raw=False):
        if not _raw:
            r = self._expand(r)
            w = self._expand(w)
            if fn is not None:
                rec = _Rec()
                fn(rec)
                fn = rec.__dict__["call"]
            if self.cur is not None:
                self.bufs[self.cur].append((eng, fn, r, w, dma))
                return None
        op = _Op()
        op.eng, op.fn, op.dma = eng, fn, dma
        op.sig, op.sem, op.val, op.waits = False, None, 0, []
        op.idx = len(self.ops)
        deps = set()
        for k in r:
            x = self.lastw.get(k)
            if x is not None:
                deps.add(x)
        for k in w:
            x = self.lastw.get(k)
            if x is not None:
                deps.add(x)
            for x in self.readers.get(k, ()):
                deps.add(x)
        op.deps = deps
        for k in r:
            self.readers.setdefault(k, []).append(op)
        for k in w:
            self.lastw[k] = op
            self.readers[k] = []
        self.ops.append(op)
        self.last_eng[eng] = op
        if dma:
            self.dmas_since_bar.append(op)
        return op

    def barrier(self):
        lasts = [o for o in self.last_eng.values()]
        dm = list(self.dmas_since_bar)
        self.dmas_since_bar = []
        for e in self.ENG:
            op = self.add(e, None)
            op.deps = set(lasts) | set(dm)

    def finalize(self, nc, stack):
        ops = self.ops
        pools = {}
        dcount = {}
        last_on_sem = {}
        for op in ops:
            if op.dma:
                n = dcount.get(op.eng, 0)
                dcount[op.eng] = n + 1
                if op.eng not in pools:
                    pools[op.eng] = [stack.enter_context(nc.semaphore(f"d_{op.eng}_{i}"))
                                     for i in range(DMA_POOL)]
                j = n % DMA_POOL
                op.sem = pools[op.eng][j]
                op.val = 16 * (n // DMA_POOL + 1)
                prev = last_on_sem.get((op.eng, j))
                if prev is not None:
                    op.deps.add(prev)
                last_on_sem[(op.eng, j)] = op
                op.sig = True
        for op in ops:
            for d in op.deps:
                if not d.dma:
                    d.sig = True
        cnt = {}
        esems = {}
        for op in ops:
            if op.dma or not op.sig or op.fn is None:
                continue
            c = cnt.get(op.eng, 0)
            cnt[op.eng] = c + 1
            si = c // SEM_CAP
            lst = esems.setdefault(op.eng, [])
            if si >= len(lst):
                lst.append(stack.enter_context(nc.semaphore(f"e_{op.eng}_{si}")))
            op.sem = lst[si]
            op.val = c % SEM_CAP + 1
        waited = {e: {} for e in self.ENG}
        for op in ops:
            wl = waited[op.eng]
            for d in sorted(op.deps, key=lambda o: o.idx):
                if d.fn is None:
                    continue
                if d.eng == op.eng and not d.dma:
                    if op.eng == "tensor":
                        continue
                    if not SAME_ENGINE_SYNC and not op.dma:
                        continue
                key = id(d.sem)
                if wl.get(key, 0) >= d.val:
                    continue
                wl[key] = d.val
                op.waits.append((d.sem, d.val))
        self.per_eng = {e: [o for o in ops if o.eng == e] for e in self.ENG}

    def emit(self, block):
        def mk(name):
            lst = self.per_eng[name]

            def body(e):
                for op in lst:
                    for s, v in op.waits:
                        e.wait_ge(s, v)
                    if op.fn is not None:
                        name_, a_, k_ = op.fn
                        ins = getattr(e, name_)(*a_, **k_)
                        if op.dma:
                            ins.then_inc(op.sem, 16)
                        elif op.sig:
                            ins.then_inc(op.sem, 1)
            return body
        block.sync(mk("sync"))
        block.scalar(mk("scalar"))
        block.vector(mk("vector"))
        block.gpsimd(mk("gpsimd"))
        block.tensor(mk("tensor"))


class Arena:
    def __init__(self, ap_bf16, nbytes):
        self.ap = ap_bf16
        self.nbytes = nbytes
        self.off = 0
        self.peak = 0

    def reset(self):
        self.off = 0

    def alloc(self, shape, dtype, parts=128):
        es = 4 if dtype == F32 else 2
        nel = int(np.prod(shape[1:]))
        nb = (nel * es + 31) // 32 * 32
        assert self.off + nb <= self.nbytes, f"arena overflow {self.off}+{nb}>{self.nbytes}"
        a = self.ap[0:shape[0], self.off // 2:(self.off + nel * es) // 2]
        self.off += nb
        self.peak = max(self.peak, self.off)
        if dtype == F32:
            a = a.bitcast(F32)
        if len(shape) == 3:
            a = a.rearrange("p (a b) -> p a b", a=shape[1], b=shape[2])
        elif len(shape) == 4:
            a = a.rearrange("p (a b c) -> p a b c", a=shape[1], b=shape[2], c=shape[3])
        elif len(shape) == 5:
            a = a.rearrange("p (a b c d) -> p a b c d", a=shape[1], b=shape[2], c=shape[3],
                            d=shape[4])
        return a


class Rot:
    def __init__(self, arena, name, n, shape, dtype):
        self.bufs = [arena.alloc(shape, dtype) for _ in range(n)]
        self.name = name
        self.i = 0

    def get(self):
        j = self.i % len(self.bufs)
        self.i += 1
        return self.bufs[j], (self.name, j)


def build_program(debug=False, nlayers=2, stop_after=None, a_groups=9, a_stage=9):
    nc = bass.Bass("TRN2", target_bir_lowering=False)
    Sched.ALIAS = {}
    for b_ in (2, 3, 4, 6):
        Sched.ALIAS[f"ps{b_}"] = [f"ps{b_}"] + [(f"ps{b_}", s_) for s_ in range(4)]
    for b_ in (0, 1):
        Sched.ALIAS[f"ps{b_}"] = [f"ps{b_}"] + [(f"ps{b_}", s_) for s_ in range(2)]
    from contextlib import ExitStack
    stack = ExitStack()
    S = Sched()

    def din(name, shape, dt=F32):
        return nc.dram_tensor(name, list(shape), dt, kind="ExternalInput").ap()

    def dscr(name, shape, dt):
        return nc.dram_tensor(name, list(shape), dt,
                              kind="ExternalOutput" if debug else "Internal").ap()

    xin = din("xin", [T, D])
    cc_d = din("cc", [128, 8, 2])
    wmod_d = din("w_mod", [2, D, 3 * D])
    bmod_d = din("b_mod", [2, 3 * D])
    win_d = din("w_in", [2, D, WCOLS])
    wout_d = din("w_out", [2, D, D])
    wuq_d = din("w_uq", [2, 256, 1024])
    wukv_d = din("w_ukv", [2, 128, 1024])
    wa2_d = din("w_a2", [2, 2, 16, 128])
    lng_d = din("ln_g", [2, D])
    lnb_d = din("ln_b", [2, D])
    vecs_d = din("vecs", [128, NVEC])
    ropeC_d = din("ropeC", [64, NLAT])
    ropeS_d = din("ropeS", [64, NLAT])
    cf32_d = din("cf32", [128, 6, 128])
    reset_d = din("resetm", [128, 512])
    out_d = nc.dram_tensor("out", [NLAT, D], F32, kind="ExternalOutput").ap()

    x1_d = dscr("x1", [T, D], F32)
    qn_d = dscr("qn_s", [128, 4, T], BF16)
    qr_d = dscr("qr_s", [64, 4, T], BF16)
    kn_d = dscr("kn_s", [128, 4, T], BF16)
    kr_d = dscr("kr_s", [64, T], BF16)
    vm_d = dscr("vm_s", [128, NT, 512], BF16)
    sgm_d = dscr("sgm_s", [128, 4, T], BF16)
    sgr_d = dscr("sgr_s", [128, 4, T], BF16)
    oin_d = dscr("oin_s", [128, 4, T], F32)
    qt_d = dscr("qt_s", [128, 2, 3, T], BF16)
    U_d = dscr("U_s", [128, 2, 3, NCH, 64], F32)
    snap_d = dscr("snap_s", [128, 2, 3, NCH, 64], BF16)

    ARENA_BYTES = 190 * 1024
    arena_t = stack.enter_context(nc.sbuf_tensor("arena", [128, ARENA_BYTES // 2], BF16))
    PB = 12 * 1024
    pers_t = stack.enter_context(nc.sbuf_tensor("pers", [128, PB // 2], BF16))
    pers = Arena(pers_t, PB)
    ar = Arena(arena_t, ARENA_BYTES)
    psb = [stack.enter_context(nc.psum_tensor(f"ps{i}", [128, 512], F32)) for i in range(8)]
    ps = [p[:, :] for p in psb]

    cf32 = pers.alloc([128, 6, 128], F32)
    ident_f, ones_f, blk_f = cf32[:, 0, :], cf32[:, 1, :], cf32[:, 2, :]
    maskf = cf32[:, 3:5, :]
    cbf = pers.alloc([128, 2, 128], BF16)
    ident_b, ones_b = cbf[:, 0, :], cbf[:, 1, :]
    resetm = pers.alloc([128, 512], F32)
    vecs = pers.alloc([128, NVEC], F32)
    cc = pers.alloc([128, 8, 2], F32)
    sc = pers.alloc([128, 8, 2], F32)
    modT = pers.alloc([128, 16, 2], F32)
    lbv = pers.alloc([128, 4, 4], F32)
    A_sb = pers.alloc([128, 3, 2, NCH], F32)
    E_sb = pers.alloc([128, 3, 2, NCH], F32)
    R_sb = pers.alloc([128, 3, 2, NCH], F32)
    st6 = pers.alloc([128, 2, 6], F32)
    mv = pers.alloc([128, 2], F32)
    rstd = pers.alloc([128, 1], F32)
    tmpc = pers.alloc([128, 16], F32)
    negb = pers.alloc([128, 2], F32)

    def V(col):
        return vecs[:, col:col + 1]

    def dma(eng, out, in_, r, w):
        S.add(eng, lambda e: e.dma_start(out=out, in_=in_), r=r, w=w, dma=True)

    dma("sync", cf32, cf32_d, [], ["cf32"])
    dma("sync", resetm, reset_d, [], ["resetm"])
    dma("sync", vecs, vecs_d, [], ["vecs"])
    dma("sync", cc, cc_d, [], ["cc"])
    S.add("vector", lambda e: e.tensor_copy(out=cbf[:, 0, :], in_=cf32[:, 0, :]), r=["cf32"],
          w=["cbf0"])
    S.add("vector", lambda e: e.tensor_copy(out=cbf[:, 1, :], in_=cf32[:, 1, :]), r=["cf32"],
          w=["cbf1"])
    S.add("scalar", lambda e: e.activation(out=sc, in_=cc, func=AF.Silu), r=["cc"], w=["sc"])
    CONST_R = ["cf32", "cbf0", "cbf1", "resetm", "vecs"]

    for l in range(nlayers):
        xsrc = xin if l == 0 else x1_d
        xsrc_key = "xin" if l == 0 else "x1"
        last = (l == nlayers - 1)
        ar.reset()
        wm_rot = Rot(ar, "wm", 2, [128, 8, 512], F32)
        screp = ar.alloc([128, 2, 8, 128], F32)
        gate_ps_key = "ps1"
        gate_bc = pers_gate = None
        GATE_OFF = ARENA_BYTES - 16 * 1024
        gb = arena_t[:, GATE_OFF // 2:(GATE_OFF + 8192) // 2].bitcast(F32).rearrange(
            "p (a b) -> p a b", a=2, b=1024)
        lnbc = arena_t[:, (GATE_OFF + 8192) // 2:(GATE_OFF + 16384) // 2].bitcast(F32).rearrange(
            "p (a b) -> p a b", a=2, b=1024)
        ar.nbytes = GATE_OFF
        bmg = ar.alloc([128, 1024], F32)
        for j in range(2):
            S.add("vector", lambda e, j=j: e.tensor_copy(
                out=screp[:, j, :, :], in_=sc[:, :, j:j + 1].broadcast_to((128, 8, 128))),
                r=["sc"], w=[("screp", j)])
        dma("sync", bmg, bmod_d[l:l + 1, 2048:3072].partition_broadcast(128)
            .rearrange("p a b -> p (a b)"), [], ["bmg"])
        dma("sync", lnbc[:, 0, :], lng_d[l:l + 1, :].partition_broadcast(128)
            .rearrange("p a b -> p (a b)"), [], ["lnbc0"])
        dma("sync", lnbc[:, 1, :], lnb_d[l:l + 1, :].partition_broadcast(128)
            .rearrange("p a b -> p (a b)"), [], ["lnbc1"])
        wmv = wmod_d[l].rearrange("(k p) n -> p k n", p=128)
        for blk in range(6):
            wm, wk = wm_rot.get()
            dma("sync", wm, wmv[:, :, blk * 512:(blk + 1) * 512], [], [wk])
            if blk < 4:
                for j in range(4):
                    ci = blk * 4 + j
                    for k in range(8):
                        S.add("tensor", lambda e, k=k, j=j, ci=ci, wm=wm: e.matmul(
                            ps[0][:, ci * 2:ci * 2 + 2], lhsT=wm[:, k, j * 128:(j + 1) * 128],
                            rhs=sc[:, k, 0:2], start=(k == 0), stop=(k == 7)),
                            r=[wk, "sc"], w=["ps0"])
            else:
                for j in range(2):
                    for k in range(8):
                        S.add("tensor", lambda e, k=k, j=j, wm=wm: e.matmul(
                            ps[1 + j], lhsT=screp[:, j, k, :], rhs=wm[:, k, :],
                            start=(k == 0), stop=(k == 7)),
                            r=[wk, ("screp", j)], w=[f"ps{1 + j}"])
                    hb = blk - 4
                    S.add("vector", lambda e, j=j, hb=hb: e.tensor_tensor(
                        out=gb[:, j, hb * 512:(hb + 1) * 512], in0=ps[1 + j],
                        in1=bmg[:, hb * 512:(hb + 1) * 512], op=ALU.add),
                        r=[f"ps{1 + j}", "bmg"], w=[("gb", j, hb)])
        bm = vecs[:, 22 + 24 * l:22 + 24 * l + 16]
        S.add("vector", lambda e, bm=bm: e.tensor_tensor(
            out=modT, in0=ps[0][:, 0:32].rearrange("p (a b) -> p a b", a=16, b=2),
            in1=bm.unsqueeze(2).broadcast_to((128, 16, 2)), op=ALU.add),
            r=["ps0", "vecs"], w=["modT"])
        S.add("vector", lambda e: e.tensor_scalar(out=modT[:, 8:16, :], in0=modT[:, 8:16, :],
                                                   scalar1=1.0, scalar2=None, op0=ALU.add),
              r=["modT"], w=["modT"])
        if l == 0:
            S.add("vector", lambda e: e.memset(lbv[:, :, 0:1], 0.0), w=["lbv0"])
        else:
            S.add("vector", lambda e: e.tensor_tensor(
                out=tmpc[:, 0:4], in0=vecs[:, 4:8], in1=vecs[:, 0:4], op=ALU.subtract),
                r=["vecs"], w=["tmpc"])
            S.add("scalar", lambda e: e.activation(out=lbv[:, :, 0], in_=tmpc[:, 0:4],
                                                   func=AF.Sigmoid), r=["tmpc"], w=["lbv0"])
        S.add("vector", lambda e: e.tensor_scalar(out=lbv[:, :, 1:2], in0=lbv[:, :, 0:1],
                                                   scalar1=-1.0, scalar2=1.0, op0=ALU.mult,
                                                   op1=ALU.add), r=["lbv0"], w=["lbv1"])
        S.add("vector", lambda e: e.tensor_scalar(out=lbv[:, :, 2:3], in0=lbv[:, :, 1:2],
                                                   scalar1=-1.0, scalar2=None, op0=ALU.mult),
              r=["lbv1"], w=["lbv2"])
        S.add("vector", lambda e: e.reciprocal(out=tmpc[:, 4:8], in_=lbv[:, :, 1]),
              r=["lbv1", "tmpc"], w=["tmpc2"])
        S.add("vector", lambda e: e.tensor_scalar(out=tmpc[:, 8:12], in0=lbv[:, :, 0],
                                                   scalar1=-1.0, scalar2=1e-6, op0=ALU.mult,
                                                   op1=ALU.add), r=["lbv0", "tmpc2"], w=["tmpc3"])
        S.add("vector", lambda e: e.tensor_tensor(out=lbv[:, :, 3], in0=tmpc[:, 8:12],
                                                  in1=tmpc[:, 4:8], op=ALU.mult),
              r=["tmpc3", "tmpc2"], w=["lbv3"])
        LBV_R = ["lbv0", "lbv1", "lbv2", "lbv3"]
        S.add("vector", lambda e: e.tensor_scalar(out=negb, in0=vecs[:, 14 + 2 * l:16 + 2 * l],
                                                   scalar1=-1.0, scalar2=None, op0=ALU.mult),
              r=["vecs"], w=["negb"])
        S.barrier()
        if debug and l == 0:
            dbg_mod = nc.dram_tensor("dbg_mod", [128, 32], F32, kind="ExternalOutput").ap()
            dbg_gb = nc.dram_tensor("dbg_gb", [128, 2, 1024], F32, kind="ExternalOutput").ap()
            dbg_lbv = nc.dram_tensor("dbg_lbv", [128, 16], F32, kind="ExternalOutput").ap()
            dma("sync", dbg_mod, modT.rearrange("p a b -> p (a b)"), ["modT"], ["dbg_mod"])
            dma("sync", dbg_gb, gb, [("gb", j, hb) for j in range(2) for hb in range(2)], ["dbg_gb"])
            dma("sync", dbg_lbv, lbv.rearrange("p a b -> p (a b)"), LBV_R, ["dbg_lbv"])
            S.barrier()
        if stop_after == "mod":
            break

        ar.reset()
        win = ar.alloc([128, 8, WCOLS], BF16)
        wuq = ar.alloc([128, 2, 1024], BF16)
        wukv = ar.alloc([128, 1024], BF16)
        wa2 = ar.alloc([16, 2, 128], BF16)
        winv = win_d[l].rearrange("(k p) n -> p k n", p=128)
        for k in range(8):
            for hf in range(2):
                dma("gpsimd", win[:, k, hf * 1552:(hf + 1) * 1552],
                    winv[:, k, hf * 1552:(hf + 1) * 1552], [], [("win", k, hf)])
        WIN_R = [("win", k, hf) for k in range(8) for hf in range(2)]
        dma("gpsimd", wuq, wuq_d[l].rearrange("(c p) n -> p c n", p=128), [], ["wuq"])
        dma("gpsimd", wukv, wukv_d[l], [], ["wukv"])
        dma("gpsimd", wa2, wa2_d[l].rearrange("d r n -> r d n"), [], ["wa2"])

        xt_rot = Rot(ar, "xt", 2, [128, 1024], F32)
        hTs = [ar.alloc([128, 8, 512], BF16), ar.alloc([128, 8, 512], BF16)]
        f32p = Rot(ar, "f32p", 8, [128, 512], F32)
        ded = {nm: (ar.alloc([128, 512], F32), "ded_" + nm) for nm in ("hq0", "hq1", "gq", "gk", "rC", "rS")}
        sg_rot = Rot(ar, "sgst", 3, [128, 512], BF16)
        f32m = Rot(ar, "f32m", 6, [128, 512], F32)
        qn_rot = Rot(ar, "qn", 2, [128, 512], BF16)
        qr_rot = Rot(ar, "qr", 2, [128, 512], BF16)
        kn_rot = Rot(ar, "kn", 1, [128, 4, 512], BF16)
        krt_rot = Rot(ar, "krt", 1, [128, 512], BF16)
        vm_rot = Rot(ar, "vm", 1, [128, 4, 512], BF16)
        vtm_rot = Rot(ar, "vtm", 1, [128, 4, 512], BF16)
        cqn_rot = Rot(ar, "cqn", 1, [128, 3, 512], BF16)
        za_rot = Rot(ar, "za", 1, [128, 2, 512], BF16)
        qt_rot = Rot(ar, "qt", 1, [128, 2, 3, 512], BF16)
        kt_rot = Rot(ar, "kt", 1, [128, 2, 3, 512], BF16)
        ktm_rot = Rot(ar, "ktm", 2, [128, 2, 3, 128], BF16)
        pTH_rot = Rot(ar, "pTH", 2, [128, 2, 4, 128], BF16)
        pTG_rot = Rot(ar, "pTG", 2, [128, 4, 2, 128], BF16)
        for rot_ in (pTH_rot, pTG_rot):
            for bi, b_ in enumerate(rot_.bufs):
                S.add("gpsimd", lambda e, b_=b_: e.memset(b_.rearrange("p a b c -> p (a b c)"), 0.0),
                      w=[((rot_.name, bi), x_) for x_ in range(4)])
        maskH = ar.alloc([128, 4, 128], F32)
        for dr_ in range(2):
            S.add("vector", lambda e, dr_=dr_: e.tensor_copy(
                out=maskH[:, 2 * dr_:2 * dr_ + 2, :],
                in_=maskf[:, dr_, :].unsqueeze(1).broadcast_to((128, 2, 128))),
                r=["cf32"], w=[("mask4", dr_)])
        maskH_u = maskH.bitcast(mybir.dt.uint32)
        maskG_u = maskf.bitcast(mybir.dt.uint32)
        oin_rot = Rot(ar, "oin", 1, [128, 4, 128], F32)
        U_rot = Rot(ar, "Ut", 1, [128, 2, 3, 128], F32)
        ps5b = ps[5].bitcast(BF16)
        fm_i = [0]

        def fm_bank():
            if S.cur == "A":
                return ps[4], "ps4"
            j = 2 + fm_i[0] % 2
            fm_i[0] += 1
            return ps[j], f"ps{j}"

        _slot_keys = {f"ps{b_}": [(f"ps{b_}", s_) for s_ in range(4)] for b_ in (2, 3, 4, 6)}

        for gi, (g0, n) in enumerate(GROUPS):
            if gi >= a_groups:
                continue
            ntl = n // 128
            nch = n // 64
            ch0 = g0 // 64
            mc = 1 if gi == 0 else 0
            isctx = gi == 0
            if a_stage < 1:
                continue
            hb = gi % 2
            hT = hTs[hb]

            def emit_LN(gi_):
                g0_, n_ = GROUPS[gi_]
                mc_ = 1 if gi_ == 0 else 0
                hb_ = gi_ % 2
                hT_ = hTs[hb_]
                for j in range(n_ // 128):
                    tok0 = g0_ + j * 128
                    xt, xk = xt_rot.get()
                    dma("sync", xt, xsrc[tok0:tok0 + 128, :], [(xsrc_key, tok0 // 128)], [xk])
                    S.add("vector", lambda e: e.bn_stats(out=st6[:, 0, :], in_=xt[:, 0:512]),
                          r=[xk], w=["st6a"])
                    S.add("vector", lambda e: e.bn_stats(out=st6[:, 1, :], in_=xt[:, 512:1024]),
                          r=[xk], w=["st6b"])
                    S.add("vector", lambda e: e.bn_aggr(out=mv, in_=st6.rearrange("p a b -> p (a b)")),
                          r=["st6a", "st6b"], w=["mv"])
                    S.add("scalar", lambda e: e.activation(out=rstd, in_=mv[:, 1:2], func=AF.Ln,
                                                           bias=EPS), r=["mv"], w=["rstd"])
                    S.add("scalar", lambda e: e.activation(out=rstd, in_=rstd, func=AF.Exp,
                                                           scale=-0.5), r=["rstd"], w=["rstd"])
                    S.add("vector", lambda e: e.tensor_scalar(
                        out=xt, in0=xt, scalar1=mv[:, 0:1], scalar2=rstd, op0=ALU.subtract,
                        op1=ALU.mult), r=[xk, "mv", "rstd"], w=[xk])
                    for k in range(8):
                        S.add("tensor", lambda e: e.transpose(
                            out=ps[k // 4][:, (k % 4) * 128:(k % 4 + 1) * 128],
                            in_=xt[:, k * 128:(k + 1) * 128], identity=ident_f),
                            r=[xk, "cf32"], w=[f"ps{k // 4}"])
                    for k in range(8):
                        if k < 4:
                            S.add("scalar", lambda e: e.activation(
                                out=hT_[:, k, j * 128:(j + 1) * 128],
                                in_=ps[k // 4][:, (k % 4) * 128:(k % 4 + 1) * 128], func=AF.Identity,
                                scale=modT[:, 8 + k, mc_:mc_ + 1], bias=modT[:, k, mc_:mc_ + 1]),
                                r=[f"ps{k // 4}", "modT"], w=[("hT", hb_, k, j)])
                        else:
                            S.add("vector", lambda e: e.tensor_scalar(
                                out=hT_[:, k, j * 128:(j + 1) * 128],
                                in0=ps[k // 4][:, (k % 4) * 128:(k % 4 + 1) * 128],
                                scalar1=modT[:, 8 + k, mc_:mc_ + 1], scalar2=modT[:, k, mc_:mc_ + 1],
                                op0=ALU.mult, op1=ALU.add),
                                r=[f"ps{k // 4}", "modT"], w=[("hT", hb_, k, j)])

            if gi == 0:
                emit_LN(0)
            HT_R = [("hT", hb, k, j) for k in range(8) for j in range(ntl)]

            def fm_chain(col0, m):
                bank, bk = fm_bank()
                for k in range(8):
                    S.add("tensor", lambda e, k=k, bank=bank: e.matmul(
                        bank[0:m, 0:n], lhsT=win[:, k, col0:col0 + m], rhs=hT[:, k, 0:n],
                        start=(k == 0), stop=(k == 7)),
                        r=WIN_R + HT_R, w=[bk])
                return bank, bk

            if debug and gi == 0 and l == 0:
                dbg_hT = nc.dram_tensor("dbg_hT", [128, 8, 256], BF16, kind="ExternalOutput").ap()
                dma("sync", dbg_hT, hTs[0][:, :, 0:256], HT_R, ["dbg_hT"])
            if a_stage < 2:
                continue
            vtm, vtmk = vtm_rot.get()
            for j in range(ntl):
                for k in range(8):
                    S.add("tensor", lambda e, k=k, j=j: e.matmul(
                        ps[5], lhsT=hT[:, k, j * 128:(j + 1) * 128], rhs=win[:, k, C_HI:C_HI + 512],
                        start=(k == 0), stop=(k == 7)), r=WIN_R + HT_R, w=["ps5"])
                S.add("scalar", lambda e, j=j, vtm=vtm: e.copy(out=vtm[:, j, :], in_=ps[5]),
                      r=["ps5"], w=[(vtmk, j)])
            VTM_R = [(vtmk, j) for j in range(ntl)]

            hq = []
            for c in range(2):
                bank, bk = fm_chain(C_ZQ + c * 128, 128)
                t, tk = ded["hq%d" % c]
                S.add("scalar", lambda e, t=t, bank=bank: e.activation(
                    out=t[:, 0:n], in_=bank[:, 0:n], func=AF.Silu), r=[bk], w=[tk])
                hq.append((t, tk))
            for (c0, dst, coff) in ((C_HG, sgr_d, 0), (C_GG, sgr_d, 2), (C_MG, sgm_d, 0)):
                for c in range(2 if dst is sgr_d else 4):
                    bank, bk = fm_chain(c0 + c * 128, 128)
                    sgt, sgtk = sg_rot.get()
                    S.add("scalar", lambda e, bank=bank, sgt=sgt: e.activation(
                        out=sgt[:, 0:n], in_=bank[:, 0:n], func=AF.Silu), r=[bk], w=[sgtk])
                    nm = "sgr_d" if dst is sgr_d else "sgm_d"
                    dma("sync", dst[:, coff + c, g0:g0 + n], sgt[:, 0:n], [sgtk], [(nm, gi, coff + c)])
            S.capture("B")
            cqn, cqnk = cqn_rot.get()
            zc = []
            for c in range(3):
                bank, bk = fm_chain(C_ZCQ + c * 128, 128)
                t, tk = f32m.get()
                S.add("vector", lambda e, t=t, bank=bank: e.tensor_copy(out=t[:, 0:n],
                                                                        in_=bank[:, 0:n]),
                      r=[bk], w=[tk])
                zc.append((t, tk))
            for which in range(2):
                idx = [0, 1] if which == 0 else [2]
                rank = 256.0 if which == 0 else 128.0
                sqs = []
                for c in idx:
                    sq, sqk = f32m.get()
                    S.add("scalar", lambda e, sq=sq, c=c: e.activation(
                        out=sq[:, 0:n], in_=zc[c][0][:, 0:n], func=AF.Square),
                        r=[zc[c][1]], w=[sqk])
                    sqs.append((sq, sqk))
                bank, bk = fm_bank()
                for i, (sq, sqk) in enumerate(sqs):
                    S.add("tensor", lambda e, sq=sq, i=i, bank=bank: e.matmul(
                        bank[:, 0:n], lhsT=ones_f, rhs=sq[:, 0:n], start=(i == 0),
                        stop=(i == len(sqs) - 1)), r=[sqk, "cf32"], w=[bk])
                rr, rrk = f32m.get()
                S.add("scalar", lambda e, rr=rr, bank=bank, rank=rank: e.activation(
                    out=rr[:, 0:n], in_=bank[:, 0:n], func=AF.Ln, scale=1.0 / rank, bias=EPS),
                    r=[bk], w=[rrk])
                S.add("scalar", lambda e, rr=rr: e.activation(
                    out=rr[:, 0:n], in_=rr[:, 0:n], func=AF.Exp, scale=-0.5), r=[rrk], w=[rrk])
                for c in idx:
                    gcol = V(8 + l * 2 + c) if which == 0 else V(12 + l)
                    S.add("vector", lambda e, c=c, gcol=gcol, rr=rr, cqn=cqn: e.scalar_tensor_tensor(
                        out=cqn[:, c, 0:n], in0=zc[c][0][:, 0:n], scalar=gcol, in1=rr[:, 0:n],
                        op0=ALU.mult, op1=ALU.mult), r=[zc[c][1], rrk, "vecs"], w=[(cqnk, c)])
            if not isctx:
                rC, rCk = ded["rC"]
                rS, rSk = ded["rS"]
                dma("sync", rC[0:64, 0:n], ropeC_d[:, g0 - LCTX:g0 - LCTX + n], [], [rCk])
                dma("sync", rS[0:64, 0:n], ropeS_d[:, g0 - LCTX:g0 - LCTX + n], [], [rSk])

            def rope_evac(bank_a, bka, bank_b, bkb, out_ap, outk, scale):
                if isctx:
                    S.add("scalar", lambda e: e.activation(out=out_ap, in_=bank_a[0:64, 0:n],
                                                           func=AF.Identity, scale=scale),
                          r=[bka], w=[outk])
                    return
                t1, t1k = f32m.get()
                t2, t2k = f32m.get()
                S.add("vector", lambda e: e.tensor_tensor(out=t1[0:64, 0:n], in0=bank_a[0:64, 0:n],
                                                          in1=rC[0:64, 0:n], op=ALU.mult),
                      r=[bka, rCk], w=[t1k])
                S.add("vector", lambda e: e.tensor_tensor(out=t2[0:64, 0:n], in0=bank_b[0:64, 0:n],
                                                          in1=rS[0:64, 0:n], op=ALU.mult),
                      r=[bkb, rSk], w=[t2k])
                S.add("vector", lambda e: e.scalar_tensor_tensor(
                    out=out_ap, in0=t1[0:64, 0:n], scalar=scale, in1=t2[0:64, 0:n],
                    op0=ALU.mult, op1=ALU.add), r=[t1k, t2k], w=[outk])

            krt, krtk = krt_rot.get()
            bank_a, bka = fm_chain(C_KR, 64)
            if not isctx:
                bank_b, bkb = fm_chain(C_KRS, 64)
            else:
                bank_b, bkb = bank_a, bka
            rope_evac(bank_a, bka, bank_b, bkb, krt[0:64, 0:n], krtk, 1.0)
            dma("sync", kr_d[:, g0:g0 + n], krt[0:64, 0:n], [krtk], [("kr_d", gi)])

            for h in range(4):
                qn, qnk = qn_rot.get()
                qr, qrk = qr_rot.get()
                bank, bk = fm_bank()
                for c in range(2):
                    S.add("tensor", lambda e, c=c, h=h, bank=bank: e.matmul(
                        bank[:, 0:n], lhsT=wuq[:, c, h * 128:(h + 1) * 128], rhs=cqn[:, c, 0:n],
                        start=(c == 0), stop=(c == 1)), r=["wuq", (cqnk, c)], w=[bk])
                S.add("scalar", lambda e, h=h, bank=bank, qn=qn: e.activation(
                    out=qn[:, 0:n], in_=bank[:, 0:n], func=AF.Identity, scale=MLA_SCALE),
                    r=[bk], w=[qnk])
                dma("sync", qn_d[:, h, g0:g0 + n], qn[:, 0:n], [qnk], [("qn_d", gi, h)])
                banka, bka = fm_bank()
                for c in range(2):
                    S.add("tensor", lambda e, c=c, h=h, banka=banka: e.matmul(
                        banka[0:64, 0:n], lhsT=wuq[:, c, 512 + h * 128:512 + h * 128 + 64],
                        rhs=cqn[:, c, 0:n], start=(c == 0), stop=(c == 1)),
                        r=["wuq", (cqnk, c)], w=[bka])
                if not isctx:
                    bankb, bkb = fm_bank()
                    for c in range(2):
                        S.add("tensor", lambda e, c=c, h=h, bankb=bankb: e.matmul(
                            bankb[0:64, 0:n], lhsT=wuq[:, c, 512 + h * 128 + 64:512 + h * 128 + 128],
                            rhs=cqn[:, c, 0:n], start=(c == 0), stop=(c == 1)),
                            r=["wuq", (cqnk, c)], w=[bkb])
                    t1, t1k = f32m.get()
                    t2, t2k = f32m.get()
                    S.add("vector", lambda e, t1=t1, banka=banka: e.scalar_tensor_tensor(
                        out=t1[0:64, 0:n], in0=banka[0:64, 0:n], scalar=MLA_SCALE,
                        in1=rC[0:64, 0:n], op0=ALU.mult, op1=ALU.mult), r=[bka, rCk], w=[t1k])
                    S.add("vector", lambda e, t2=t2, bankb=bankb: e.scalar_tensor_tensor(
                        out=t2[0:64, 0:n], in0=bankb[0:64, 0:n], scalar=MLA_SCALE,
                        in1=rS[0:64, 0:n], op0=ALU.mult, op1=ALU.mult), r=[bkb, rSk], w=[t2k])
                    S.add("gpsimd", lambda e, t1=t1, t2=t2, h=h, qr=qr: e.tensor_tensor(
                        out=qr[0:64, 0:n], in0=t1[0:64, 0:n], in1=t2[0:64, 0:n], op=ALU.add),
                        r=[t1k, t2k], w=[qrk])
                else:
                    S.add("scalar", lambda e, h=h, banka=banka, qr=qr: e.activation(
                        out=qr[0:64, 0:n], in_=banka[0:64, 0:n], func=AF.Identity,
                        scale=MLA_SCALE), r=[bka], w=[qrk])
                dma("sync", qr_d[:, h, g0:g0 + n], qr[0:64, 0:n], [qrk], [("qr_d", gi, h)])
            kn, knk = kn_rot.get()
            for h in range(4):
                bank, bk = fm_bank()
                S.add("tensor", lambda e, h=h, bank=bank: e.matmul(
                    bank[:, 0:n], lhsT=wukv[:, h * 128:(h + 1) * 128], rhs=cqn[:, 2, 0:n],
                    start=True, stop=True), r=["wukv", (cqnk, 2)], w=[bk])
                S.add("vector", lambda e, h=h, bank=bank, kn=kn: e.tensor_copy(
                    out=kn[:, h, 0:n], in_=bank[:, 0:n]), r=[bk], w=[(knk, h)])
            dma("sync", kn_d[:, :, g0:g0 + n], kn[:, :, 0:n], [(knk, h) for h in range(4)],
                [("kn_d", gi)])
            vm, vmk = vm_rot.get()
            for j in range(ntl):
                bank, bk = fm_bank()
                S.add("tensor", lambda e, j=j, bank=bank: e.matmul(
                    bank, lhsT=cqn[:, 2, j * 128:(j + 1) * 128], rhs=wukv[:, 512:1024],
                    start=True, stop=True), r=["wukv", (cqnk, 2)], w=[bk])
                S.add("scalar", lambda e, j=j, bank=bank, vm=vm: e.copy(out=vm[:, j, :], in_=bank),
                      r=[bk], w=[(vmk, j)])
            dma("sync", vm_d[:, g0 // 128:g0 // 128 + ntl, :], vm[:, 0:ntl, :],
                [(vmk, j) for j in range(ntl)], [("vm_d", gi)])

            S.capture("A")
            qt, qtk = qt_rot.get()
            kt, ktk = kt_rot.get()

            def prep(mt, dr, q, qk, k_, kk, g, gk, esc):
                P, Pk = f32p.get()
                S.add("vector", lambda e: e.tensor_tensor_scan(
                    out=P[:, 0:n], data0=resetm[:, 0:n], data1=g[:, 0:n], initial=0.0,
                    op0=ALU.mult, op1=ALU.add), r=[gk, "resetm"], w=[Pk])
                P3 = P[:, 0:n].rearrange("p (c t) -> p c t", t=64)
                if dr == 0:
                    b, bk_ = P, Pk
                    mid = 31
                else:
                    b, bk_ = f32p.get()
                    S.add("vector", lambda e: e.tensor_tensor(out=b[:, 0:n], in0=g[:, 0:n],
                                                              in1=P[:, 0:n], op=ALU.subtract),
                          r=[gk, Pk], w=[bk_])
                    b3_ = b[:, 0:n].rearrange("p (c t) -> p c t", t=64)
                    S.add("vector", lambda e: e.tensor_tensor(
                        out=b3_, in0=b3_, in1=P3[:, :, 63:64].broadcast_to((128, nch, 64)),
                        op=ALU.add), r=[bk_, Pk], w=[bk_])
                    mid = 32
                b3 = b[:, 0:n].rearrange("p (c t) -> p c t", t=64)
                d, dk_ = f32p.get()
                d3 = d[:, 0:n].rearrange("p (c t) -> p c t", t=64)
                S.add("gpsimd", lambda e: e.tensor_tensor(
                    out=d3, in0=b3, in1=b3[:, :, mid:mid + 1].broadcast_to((128, nch, 64)),
                    op=ALU.subtract), r=[bk_], w=[dk_])
                import os as _os2
                _cl = 40.0 if _os2.environ.get("KCLAMP") else 80.0
                S.add("gpsimd", lambda e: e.tensor_scalar(out=d[:, 0:n], in0=d[:, 0:n], scalar1=_cl,
                                                           scalar2=-_cl, op0=ALU.min, op1=ALU.max),
                      r=[dk_], w=[dk_])
                Bc = P3[:, :, 63]
                rc = b3[:, :, mid]
                S.add("scalar", lambda e: e.activation(out=A_sb[:, mt, dr, ch0:ch0 + nch], in_=Bc,
                                                       func=AF.Exp, scale=esc),
                      r=[Pk], w=[("A", mt, dr, gi)])
                S.add("scalar", lambda e: e.activation(out=R_sb[:, mt, dr, ch0:ch0 + nch], in_=rc,
                                                       func=AF.Exp, scale=esc),
                      r=[bk_], w=[("R", mt, dr, gi)])
                S.add("vector", lambda e: e.tensor_tensor(out=tmpc[:, 0:nch], in0=Bc, in1=rc,
                                                          op=ALU.subtract),
                      r=[Pk, bk_, "tmpc", "tmpc2", "tmpc3"], w=["tmpc"])
                S.add("scalar", lambda e: e.activation(out=E_sb[:, mt, dr, ch0:ch0 + nch],
                                                       in_=tmpc[:, 0:nch], func=AF.Exp, scale=esc),
                      r=["tmpc"], w=[("E", mt, dr, gi)])
                e1, e1k = f32p.get()
                e2, e2k = f32p.get()
                S.add("scalar", lambda e: e.activation(out=e1[:, 0:n], in_=d[:, 0:n], func=AF.Exp,
                                                       scale=esc), r=[dk_], w=[e1k])
                S.add("scalar", lambda e: e.activation(out=e2[:, 0:n], in_=d[:, 0:n], func=AF.Exp,
                                                       scale=-esc), r=[dk_], w=[e2k])
                S.add("gpsimd", lambda e: e.tensor_tensor(out=qt[:, dr, mt, 0:n], in0=q[:, 0:n],
                                                          in1=e1[:, 0:n], op=ALU.mult),
                      r=[qk, e1k], w=[(qtk, dr, mt)])
                S.add("gpsimd", lambda e: e.tensor_tensor(out=kt[:, dr, mt, 0:n], in0=k_[:, 0:n],
                                                          in1=e2[:, 0:n], op=ALU.mult),
                      r=[kk, e2k], w=[(ktk, dr, mt)])

            for c in range(2):
                for dr in range(2):
                    bank, bk = fm_chain((C_ZFF if dr == 0 else C_ZFB) + c * 128, 128)
                    sg, sgk = f32p.get()
                    gt, gtk = f32p.get()
                    li = dr * 2 + c
                    S.add("scalar", lambda e, sg=sg, bank=bank: e.activation(
                        out=sg[:, 0:n], in_=bank[:, 0:n], func=AF.Exp, scale=-1.0), r=[bk], w=[sgk])
                    S.add("scalar", lambda e, sg=sg, gt=gt: e.activation(
                        out=gt[:, 0:n], in_=sg[:, 0:n], func=AF.Ln, bias=1.0), r=[sgk], w=[gtk])
                    S.add("scalar", lambda e, sg=sg, gt=gt: e.activation(
                        out=sg[:, 0:n], in_=gt[:, 0:n], func=AF.Exp, scale=-1.0), r=[gtk], w=[sgk])
                    S.add("vector", lambda e, sg=sg, gt=gt, li=li: e.tensor_scalar(
                        out=gt[:, 0:n], in0=sg[:, 0:n], scalar1=lbv[:, li, 3:4], scalar2=None,
                        op0=ALU.max), r=[sgk] + LBV_R, w=[gtk])
                    S.add("scalar", lambda e, gt=gt, li=li: e.activation(
                        out=gt[:, 0:n], in_=gt[:, 0:n], func=AF.Ln, scale=lbv[:, li, 1:2],
                        bias=lbv[:, li, 0:1]), r=[gtk] + LBV_R, w=[gtk])
                    S.add("vector", lambda e, sg=sg, li=li: e.tensor_scalar(
                        out=sg[:, 0:n], in0=sg[:, 0:n], scalar1=lbv[:, li, 2:3],
                        scalar2=lbv[:, li, 1:2], op0=ALU.mult, op1=ALU.add),
                        r=[sgk] + LBV_R, w=[sgk])
                    prep(c, dr, hq[c][0], hq[c][1], sg, sgk, gt, gtk, 1.0)
            bank, bk = fm_chain(C_GQ, 128)
            gq, gqk = ded["gq"]
            S.add("scalar", lambda e, gq=gq, bank=bank: e.activation(
                out=gq[:, 0:n], in_=bank[:, 0:n], func=AF.Identity, scale=32.0 ** -0.5),
                r=[bk], w=[gqk])
            bank, bk = fm_chain(C_GK, 128)
            gk_, gkk = ded["gk"]
            S.add("vector", lambda e, gk_=gk_, bank=bank: e.tensor_copy(out=gk_[:, 0:n],
                                                                        in_=bank[:, 0:n]),
                  r=[bk], w=[gkk])
            za, zak = za_rot.get()
            for dr in range(2):
                bank, bk = fm_chain(C_GAF + 16 * dr, 16)
                S.add("vector", lambda e, dr=dr, bank=bank, za=za: e.tensor_copy(
                    out=za[0:16, dr, 0:n], in_=bank[0:16, 0:n]), r=[bk], w=[(zak, dr)])
                bank2, bk2 = fm_bank()
                S.add("tensor", lambda e, dr=dr, bank2=bank2, za=za: e.matmul(
                    bank2[:, 0:n], lhsT=wa2[0:16, dr, :], rhs=za[0:16, dr, 0:n], start=True,
                    stop=True), r=["wa2", (zak, dr)], w=[bk2])
                gg, ggk = f32p.get()
                S.add("scalar", lambda e, dr=dr, bank2=bank2, gg=gg: e.activation(
                    out=gg[:, 0:n], in_=bank2[:, 0:n], func=AF.Exp, scale=-1.0,
                    bias=negb[:, dr:dr + 1]), r=[bk2, "negb"], w=[ggk])
                S.add("scalar", lambda e, gg=gg: e.activation(out=gg[:, 0:n], in_=gg[:, 0:n],
                                                               func=AF.Ln, bias=1.0), r=[ggk], w=[ggk])
                prep(2, dr, gq, gqk, gk_, gkk, gg, ggk, -1.0 / 16.0)
            QT_R = [(qtk, dr, mt) for dr in range(2) for mt in range(3)]
            KT_R = [(ktk, dr, mt) for dr in range(2) for mt in range(3)]
            if gi + 1 < min(len(GROUPS), a_groups):
                S.capture("L")
                emit_LN(gi + 1)
            S.flush_interleaved(["A", "B", "L"], blk=6)
            dma("sync", qt_d[:, :, :, g0:g0 + n], qt[:, :, :, 0:n], QT_R, [("qt_d", gi)])

            if a_stage < 4:
                continue
            for j in range(ntl):
                tk0 = j * 128
                ktm, ktmk = ktm_rot.get()
                for dr in range(2):
                    for mt in range(3):
                        slot = dr * 3 + mt
                        S.add("tensor", lambda e, dr=dr, mt=mt, slot=slot: e.transpose(
                            out=ps5b[:, slot * 128:(slot + 1) * 128],
                            in_=kt[:, dr, mt, tk0:tk0 + 128], identity=ident_b),
                            r=[(ktk, dr, mt), "cbf0"], w=["ps5"])
                S.add("vector", lambda e, ktm=ktm: e.tensor_copy(
                    out=ktm.rearrange("p a b c -> p (a b c)"), in_=ps5b[:, 0:768]),
                    r=["ps5"], w=[ktmk])
                oin, oink = oin_rot.get()
                Ut, Utk = U_rot.get()
                pTH, pTHk = pTH_rot.get()
                pTG, pTGk = pTG_rot.get()
                GB = (6, 2, 3, 4)

                def hg_st(dr, h):
                    hh, pair = h % 2, h // 2
                    bi = 6 if hh == 0 else 3
                    col = (dr * 2 + pair) * 128
                    S.add("tensor", lambda e: e.matmul(
                        ps[bi][:, col:col + 128], lhsT=kt[64 * hh:64 * hh + 64, dr, pair, tk0:tk0 + 128],
                        rhs=qt[64 * hh:64 * hh + 64, dr, pair, tk0:tk0 + 128], start=True, stop=True,
                        tile_position=(64 * hh, 0)),
                        r=[(ktk, dr, pair), (qtk, dr, pair)], w=[f"ps{bi}"])

                def gla_st(dr, h):
                    bi = GB[h]
                    col = dr * 128
                    S.add("tensor", lambda e: e.matmul(
                        ps[bi][:, col:col + 128], lhsT=kt[32 * h:32 * h + 32, dr, 2, tk0:tk0 + 128],
                        rhs=qt[32 * h:32 * h + 32, dr, 2, tk0:tk0 + 128], start=True, stop=True,
                        tile_position=(32 * h, 0)),
                        r=[(ktk, dr, 2), (qtk, dr, 2)], w=[f"ps{bi}"])

                def hg_evac(hh):
                    bi = 6 if hh == 0 else 3
                    S.add("vector", lambda e: e.copy_predicated(
                        out=pTH[:, hh, :, :].rearrange("p a c -> p (a c)"),
                        mask=maskH_u.rearrange("p a c -> p (a c)"),
                        data=ps[bi]),
                        r=[f"ps{bi}", ("mask4", 0), ("mask4", 1), (pTHk, hh)], w=[(pTHk, hh)])

                def gla_evac(h):
                    bi = GB[h]
                    S.add("vector", lambda e: e.copy_predicated(
                        out=pTG[:, h, :, :].rearrange("p a c -> p (a c)"),
                        mask=maskG_u.rearrange("p a c -> p (a c)"),
                        data=ps[bi][:, 0:256]),
                        r=[f"ps{bi}", "cf32", (pTGk, h)], w=[(pTGk, h)])

                for dr in range(2):
                    for h in range(4):
                        hg_st(dr, h)
                for dr in range(2):
                    gla_st(dr, 1)
                    gla_st(dr, 3)
                hg_evac(0)
                hg_evac(1)
                for dr in range(2):
                    gla_st(dr, 0)
                    gla_st(dr, 2)
                gla_evac(1)
                gla_evac(3)
                gla_evac(0)
                gla_evac(2)
                first_q = [True, True]
                n_oi = [0]
                for mix in range(2):
                    for dr in range(2):
                        for h in range(4):
                            hh = h % 2
                            ptile = mix * 2 + h // 2
                            vcol = mix * 256 + h * 64
                            st = first_q[hh]
                            first_q[hh] = False
                            n_oi[0] += 1
                            if mix == 0:
                                rhs_ap, rk = pTH[:, hh, dr * 2 + h // 2, :], (pTHk, hh)
                            else:
                                rhs_ap, rk = pTG[:, h, dr, :], (pTGk, h)
                            S.add("tensor", lambda e: e.matmul(
                                ps[7][64 * hh:64 * hh + 64, ptile * 128:(ptile + 1) * 128],
                                lhsT=vtm[:, j, vcol:vcol + 64], rhs=rhs_ap, start=st,
                                stop=(n_oi[0] == 16), skip_group_check=True,
                                tile_position=(0, 64 * hh)),
                                r=[rk, (vtmk, j)], w=["ps7"])
                for cc_ in range(2):
                    ub = ps[cc_]
                    ubk = f"ps{cc_}"
                    for dr in range(2):
                        for mt in range(3):
                            col = (dr * 3 + mt) * 64
                            for hx in range(2 if mt < 2 else 4):
                                if mt < 2:
                                    p0, dk, vcol = 64 * hx, 64, (2 * mt + hx) * 64
                                else:
                                    p0, dk, vcol = 32 * hx, 32, 256 + hx * 64
                                S.add("tensor", lambda e: e.matmul(
                                    ub[p0:p0 + dk, col:col + 64],
                                    lhsT=ktm[cc_ * 64:(cc_ + 1) * 64, dr, mt, p0:p0 + dk],
                                    rhs=vtm[cc_ * 64:(cc_ + 1) * 64, j, vcol:vcol + 64],
                                    start=True, stop=True, tile_position=(cc_ * 64, p0)),
                                    r=[ktmk, (vtmk, j)], w=[ubk])
                    cidx = ch0 + 2 * j + cc_
                    for dr in range(2):
                        S.add("vector", lambda e: e.tensor_tensor(
                            out=Ut[:, dr, :, cc_ * 64:(cc_ + 1) * 64],
                            in0=ub[:, dr * 192:(dr + 1) * 192].rearrange("p (a v) -> p a v", a=3, v=64),
                            in1=E_sb[:, :, dr, cidx:cidx + 1].broadcast_to((128, 3, 64)),
                            op=ALU.mult),
                            r=[ubk] + [("E", mt, dr, gi) for mt in range(3)], w=[(Utk, dr, cc_)])
                S.add("scalar", lambda e, oin=oin: e.copy(
                    out=oin.rearrange("p a b -> p (a b)"), in_=ps[7]), r=["ps7"], w=[oink])
                dma("sync", oin_d[:, :, g0 + tk0:g0 + tk0 + 128], oin, [oink], [("oin_d", gi, j)])
                cidx = ch0 + 2 * j
                dma("sync", U_d[:, :, :, cidx:cidx + 2, :],
                    Ut.rearrange("p a b (c v) -> p a b c v", c=2, v=64),
                    [(Utk, a_, b_) for a_ in range(2) for b_ in range(2)], [("U_d", gi, j)])
        S.barrier()
        if stop_after == "A":
            break

        ar.reset()
        U_sb = ar.alloc([128, 2, 3, NCH, 64], F32)
        snap = ar.alloc([128, 2, 3, NCH, 64], BF16)
        Sst = ar.alloc([128, 2, 3, 2, 64], F32)
        for dr in range(2):
            for mt in range(3):
                dma("sync", U_sb[:, dr, mt, :, :], U_d[:, dr, mt, :, :],
                    [("U_d", gi, j) for gi, (g0, n) in enumerate(GROUPS) for j in range(n // 128)],
                    [("U_sb", dr, mt)])
        S.add("vector", lambda e: e.memset(Sst.rearrange("p a b c d -> p (a b c d)"), 0.0),
              w=[("Sst", dr, mt, pp) for dr in range(2) for mt in range(3) for pp in range(2)])
        order = [list(range(NCH)), [3, 2, 1, 0] + list(range(NCH - 1, 3, -1))]
        AER = [(nm, mt, dr, gi) for nm in ("A", "E", "R") for mt in range(3) for dr in range(2)
               for gi in range(len(GROUPS))]
        for step in range(NCH):
            pp = step % 2
            for dr in range(2):
                c = order[dr][step]
                for mt in range(3):
                    S.add("scalar", lambda e, dr=dr, mt=mt, c=c, pp=pp: e.activation(
                        out=snap[:, dr, mt, c, :], in_=Sst[:, dr, mt, pp, :], func=AF.Identity,
                        scale=R_sb[:, mt, dr, c:c + 1]),
                        r=[("Sst", dr, mt, pp)] + (AER if step == 0 else []),
                        w=[("snap", dr, mt, c)])
                    if step < NCH - 1:
                        S.add("vector", lambda e, dr=dr, mt=mt, c=c, pp=pp: e.scalar_tensor_tensor(
                            out=Sst[:, dr, mt, 1 - pp, :], in0=Sst[:, dr, mt, pp, :],
                            scalar=A_sb[:, mt, dr, c:c + 1], in1=U_sb[:, dr, mt, c, :],
                            op0=ALU.mult, op1=ALU.add),
                            r=[("Sst", dr, mt, pp), ("U_sb", dr, mt)] + (AER if step == 0 else []),
                            w=[("Sst", dr, mt, 1 - pp)])
        dma("sync", snap_d, snap, [("snap", dr, mt, c) for dr in range(2) for mt in range(3)
                                   for c in range(NCH)], ["snap_d"])
        S.barrier()
        if stop_after == "S":
            break

        ar.reset()
        KnT = ar.alloc([128, 4, T], BF16)
        KrT = ar.alloc([128, T], BF16)
        Vsb = ar.alloc([128, NT, 512], BF16)
        wout = ar.alloc([128, 8, 1024], BF16)
        for h in range(4):
            dma("sync", KnT[:, h, :], kn_d[:, h, :], [("kn_d", gi) for gi in range(9)],
                [("KnT", h)])
        dma("sync", KrT[0:64, :], kr_d, [("kr_d", gi) for gi in range(9)], ["KrT"])
        for q4 in range(2):
            dma("sync", Vsb[:, q4 * 17:(q4 + 1) * 17, :], vm_d[:, q4 * 17:(q4 + 1) * 17, :],
                [("vm_d", gi) for gi in range(9)], [("Vsb", q4)])
        woutv = wout_d[l].rearrange("(k p) n -> p k n", p=128)
        for k in range(8):
            dma("gpsimd", wout[:, k, :], woutv[:, k, :], [], [("wout", k)])
        WOUT_R = [("wout", k) for k in range(8)]
        KV_R = [("KnT", h) for h in range(4)] + ["KrT", ("Vsb", 0), ("Vsb", 1)]
        qn_rot = Rot(ar, "cqn", 1, [128, 4, 512], BF16)
        qr_rot = Rot(ar, "cqr", 1, [128, 4, 512], BF16)
        sgm_rot = Rot(ar, "csgm", 1, [128, 4, 512], BF16)
        sgr_rot = Rot(ar, "csgr", 1, [128, 4, 512], BF16)
        oin_rot = Rot(ar, "coin", 1, [128, 4, 512], F32)
        qt_rot = Rot(ar, "cqt", 1, [128, 2, 3, 512], BF16)
        sn_rot = Rot(ar, "csn", 1, [128, 2, 3, 8, 64], BF16)
        pT_rot = Rot(ar, "cpT", 4, [128, 512], BF16)
        yT_rot = Rot(ar, "yT", 2, [128, 8, 512], BF16)
        f32p = Rot(ar, "cf32p", 4, [128, 512], F32)
        acc_rot = Rot(ar, "cacc", 2, [128, 512], F32)
        xt_rot = Rot(ar, "cxt", 1, [128, 1024], F32)
        zt_rot = Rot(ar, "czt", 2, [128, 1024], F32)
        sbank_i = [0]
        obank_i = [0]

        def outproj_steps(gi, g0, n, yT, yTk):
            mc = 1 if gi == 0 else 0
            YT_R = [(yTk, k) for k in range(8)]
            steps = []
            for j in range(n // 128):
                def step(j=j):
                    tok0 = g0 + j * 128
                    xt, xk = xt_rot.get()
                    zt, zk = zt_rot.get()
                    dma("sync", xt, xsrc[tok0:tok0 + 128, :], [(xsrc_key, tok0 // 128)], [xk])
                    for hf in range(2):
                        ob = ps[3 if hf == 0 else 7]
                        obk = "ps3" if hf == 0 else "ps7"
                        for k in range(8):
                            S.add("tensor", lambda e, k=k: e.matmul(
                                ob, lhsT=yT[:, k, j * 128:(j + 1) * 128],
                                rhs=wout[:, k, hf * 512:(hf + 1) * 512], start=(k == 0),
                                stop=(k == 7)), r=YT_R + WOUT_R, w=[obk])
                        S.add("vector", lambda e: e.tensor_tensor(
                            out=zt[:, hf * 512:(hf + 1) * 512], in0=ob,
                            in1=gb[:, mc, hf * 512:(hf + 1) * 512], op=ALU.mult),
                            r=[obk, ("gb", mc, hf)], w=[(zk, hf)])
                        S.add("vector", lambda e: e.scalar_tensor_tensor(
                            out=zt[:, hf * 512:(hf + 1) * 512], in0=xt[:, hf * 512:(hf + 1) * 512],
                            scalar=ALPHA, in1=zt[:, hf * 512:(hf + 1) * 512], op0=ALU.mult,
                            op1=ALU.add), r=[(zk, hf), xk], w=[(zk, hf)])
                    S.add("vector", lambda e: e.bn_stats(out=st6[:, 0, :], in_=zt[:, 0:512]),
                          r=[(zk, 0)], w=["st6a"])
                    S.add("vector", lambda e: e.bn_stats(out=st6[:, 1, :], in_=zt[:, 512:1024]),
                          r=[(zk, 1)], w=["st6b"])
                    S.add("vector", lambda e: e.bn_aggr(out=mv, in_=st6.rearrange("p a b -> p (a b)")),
                          r=["st6a", "st6b"], w=["mv"])
                    S.add("scalar", lambda e: e.activation(out=rstd, in_=mv[:, 1:2], func=AF.Ln,
                                                           bias=EPS), r=["mv"], w=["rstd"])
                    S.add("scalar", lambda e: e.activation(out=rstd, in_=rstd, func=AF.Exp,
                                                           scale=-0.5), r=["rstd"], w=["rstd"])
                    S.add("vector", lambda e: e.tensor_scalar(
                        out=zt, in0=zt, scalar1=mv[:, 0:1], scalar2=rstd, op0=ALU.subtract,
                        op1=ALU.mult), r=[(zk, 0), (zk, 1), "mv", "rstd"], w=[(zk, 0), (zk, 1)])
                    S.add("gpsimd", lambda e: e.tensor_tensor(
                        out=zt, in0=zt, in1=lnbc[:, 0, :], op=ALU.mult),
                        r=[(zk, 0), (zk, 1), "lnbc0"], w=[(zk, 0), (zk, 1)])
                    S.add("gpsimd", lambda e: e.tensor_tensor(
                        out=zt, in0=zt, in1=lnbc[:, 1, :], op=ALU.add),
                        r=[(zk, 0), (zk, 1), "lnbc1"], w=[(zk, 0), (zk, 1)])
                    if last:
                        dma("sync", out_d[tok0 - LCTX:tok0 - LCTX + 128, :], zt, [(zk, 0), (zk, 1)],
                            [("out", tok0 // 128)])
                    else:
                        dma("sync", x1_d[tok0:tok0 + 128, :], zt, [(zk, 0), (zk, 1)],
                            [("x1", tok0 // 128)])
                steps.append(step)
            return steps

        pending_tail = []
        for gi, (g0, n) in enumerate(GROUPS):
            isctx = gi == 0
            if isctx and last:
                continue
            ntl = n // 128
            nch = n // 64
            ch0 = g0 // 64
            kt_lo, kt_hi = (0, 2) if isctx else (0, NT)
            qn, qnk = qn_rot.get()
            qr, qrk = qr_rot.get()
            sgm, sgmk = sgm_rot.get()
            sgr, sgrk = sgr_rot.get()
            oin, oink = oin_rot.get()
            qt, qtk = qt_rot.get()
            sn, snk = sn_rot.get()
            yT, yTk = yT_rot.get()
            dma("sync", qn[:, :, 0:n], qn_d[:, :, g0:g0 + n], [("qn_d", gi, h_) for h_ in range(4)], [qnk])
            dma("sync", qr[0:64, :, 0:n], qr_d[:, :, g0:g0 + n], [("qr_d", gi, h_) for h_ in range(4)], [qrk])
            dma("sync", qt[:, :, :, 0:n], qt_d[:, :, :, g0:g0 + n], [("qt_d", gi)], [qtk])
            for dr in range(2):
                dma("sync", sn[:, dr, :, 0:nch, :], snap_d[:, dr, :, ch0:ch0 + nch, :],
                    ["snap_d"], [(snk, dr)])
            dma("sync", oin[:, :, 0:n], oin_d[:, :, g0:g0 + n],
                [("oin_d", gi, j) for j in range(ntl)], [oink])
            dma("sync", sgr[:, :, 0:n], sgr_d[:, :, g0:g0 + n], [("sgr_d", gi, c_) for c_ in range(4)], [sgrk])
            dma("sync", sgm[:, :, 0:n], sgm_d[:, :, g0:g0 + n], [("sgm_d", gi, c_) for c_ in range(4)], [sgmk])

            deferred = []
            for ptile in range(4):
                def ro_mm(ptile=ptile):
                    rb = (3, 7)
                    firsts = [True, True]
                    cnt = [0, 0]
                    for cidx in range(nch):
                        for dr in range(2):
                            for hh in range(2):
                                if ptile < 2:
                                    mt, p0, dk = ptile, 64 * hh, 64
                                else:
                                    mt, p0, dk = 2, 32 * (2 * (ptile - 2) + hh), 32
                                st = firsts[hh]
                                firsts[hh] = False
                                cnt[hh] += 1
                                S.add("tensor", lambda e: e.matmul(
                                    ps[rb[hh]][64 * hh:64 * hh + 64, cidx * 64:(cidx + 1) * 64],
                                    lhsT=sn[p0:p0 + dk, dr, mt, cidx, :],
                                    rhs=qt[p0:p0 + dk, dr, mt, cidx * 64:(cidx + 1) * 64],
                                    start=st, stop=(cnt[hh] == nch * 2), skip_group_check=True,
                                    tile_position=(p0, 64 * hh)),
                                    r=[(snk, dr), qtk], w=[f"ps{rb[hh]}"])
                    o, ok_ = f32p.get()
                    for hh in range(2):
                        S.add("vector", lambda e: e.tensor_tensor(
                            out=o[64 * hh:64 * hh + 64, 0:n], in0=ps[rb[hh]][64 * hh:64 * hh + 64, 0:n],
                            in1=oin[64 * hh:64 * hh + 64, ptile, 0:n], op=ALU.add),
                            r=[f"ps{rb[hh]}", oink] + ([ok_] if hh == 1 else []), w=[ok_])
                    sq, sqk = f32p.get()
                    S.add("scalar", lambda e: e.activation(out=sq[:, 0:n], in_=o[:, 0:n],
                                                           func=AF.Square), r=[ok_], w=[sqk])

                    def ro_ss():
                        S.add("tensor", lambda e: e.matmul(
                            ps[3][:, 0:n], lhsT=blk_f, rhs=sq[:, 0:n], start=True, stop=True),
                            r=[sqk, "cf32"], w=["ps3"])
                        S.add("scalar", lambda e: e.activation(
                            out=sq[:, 0:n], in_=ps[3][:, 0:n], func=AF.Ln, scale=1.0 / 64.0, bias=EPS),
                            r=["ps3"], w=[sqk])
                        S.add("scalar", lambda e: e.activation(out=sq[:, 0:n], in_=sq[:, 0:n],
                                                               func=AF.Exp, scale=-0.5),
                              r=[sqk], w=[sqk])
                        gcol = V(18 + l) if ptile < 2 else V(20 + l)
                        S.add("vector", lambda e: e.scalar_tensor_tensor(
                            out=o[:, 0:n], in0=o[:, 0:n], scalar=gcol, in1=sq[:, 0:n], op0=ALU.mult,
                            op1=ALU.mult), r=[ok_, sqk, "vecs"], w=[ok_])
                        ychunk = ptile if ptile < 2 else 4 + ptile
                        S.add("gpsimd", lambda e: e.tensor_tensor(
                            out=yT[:, ychunk, 0:n], in0=o[:, 0:n], in1=sgr[:, ptile, 0:n],
                            op=ALU.mult), r=[ok_, sgrk], w=[(yTk, ychunk)])
                    return ro_ss
                deferred.append(ro_mm)

            LA = 2
            nk = kt_hi - kt_lo
            iters = [(h, ki, kt_lo + ki) for h in range(4) for ki in range(nk)]
            hbanks = {}
            for h in range(4):
                par = obank_i[0] % 2
                obank_i[0] += 1
                acc, acck = acc_rot.get()
                hbanks[h] = (ps[4 + par], f"ps{4 + par}", acc, acck)
            pend = {}

            def emit_scores(it):
                h, ki, ktile = it
                sb = ps[sbank_i[0] % 3]
                sbk = f"ps{sbank_i[0] % 3}"
                sbank_i[0] += 1
                S.add("tensor", lambda e: e.matmul(
                    sb[:, 0:n], lhsT=KnT[:, h, ktile * 128:(ktile + 1) * 128], rhs=qn[:, h, 0:n],
                    start=True, stop=False), r=[("KnT", h), qnk], w=[sbk])
                S.add("tensor", lambda e: e.matmul(
                    sb[:, 0:n], lhsT=KrT[0:64, ktile * 128:(ktile + 1) * 128],
                    rhs=qr[0:64, h, 0:n], start=False, stop=True), r=["KrT", qrk], w=[sbk])
                pT, pTk = pT_rot.get()
                S.add("scalar", lambda e: e.activation(
                    out=pT[:, 0:n], in_=sb[:, 0:n], func=AF.Exp), r=[sbk], w=[pTk])
                pend[it] = (pT, pTk)

            def emit_pv(it):
                h, ki, ktile = it
                pT, pTk = pend.pop(it)
                ob, obk, acc, acck = hbanks[h]
                S.add("tensor", lambda e: e.matmul(
                    ob[:, 0:n], lhsT=Vsb[:, ktile, h * 128:(h + 1) * 128], rhs=pT[:, 0:n],
                    start=(ki == 0), stop=(ki == nk - 1)),
                    r=[pTk, ("Vsb", ktile // 17)], w=[obk])
                S.add("tensor", lambda e: e.matmul(
                    ps[6][:, 0:n], lhsT=ones_b, rhs=pT[:, 0:n], start=(ki == 0),
                    stop=(ki == nk - 1)), r=[pTk, "cbf1"], w=["ps6"])
                if ki == nk - 1:
                    S.add("vector", lambda e: e.tensor_copy(out=acc[:, 0:n], in_=ps[6][:, 0:n]),
                          r=["ps6"], w=[acck])
                    rd, rdk = f32p.get()
                    S.add("vector", lambda e: e.reciprocal(out=rd[:, 0:n], in_=acc[:, 0:n]),
                          r=[acck], w=[rdk])
                    S.add("vector", lambda e: e.tensor_tensor(
                        out=rd[:, 0:n], in0=ob[:, 0:n], in1=rd[:, 0:n], op=ALU.mult),
                        r=[obk, rdk], w=[rdk])
                    S.add("gpsimd", lambda e: e.tensor_tensor(
                        out=yT[:, 2 + h, 0:n], in0=rd[:, 0:n], in1=sgm[:, h, 0:n], op=ALU.mult),
                        r=[rdk, sgmk], w=[(yTk, 2 + h)])

            extra = {}
            nit = len(iters)
            ro_pos = [0, 3, 6, 9]
            ss_list = []
            if nit < 40:
                for ro in deferred:
                    ro()()
            else:
                for k_, ro in enumerate(deferred):
                    extra.setdefault(ro_pos[k_], []).append(("ro", ro))
            tail_steps = pending_tail
            pending_tail = []
            if tail_steps:
                gap = max(1, (nit - 30) // len(tail_steps))
                for k_, stp in enumerate(tail_steps):
                    extra.setdefault(min(24 + k_ * gap, nit - 1), []).append(("tail", stp))
            for i in range(nit + LA):
                if i < nit:
                    emit_scores(iters[i])
                if i >= LA:
                    emit_pv(iters[i - LA])
                for kind, fn_ in list(extra.get(i, [])):
                    if kind == "ro":
                        extra.setdefault(i + 5, []).append(("ss", fn_()))
                    else:
                        fn_()
            for ss in ss_list:
                ss()
            pending_tail = outproj_steps(gi, g0, n, yT, yTk)
        for stp in pending_tail:
            stp()
        S.barrier()

    S.finalize(nc, stack)
    blk = stack.enter_context(nc.Block())
    S.emit(blk)
    stack.close()
    return nc


def _perm_w_in():
    o = dict(hq=0, hff=256, hfb=512, hi=768, hgate=1024, mcq=1280, mckv=1536, mkr=1664,
             mgate=1728, gq=2240, gk=2368, gv=2496, gaf=2752, gab=2768, ggate=2784)
    r = np.arange
    p = np.arange(64)
    a, hf, f = p // 32, (p // 16) % 2, p % 16
    partner = a * 32 + (1 - hf) * 16 + f
    idx = np.concatenate([
        r(o["hq"], o["hq"] + 256), r(o["hff"], o["hff"] + 256), r(o["hfb"], o["hfb"] + 256),
        r(o["hgate"], o["hgate"] + 256), r(o["mcq"], o["mcq"] + 256), r(o["mckv"], o["mckv"] + 128),
        r(o["mkr"], o["mkr"] + 64), o["mkr"] + partner, r(o["mgate"], o["mgate"] + 512),
        r(o["gq"], o["gq"] + 128), r(o["gk"], o["gk"] + 128), r(o["ggate"], o["ggate"] + 256),
        r(o["gaf"], o["gaf"] + 16), r(o["gab"], o["gab"] + 16), r(o["hi"], o["hi"] + 256),
        r(o["gv"], o["gv"] + 256)])
    assert idx.size == WCOLS
    return idx, partner


def _consts():
    pos = np.arange(NLAT)
    pos_r = (pos // 64).astype(np.float32)
    pos_c = (pos % 64).astype(np.float32)
    inv = (1.0 / (np.float32(10000.0) ** (np.arange(16, dtype=np.float32) / np.float32(16)))
           ).astype(np.float32)
    p = np.arange(64)
    a, hf, f = p // 32, (p // 16) % 2, p % 16
    posa = np.where(a[:, None] == 0, pos_r[None, :], pos_c[None, :]).astype(np.float32)
    ang = (posa * inv[f][:, None]).astype(np.float32)
    ropeC = np.cos(ang).astype(np.float32)
    sgn = np.where(hf == 0, -1.0, 1.0).astype(np.float32)
    ropeS = (np.sin(ang).astype(np.float32) * sgn[:, None]).astype(np.float32)
    cf = np.zeros((128, 6, 128), np.float32)
    i = np.arange(128)
    cf[:, 0, :] = np.eye(128, dtype=np.float32)
    cf[:, 1, :] = 1.0
    cf[:, 2, :] = (i[:, None] // 64 == i[None, :] // 64)
    same = (i[:, None] // 64 == i[None, :] // 64)
    cf[:, 3, :] = same & (i[:, None] <= i[None, :])
    cf[:, 4, :] = same & (i[:, None] >= i[None, :])
    resetm = np.ones((128, 512), np.float32)
    resetm[:, ::64] = 0.0
    return ropeC, ropeS, cf, resetm


_NC_CACHE = {}


def _prep_shared(inp):
    idx, partner = _perm_w_in()
    w_in_p = np.ascontiguousarray(inp["w_in"][:, :, idx])
    wuq = inp["mla_w_uq"].reshape(2, 256, 4, 192)
    rope = wuq[..., 128:]
    w_uq_p = np.concatenate(
        [wuq[..., :128].reshape(2, 256, 512),
         np.concatenate([rope, rope[..., partner]], axis=-1).reshape(2, 256, 512)], axis=-1)
    wukv = inp["mla_w_ukv"].reshape(2, 128, 4, 256)
    w_ukv_p = np.concatenate([wukv[..., :128].reshape(2, 128, 512),
                              wukv[..., 128:].reshape(2, 128, 512)], axis=-1)
    vecs = np.zeros((128, NVEC), np.float32)
    p = np.arange(128)
    lbl = inp["hg_lb_logits"]
    for l in range(2):
        for d in range(2):
            for c in range(2):
                vecs[:, l * 4 + d * 2 + c] = lbl[l, d, c * 128 + p]
        for c in range(2):
            vecs[:, 8 + l * 2 + c] = inp["mla_q_norm_g"][l, c * 128 + p]
        vecs[:, 12 + l] = inp["mla_kv_norm_g"][l, p]
        for d in range(2):
            vecs[:, 14 + l * 2 + d] = inp["gla_b_a"][l, d, p]
        vecs[:, 18 + l] = inp["hg_norm_g"][l, p % 64]
        vecs[:, 20 + l] = inp["gla_norm_g"][l, p % 64]
        for ci in range(24):
            vecs[:, 22 + 24 * l + ci] = inp["b_mod"][l, ci * 128 + p]
    ropeC, ropeS, cf, resetm = _consts()
    return dict(w_mod=np.ascontiguousarray(inp["w_mod"]), b_mod=np.ascontiguousarray(inp["b_mod"]),
                w_in=w_in_p, w_out=np.ascontiguousarray(inp["w_out"]),
                w_uq=np.ascontiguousarray(w_uq_p), w_ukv=np.ascontiguousarray(w_ukv_p),
                w_a2=np.ascontiguousarray(inp["gla_w_a2"]), ln_g=np.ascontiguousarray(inp["ln_g"]),
                ln_b=np.ascontiguousarray(inp["ln_b"]), vecs=vecs, ropeC=ropeC, ropeS=ropeS,
                cf32=cf, resetm=resetm)


def _per_core(inp, b):
    xin = np.ascontiguousarray(np.concatenate([inp["ctx"][b], inp["x"][b]], axis=0))
    cc = np.stack([inp["c"][b].reshape(8, 128).T, inp["c_ctx"].reshape(8, 128).T], axis=-1)
    return dict(xin=xin, cc=np.ascontiguousarray(cc.astype(np.float32)))


def kernel(**inputs):
    inp = {k: np.asarray(v, dtype=np.float32) for k, v in inputs.items()}
    if "nc" not in _NC_CACHE:
        _NC_CACHE["nc"] = build_program()
    nc = _NC_CACHE["nc"]
    shared = _prep_shared(inp)
    in_maps = []
    for b in range(8):
        m = dict(shared)
        m.update(_per_core(inp, b))
        in_maps.append(m)
    res = run_bass_kernel_spmd(nc, in_maps, core_ids=list(range(8)))
    out = np.stack([np.asarray(r["out"], dtype=np.float32) for r in res.results], axis=0)
    return out
```

```python
import numpy as np
import ml_dtypes
import concourse.bass as bass
import concourse.mybir as mybir
from concourse.bass_utils import run_bass_kernel_spmd

F32 = mybir.dt.float32
BF16 = mybir.dt.bfloat16
AF = mybir.ActivationFunctionType
ALU = mybir.AluOpType

T = 4352
NT = 34
NCH = 68
D = 1024
LCTX = 256
NLAT = 4096
GROUPS = [(0, 256)] + [(256 + 512 * i, 512) for i in range(8)]
WCOLS = 3104
(C_ZQ, C_ZFF, C_ZFB, C_HG, C_ZCQ, C_ZCKV, C_KR, C_KRS, C_MG, C_GQ, C_GK, C_GG, C_GAF, C_GAB,
 C_HI, C_GV) = (0, 256, 512, 768, 1024, 1280, 1408, 1472, 1536, 2048, 2176, 2304, 2560, 2576,
                2592, 2848)
EPS = 1e-6
ALPHA = 4.0 ** 0.25
MLA_SCALE = 192.0 ** -0.5
NVEC = 22 + 48
SAME_ENGINE_SYNC = True
SEM_CAP = 30000
DMA_POOL = 20


class _Op:
    __slots__ = ("eng", "fn", "deps", "sig", "sem", "val", "waits", "dma", "idx")


class _Rec:
    def __getattr__(self, name):
        def f(*a, **k):
            self.__dict__["call"] = (name, a, k)
        return f


class Sched:
    ENG = ("sync", "scalar", "vector", "gpsimd", "tensor")

    def __init__(self):
        self.ops = []
        self.lastw = {}
        self.readers = {}
        self.last_eng = {}
        self.dmas_since_bar = []

    ALIAS = {}

    def _expand(self, keys):
        out = []
        for k in keys:
            out.extend(self.ALIAS.get(k, (k,)))
        return out

    cur = None
    bufs = None

    def capture(self, name):
        if self.bufs is None:
            self.bufs = {}
        self.cur = name
        if name is not None:
            self.bufs.setdefault(name, [])

    def flush_interleaved(self, names, blk=3):
        self.cur = None
        L = [self.bufs.pop(nm, []) for nm in names]
        L = [x for x in L if x]
        idx = [0] * len(L)
        while any(idx[i] < len(L[i]) for i in range(len(L))):
            cand = [i for i in range(len(L)) if idx[i] < len(L[i])]
            i = min(cand, key=lambda i_: idx[i_] / len(L[i_]))
            for _ in range(blk):
                if idx[i] < len(L[i]):
                    o = L[i][idx[i]]
                    self.add(o[0], o[1], r=o[2], w=o[3], dma=o[4], _raw=True)
                    idx[i] += 1

    def add(self, eng, fn, r=(), w=(), dma=False, _raw=False):
        if not _raw:
            r = self._expand(r)
            w = self._expand(w)
            if fn is not None:
                rec = _Rec()
                fn(rec)
                fn = rec.__dict__["call"]
            if self.cur is not None:
                self.bufs[self.cur].append((eng, fn, r, w, dma))
                return None
        op = _Op()
        op.eng, op.fn, op.dma = eng, fn, dma
        op.sig, op.sem, op.val, op.waits = False, None, 0, []
        op.idx = len(self.ops)
        deps = set()
        for k in r:
            x = self.lastw.get(k)
            if x is not None:
                deps.add(x)
        for k in w:
            x = self.lastw.get(k)
            if x is not None:
                deps.add(x)
            for x in self.readers.get(k, ()):
                deps.add(x)
        op.deps = deps
        for k in r:
            self.readers.setdefault(k, []).append(op)
        for k in w:
            self.lastw[k] = op
            self.readers[k] = []
        self.ops.append(op)
        self.last_eng[eng] = op
        if dma:
            self.dmas_since_bar.append(op)
        return op

    def barrier(self):
        lasts = [o for o in self.last_eng.values()]
        dm = list(self.dmas_since_bar)
        self.dmas_since_bar = []
        for e in self.ENG:
            op = self.add(e, None)
            op.deps = set(lasts) | set(dm)

    def finalize(self, nc, stack):
        ops = self.ops
        pools = {}
        dcount = {}
        last_on_sem = {}
        for op in ops:
            if op.dma:
                n = dcount.get(op.eng, 0)
                dcount[op.eng] = n + 1
                if op.eng not in pools:
                    pools[op.eng] = [stack.enter_context(nc.semaphore(f"d_{op.eng}_{i}"))
                                     for i in range(DMA_POOL)]
                j = n % DMA_POOL
                op.sem = pools[op.eng][j]
                op.val = 16 * (n // DMA_POOL + 1)
                prev = last_on_sem.get((op.eng, j))
                if prev is not None:
                    op.deps.add(prev)
                last_on_sem[(op.eng, j)] = op
                op.sig = True
        for op in ops:
            for d in op.deps:
                if not d.dma:
                    d.sig = True
        cnt = {}
        esems = {}
        for op in ops:
            if op.dma or not op.sig or op.fn is None:
                continue
            c = cnt.get(op.eng, 0)
            cnt[op.eng] = c + 1
            si = c // SEM_CAP
            lst = esems.setdefault(op.eng, [])
            if si >= len(lst):
                lst.append(stack.enter_context(nc.semaphore(f"e_{op.eng}_{si}")))
            op.sem = lst[si]
            op.val = c % SEM_CAP + 1
        waited = {e: {} for e in self.ENG}
        for op in ops:
            wl = waited[op.eng]
            for d in sorted(op.deps, key=lambda o: o.idx):
                if d.fn is None:
                    continue
                if d.eng == op.eng and not d.dma:
                    if op.eng == "tensor":
                        continue
                    if not SAME_ENGINE_SYNC and not op.dma:
                        continue
                key = id(d.sem)
                if wl.get(key, 0) >= d.val:
                    continue
                wl[key] = d.val
                op.waits.append((d.sem, d.val))
        self.per_eng = {e: [o for o in ops if o.eng == e] for e in self.ENG}

    def emit(self, block):
        def mk(name):
            lst = self.per_eng[name]

            def body(e):
                for op in lst:
                    for s, v in op.waits:
                        e.wait_ge(s, v)
                    if op.fn is not None:
                        name_, a_, k_ = op.fn
                        ins = getattr(e, name_)(*a_, **k_)
                        if op.dma:
                            ins.then_inc(op.sem, 16)
                        elif op.sig:
                            ins.then_inc(op.sem, 1)
            return body
        block.sync(mk("sync"))
        block.scalar(mk("scalar"))
        block.vector(mk("vector"))
        block.gpsimd(mk("gpsimd"))
        block.tensor(mk("tensor"))


class Arena:
    def __init__(self, ap_bf16, nbytes):
        self.ap = ap_bf16
        self.nbytes = nbytes
        self.off = 0
        self.peak = 0

    def reset(self):
        self.off = 0

    def alloc(self, shape, dtype, parts=128):
        es = 4 if dtype == F32 else 2
        nel = int(np.prod(shape[1:]))
        nb = (nel * es + 31) // 32 * 32
        assert self.off + nb <= self.nbytes, f"arena overflow {self.off}+{nb}>{self.nbytes}"
        a = self.ap[0:shape[0], self.off // 2:(self.off + nel * es) // 2]
        self.off += nb
        self.peak = max(self.peak, self.off)
        if dtype == F32:
            a = a.bitcast(F32)
        if len(shape) == 3:
            a = a.rearrange("p (a b) -> p a b", a=shape[1], b=shape[2])
        elif len(shape) == 4:
            a = a.rearrange("p (a b c) -> p a b c", a=shape[1], b=shape[2], c=shape[3])
        elif len(shape) == 5:
            a = a.rearrange("p (a b c d) -> p a b c d", a=shape[1], b=shape[2], c=shape[3],
                            d=shape[4])
        return a


class Rot:
    def __init__(self, arena, name, n, shape, dtype):
        self.bufs = [arena.alloc(shape, dtype) for _ in range(n)]
        self.name = name
        self.i = 0

    def get(self):
        j = self.i % len(self.bufs)
        self.i += 1
        return self.bufs[j], (self.name, j)


def build_program(debug=False, nlayers=2, stop_after=None, a_groups=9, a_stage=9):
    nc = bass.Bass("TRN2", target_bir_lowering=False)
    Sched.ALIAS = {}
    for b_ in (2, 3, 4, 6):
        Sched.ALIAS[f"ps{b_}"] = [f"ps{b_}"] + [(f"ps{b_}", s_) for s_ in range(4)]
    for b_ in (0, 1):
        Sched.ALIAS[f"ps{b_}"] = [f"ps{b_}"] + [(f"ps{b_}", s_) for s_ in range(2)]
    from contextlib import ExitStack
    stack = ExitStack()
    S = Sched()

    def din(name, shape, dt=F32):
        return nc.dram_tensor(name, list(shape), dt, kind="ExternalInput").ap()

    def dscr(name, shape, dt):
        return nc.dram_tensor(name, list(shape), dt,
                              kind="ExternalOutput" if debug else "Internal").ap()

    xin = din("xin", [T, D])
    cc_d = din("cc", [128, 8, 2])
    wmod_d = din("w_mod", [2, D, 3 * D])
    bmod_d = din("b_mod", [2, 3 * D])
    win_d = din("w_in", [2, D, WCOLS])
    wout_d = din("w_out", [2, D, D])
    wuq_d = din("w_uq", [2, 256, 1024])
    wukv_d = din("w_ukv", [2, 128, 1024])
    wa2_d = din("w_a2", [2, 2, 16, 128])
    lng_d = din("ln_g", [2, D])
    lnb_d = din("ln_b", [2, D])
    vecs_d = din("vecs", [128, NVEC])
    ropeC_d = din("ropeC", [64, NLAT])
    ropeS_d = din("ropeS", [64, NLAT])
    cf32_d = din("cf32", [128, 6, 128])
    reset_d = din("resetm", [128, 512])
    out_d = nc.dram_tensor("out", [NLAT, D], F32, kind="ExternalOutput").ap()

    x1_d = dscr("x1", [T, D], F32)
    qn_d = dscr("qn_s", [128, 4, T], BF16)
    qr_d = dscr("qr_s", [64, 4, T], BF16)
    kn_d = dscr("kn_s", [128, 4, T], BF16)
    kr_d = dscr("kr_s", [64, T], BF16)
    vm_d = dscr("vm_s", [128, NT, 512], BF16)
    sgm_d = dscr("sgm_s", [128, 4, T], BF16)
    sgr_d = dscr("sgr_s", [128, 4, T], BF16)
    oin_d = dscr("oin_s", [128, 4, T], F32)
    qt_d = dscr("qt_s", [128, 2, 3, T], BF16)
    U_d = dscr("U_s", [128, 2, 3, NCH, 64], F32)
    snap_d = dscr("snap_s", [128, 2, 3, NCH, 64], BF16)

    ARENA_BYTES = 190 * 1024
    arena_t = stack.enter_context(nc.sbuf_tensor("arena", [128, ARENA_BYTES // 2], BF16))
    PB = 12 * 1024
    pers_t = stack.enter_context(nc.sbuf_tensor("pers", [128, PB // 2], BF16))
    pers = Arena(pers_t, PB)
    ar = Arena(arena_t, ARENA_BYTES)
    psb = [stack.enter_context(nc.psum_tensor(f"ps{i}", [128, 512], F32)) for i in range(8)]
    ps = [p[:, :] for p in psb]

    cf32 = pers.alloc([128, 6, 128], F32)
    ident_f, ones_f, blk_f = cf32[:, 0, :], cf32[:, 1, :], cf32[:, 2, :]
    maskf = cf32[:, 3:5, :]
    cbf = pers.alloc([128, 2, 128], BF16)
    ident_b, ones_b = cbf[:, 0, :], cbf[:, 1, :]
    resetm = pers.alloc([128, 512], F32)
    vecs = pers.alloc([128, NVEC], F32)
    cc = pers.alloc([128, 8, 2], F32)
    sc = pers.alloc([128, 8, 2], F32)
    modT = pers.alloc([128, 16, 2], F32)
    lbv = pers.alloc([128, 4, 4], F32)
    A_sb = pers.alloc([128, 3, 2, NCH], F32)
    E_sb = pers.alloc([128, 3, 2, NCH], F32)
    R_sb = pers.alloc([128, 3, 2, NCH], F32)
    st6 = pers.alloc([128, 2, 6], F32)
    mv = pers.alloc([128, 2], F32)
    rstd = pers.alloc([128, 1], F32)
    tmpc = pers.alloc([128, 16], F32)
    negb = pers.alloc([128, 2], F32)

    def V(col):
        return vecs[:, col:col + 1]

    def dma(eng, out, in_, r, w):
        S.add(eng, lambda e: e.dma_start(out=out, in_=in_), r=r, w=w, dma=True)

    dma("sync", cf32, cf32_d, [], ["cf32"])
    dma("sync", resetm, reset_d, [], ["resetm"])
    dma("sync", vecs, vecs_d, [], ["vecs"])
    dma("sync", cc, cc_d, [], ["cc"])
    S.add("vector", lambda e: e.tensor_copy(out=cbf[:, 0, :], in_=cf32[:, 0, :]), r=["cf32"],
          w=["cbf0"])
    S.add("vector", lambda e: e.tensor_copy(out=cbf[:, 1, :], in_=cf32[:, 1, :]), r=["cf32"],
          w=["cbf1"])
    S.add("scalar", lambda e: e.activation(out=sc, in_=cc, func=AF.Silu), r=["cc"], w=["sc"])
    CONST_R = ["cf32", "cbf0", "cbf1", "resetm", "vecs"]

    for l in range(nlayers):
        xsrc = xin if l == 0 else x1_d
        xsrc_key = "xin" if l == 0 else "x1"
        last = (l == nlayers - 1)
        ar.reset()
        wm_rot = Rot(ar, "wm", 2, [128, 8, 512], F32)
        screp = ar.alloc([128, 2, 8, 128], F32)
        gate_ps_key = "ps1"
        gate_bc = pers_gate = None
        GATE_OFF = ARENA_BYTES - 16 * 1024
        gb = arena_t[:, GATE_OFF // 2:(GATE_OFF + 8192) // 2].bitcast(F32).rearrange(
            "p (a b) -> p a b", a=2, b=1024)
        lnbc = arena_t[:, (GATE_OFF + 8192) // 2:(GATE_OFF + 16384) // 2].bitcast(F32).rearrange(
            "p (a b) -> p a b", a=2, b=1024)
        ar.nbytes = GATE_OFF
        bmg = ar.alloc([128, 1024], F32)
        for j in range(2):
            S.add("vector", lambda e, j=j: e.tensor_copy(
                out=screp[:, j, :, :], in_=sc[:, :, j:j + 1].broadcast_to((128, 8, 128))),
                r=["sc"], w=[("screp", j)])
        dma("sync", bmg, bmod_d[l:l + 1, 2048:3072].partition_broadcast(128)
            .rearrange("p a b -> p (a b)"), [], ["bmg"])
        dma("sync", lnbc[:, 0, :], lng_d[l:l + 1, :].partition_broadcast(128)
            .rearrange("p a b -> p (a b)"), [], ["lnbc0"])
        dma("sync", lnbc[:, 1, :], lnb_d[l:l + 1, :].partition_broadcast(128)
            .rearrange("p a b -> p (a b)"), [], ["lnbc1"])
        wmv = wmod_d[l].rearrange("(k p) n -> p k n", p=128)
        for blk in range(6):
            wm, wk = wm_rot.get()
            dma("sync", wm, wmv[:, :, blk * 512:(blk + 1) * 512], [], [wk])
            if blk < 4:
                for j in range(4):
                    ci = blk * 4 + j
                    for k in range(8):
                        S.add("tensor", lambda e, k=k, j=j, ci=ci, wm=wm: e.matmul(
                            ps[0][:, ci * 2:ci * 2 + 2], lhsT=wm[:, k, j * 128:(j + 1) * 128],
                            rhs=sc[:, k, 0:2], start=(k == 0), stop=(k == 7)),
                            r=[wk, "sc"], w=["ps0"])
            else:
                for j in range(2):
                    for k in range(8):
                        S.add("tensor", lambda e, k=k, j=j, wm=wm: e.matmul(
                            ps[1 + j], lhsT=screp[:, j, k, :], rhs=wm[:, k, :],
                            start=(k == 0), stop=(k == 7)),
                            r=[wk, ("screp", j)], w=[f"ps{1 + j}"])
                    hb = blk - 4
                    S.add("vector", lambda e, j=j, hb=hb: e.tensor_tensor(
                        out=gb[:, j, hb * 512:(hb + 1) * 512], in0=ps[1 + j],
                        in1=bmg[:, hb * 512:(hb + 1) * 512], op=ALU.add),
                        r=[f"ps{1 + j}", "bmg"], w=[("gb", j, hb)])
        bm = vecs[:, 22 + 24 * l:22 + 24 * l + 16]
        S.add("vector", lambda e, bm=bm: e.tensor_tensor(
            out=modT, in0=ps[0][:, 0:32].rearrange("p (a b) -> p a b", a=16, b=2),
            in1=bm.unsqueeze(2).broadcast_to((128, 16, 2)), op=ALU.add),
            r=["ps0", "vecs"], w=["modT"])
        S.add("vector", lambda e: e.tensor_scalar(out=modT[:, 8:16, :], in0=modT[:, 8:16, :],
                                                   scalar1=1.0, scalar2=None, op0=ALU.add),
              r=["modT"], w=["modT"])
        if l == 0:
            S.add("vector", lambda e: e.memset(lbv[:, :, 0:1], 0.0), w=["lbv0"])
        else:
            S.add("vector", lambda e: e.tensor_tensor(
                out=tmpc[:, 0:4], in0=vecs[:, 4:8], in1=vecs[:, 0:4], op=ALU.subtract),
                r=["vecs"], w=["tmpc"])
            S.add("scalar", lambda e: e.activation(out=lbv[:, :, 0], in_=tmpc[:, 0:4],
                                                   func=AF.Sigmoid), r=["tmpc"], w=["lbv0"])
        S.add("vector", lambda e: e.tensor_scalar(out=lbv[:, :, 1:2], in0=lbv[:, :, 0:1],
                                                   scalar1=-1.0, scalar2=1.0, op0=ALU.mult,
                                                   op1=ALU.add), r=["lbv0"], w=["lbv1"])
        S.add("vector", lambda e: e.tensor_scalar(out=lbv[:, :, 2:3], in0=lbv[:, :, 1:2],
                                                   scalar1=-1.0, scalar2=None, op0=ALU.mult),
              r=["lbv1"], w=["lbv2"])
        S.add("vector", lambda e: e.reciprocal(out=tmpc[:, 4:8], in_=lbv[:, :, 1]),
              r=["lbv1", "tmpc"], w=["tmpc2"])
        S.add("vector", lambda e: e.tensor_scalar(out=tmpc[:, 8:12], in0=lbv[:, :, 0],
                                                   scalar1=-1.0, scalar2=1e-6, op0=ALU.mult,
                                                   op1=ALU.add), r=["lbv0", "tmpc2"], w=["tmpc3"])
        S.add("vector", lambda e: e.tensor_tensor(out=lbv[:, :, 3], in0=tmpc[:, 8:12],
                                                  in1=tmpc[:, 4:8], op=ALU.mult),
              r=["tmpc3", "tmpc2"], w=["lbv3"])
        LBV_R = ["lbv0", "lbv1", "lbv2", "lbv3"]
        S.add("vector", lambda e: e.tensor_scalar(out=negb, in0=vecs[:, 14 + 2 * l:16 + 2 * l],
                                                   scalar1=-1.0, scalar2=None, op0=ALU.mult),
              r=["vecs"], w=["negb"])
        S.barrier()
        if debug and l == 0:
            dbg_mod = nc.dram_tensor("dbg_mod", [128, 32], F32, kind="ExternalOutput").ap()
            dbg_gb = nc.dram_tensor("dbg_gb", [128, 2, 1024], F32, kind="ExternalOutput").ap()
            dbg_lbv = nc.dram_tensor("dbg_lbv", [128, 16], F32, kind="ExternalOutput").ap()
            dma("sync", dbg_mod, modT.rearrange("p a b -> p (a b)"), ["modT"], ["dbg_mod"])
            dma("sync", dbg_gb, gb, [("gb", j, hb) for j in range(2) for hb in range(2)], ["dbg_gb"])
            dma("sync", dbg_lbv, lbv.rearrange("p a b -> p (a b)"), LBV_R, ["dbg_lbv"])
            S.barrier()
        if stop_after == "mod":
            break

        ar.reset()
        win = ar.alloc([128, 8, WCOLS], BF16)
        wuq = ar.alloc([128, 2, 1024], BF16)
        wukv = ar.alloc([128, 1024], BF16)
        wa2 = ar.alloc([16, 2, 128], BF16)
        winv = win_d[l].rearrange("(k p) n -> p k n", p=128)
        for k in range(8):
            for hf in range(2):
                dma("gpsimd", win[:, k, hf * 1552:(hf + 1) * 1552],
                    winv[:, k, hf * 1552:(hf + 1) * 1552], [], [("win", k, hf)])
        WIN_R = [("win", k, hf) for k in range(8) for hf in range(2)]
        dma("gpsimd", wuq, wuq_d[l].rearrange("(c p) n -> p c n", p=128), [], ["wuq"])
        dma("gpsimd", wukv, wukv_d[l], [], ["wukv"])
        dma("gpsimd", wa2, wa2_d[l].rearrange("d r n -> r d n"), [], ["wa2"])

        xt_rot = Rot(ar, "xt", 2, [128, 1024], F32)
        hTs = [ar.alloc([128, 8, 512], BF16), ar.alloc([128, 8, 512], BF16)]
        f32p = Rot(ar, "f32p", 8, [128, 512], F32)
        ded = {nm: (ar.alloc([128, 512], F32), "ded_" + nm) for nm in ("hq0", "hq1", "gq", "gk", "rC", "rS")}
        sg_rot = Rot(ar, "sgst", 3, [128, 512], BF16)
        f32m = Rot(ar, "f32m", 6, [128, 512], F32)
        qn_rot = Rot(ar, "qn", 2, [128, 512], BF16)
        qr_rot = Rot(ar, "qr", 2, [128, 512], BF16)
        kn_rot = Rot(ar, "kn", 1, [128, 4, 512], BF16)
        krt_rot = Rot(ar, "krt", 1, [128, 512], BF16)
        vm_rot = Rot(ar, "vm", 1, [128, 4, 512], BF16)
        vtm_rot = Rot(ar, "vtm", 1, [128, 4, 512], BF16)
        cqn_rot = Rot(ar, "cqn", 1, [128, 3, 512], BF16)
        za_rot = Rot(ar, "za", 1, [128, 2, 512], BF16)
        qt_rot = Rot(ar, "qt", 1, [128, 2, 3, 512], BF16)
        kt_rot = Rot(ar, "kt", 1, [128, 2, 3, 512], BF16)
        ktm_rot = Rot(ar, "ktm", 2, [128, 2, 3, 128], BF16)
        pTH_rot = Rot(ar, "pTH", 2, [128, 2, 4, 128], BF16)
        pTG_rot = Rot(ar, "pTG", 2, [128, 4, 2, 128], BF16)
        for rot_ in (pTH_rot, pTG_rot):
            for bi, b_ in enumerate(rot_.bufs):
                S.add("gpsimd", lambda e, b_=b_: e.memset(b_.rearrange("p a b c -> p (a b c)"), 0.0),
                      w=[((rot_.name, bi), x_) for x_ in range(4)])
        maskH = ar.alloc([128, 4, 128], F32)
        for dr_ in range(2):
            S.add("vector", lambda e, dr_=dr_: e.tensor_copy(
                out=maskH[:, 2 * dr_:2 * dr_ + 2, :],
                in_=maskf[:, dr_, :].unsqueeze(1).broadcast_to((128, 2, 128))),
                r=["cf32"], w=[("mask4", dr_)])
        maskH_u = maskH.bitcast(mybir.dt.uint32)
        maskG_u = maskf.bitcast(mybir.dt.uint32)
        oin_rot = Rot(ar, "oin", 1, [128, 4, 128], F32)
        U_rot = Rot(ar, "Ut", 1, [128, 2, 3, 128], F32)
        ps5b = ps[5].bitcast(BF16)
        fm_i = [0]

        def fm_bank():
            if S.cur == "A":
                return ps[4], "ps4"
            j = 2 + fm_i[0] % 2
            fm_i[0] += 1
            return ps[j], f"ps{j}"

        _slot_keys = {f"ps{b_}": [(f"ps{b_}", s_) for s_ in range(4)] for b_ in (2, 3, 4, 6)}

        for gi, (g0, n) in enumerate(GROUPS):
            if gi >= a_groups:
                continue
            ntl = n // 128
            nch = n // 64
            ch0 = g0 // 64
            mc = 1 if gi == 0 else 0
            isctx = gi == 0
            if a_stage < 1:
                continue
            hb = gi % 2
            hT = hTs[hb]

            def emit_LN(gi_):
                g0_, n_ = GROUPS[gi_]
                mc_ = 1 if gi_ == 0 else 0
                hb_ = gi_ % 2
                hT_ = hTs[hb_]
                for j in range(n_ // 128):
                    tok0 = g0_ + j * 128
                    xt, xk = xt_rot.get()
                    dma("sync", xt, xsrc[tok0:tok0 + 128, :], [(xsrc_key, tok0 // 128)], [xk])
                    S.add("vector", lambda e: e.bn_stats(out=st6[:, 0, :], in_=xt[:, 0:512]),
                          r=[xk], w=["st6a"])
                    S.add("vector", lambda e: e.bn_stats(out=st6[:, 1, :], in_=xt[:, 512:1024]),
                          r=[xk], w=["st6b"])
                    S.add("vector", lambda e: e.bn_aggr(out=mv, in_=st6.rearrange("p a b -> p (a b)")),
                          r=["st6a", "st6b"], w=["mv"])
                    S.add("scalar", lambda e: e.activation(out=rstd, in_=mv[:, 1:2], func=AF.Ln,
                                                           bias=EPS), r=["mv"], w=["rstd"])
                    S.add("scalar", lambda e: e.activation(out=rstd, in_=rstd, func=AF.Exp,
                                                           scale=-0.5), r=["rstd"], w=["rstd"])
                    S.add("vector", lambda e: e.tensor_scalar(
                        out=xt, in0=xt, scalar1=mv[:, 0:1], scalar2=rstd, op0=ALU.subtract,
                        op1=ALU.mult), r=[xk, "mv", "rstd"], w=[xk])
                    for k in range(8):
                        S.add("tensor", lambda e: e.transpose(
                            out=ps[k // 4][:, (k % 4) * 128:(k % 4 + 1) * 128],
                            in_=xt[:, k * 128:(k + 1) * 128], identity=ident_f),
                            r=[xk, "cf32"], w=[f"ps{k // 4}"])
                    for k in range(8):
                        if k < 4:
                            S.add("scalar", lambda e: e.activation(
                                out=hT_[:, k, j * 128:(j + 1) * 128],
                                in_=ps[k // 4][:, (k % 4) * 128:(k % 4 + 1) * 128], func=AF.Identity,
                                scale=modT[:, 8 + k, mc_:mc_ + 1], bias=modT[:, k, mc_:mc_ + 1]),
                                r=[f"ps{k // 4}", "modT"], w=[("hT", hb_, k, j)])
                        else:
                            S.add("vector", lambda e: e.tensor_scalar(
                                out=hT_[:, k, j * 128:(j + 1) * 128],
                                in0=ps[k // 4][:, (k % 4) * 128:(k % 4 + 1) * 128],
                                scalar1=modT[:, 8 + k, mc_:mc_ + 1], scalar2=modT[:, k, mc_:mc_ + 1],
                                op0=ALU.mult, op1=ALU.add),
                                r=[f"ps{k // 4}", "modT"], w=[("hT", hb_, k, j)])

            if gi == 0:
                emit_LN(0)
            HT_R = [("hT", hb, k, j) for k in range(8) for j in range(ntl)]

            def fm_chain(col0, m):
                bank, bk = fm_bank()
                for k in range(8):
                    S.add("tensor", lambda e, k=k, bank=bank: e.matmul(
                        bank[0:m, 0:n], lhsT=win[:, k, col0:col0 + m], rhs=hT[:, k, 0:n],
                        start=(k == 0), stop=(k == 7)),
                        r=WIN_R + HT_R, w=[bk])
                return bank, bk

            if debug and gi == 0 and l == 0:
                dbg_hT = nc.dram_tensor("dbg_hT", [128, 8, 256], BF16, kind="ExternalOutput").ap()
                dma("sync", dbg_hT, hTs[0][:, :, 0:256], HT_R, ["dbg_hT"])
            if a_stage < 2:
                continue
            vtm, vtmk = vtm_rot.get()
            for j in range(ntl):
                for k in range(8):
                    S.add("tensor", lambda e, k=k, j=j: e.matmul(
                        ps[5], lhsT=hT[:, k, j * 128:(j + 1) * 128], rhs=win[:, k, C_HI:C_HI + 512],
                        start=(k == 0), stop=(k == 7)), r=WIN_R + HT_R, w=["ps5"])
                S.add("scalar", lambda e, j=j, vtm=vtm: e.copy(out=vtm[:, j, :], in_=ps[5]),
                      r=["ps5"], w=[(vtmk, j)])
            VTM_R = [(vtmk, j) for j in range(ntl)]

            hq = []
            for c in range(2):
                bank, bk = fm_chain(C_ZQ + c * 128, 128)
                t, tk = ded["hq%d" % c]
                S.add("scalar", lambda e, t=t, bank=bank: e.activation(
                    out=t[:, 0:n], in_=bank[:, 0:n], func=AF.Silu), r=[bk], w=[tk])
                hq.append((t, tk))
            for (c0, dst, coff) in ((C_HG, sgr_d, 0), (C_GG, sgr_d, 2), (C_MG, sgm_d, 0)):
                for c in range(2 if dst is sgr_d else 4):
                    bank, bk = fm_chain(c0 + c * 128, 128)
                    sgt, sgtk = sg_rot.get()
                    S.add("scalar", lambda e, bank=bank, sgt=sgt: e.activation(
                        out=sgt[:, 0:n], in_=bank[:, 0:n], func=AF.Silu), r=[bk], w=[sgtk])
                    nm = "sgr_d" if dst is sgr_d else "sgm_d"
                    dma("sync", dst[:, coff + c, g0:g0 + n], sgt[:, 0:n], [sgtk], [(nm, gi, coff + c)])
            S.capture("B")
            cqn, cqnk = cqn_rot.get()
            zc = []
            for c in range(3):
                bank, bk = fm_chain(C_ZCQ + c * 128, 128)
                t, tk = f32m.get()
                S.add("vector", lambda e, t=t, bank=bank: e.tensor_copy(out=t[:, 0:n],
                                                                        in_=bank[:, 0:n]),
                      r=[bk], w=[tk])
                zc.append((t, tk))
            for which in range(2):
                idx = [0, 1] if which == 0 else [2]
                rank = 256.0 if which == 0 else 128.0
                sqs = []
                for c in idx:
                    sq, sqk = f32m.get()
                    S.add("scalar", lambda e, sq=sq, c=c: e.activation(
                        out=sq[:, 0:n], in_=zc[c][0][:, 0:n], func=AF.Square),
                        r=[zc[c][1]], w=[sqk])
                    sqs.append((sq, sqk))
                bank, bk = fm_bank()
                for i, (sq, sqk) in enumerate(sqs):
                    S.add("tensor", lambda e, sq=sq, i=i, bank=bank: e.matmul(
                        bank[:, 0:n], lhsT=ones_f, rhs=sq[:, 0:n], start=(i == 0),
                        stop=(i == len(sqs) - 1)), r=[sqk, "cf32"], w=[bk])
                rr, rrk = f32m.get()
                S.add("scalar", lambda e, rr=rr, bank=bank, rank=rank: e.activation(
                    out=rr[:, 0:n], in_=bank[:, 0:n], func=AF.Ln, scale=1.0 / rank, bias=EPS),
                    r=[bk], w=[rrk])
                S.add("scalar", lambda e, rr=rr: e.activation(
                    out=rr[:, 0:n], in_=rr[:, 0:n], func=AF.Exp, scale=-0.5), r=[rrk], w=[rrk])
                for c in idx:
                    gcol = V(8 + l * 2 + c) if which == 0 else V(12 + l)
                    S.add("vector", lambda e, c=c, gcol=gcol, rr=rr, cqn=cqn: e.scalar_tensor_tensor(
                        out=cqn[:, c, 0:n], in0=zc[c][0][:, 0:n], scalar=gcol, in1=rr[:, 0:n],
                        op0=ALU.mult, op1=ALU.mult), r=[zc[c][1], rrk, "vecs"], w=[(cqnk, c)])
            if not isctx:
                rC, rCk = ded["rC"]
                rS, rSk = ded["rS"]
                dma("sync", rC[0:64, 0:n], ropeC_d[:, g0 - LCTX:g0 - LCTX + n], [], [rCk])
                dma("sync", rS[0:64, 0:n], ropeS_d[:, g0 - LCTX:g0 - LCTX + n], [], [rSk])

            def rope_evac(bank_a, bka, bank_b, bkb, out_ap, outk, scale):
                if isctx:
                    S.add("scalar", lambda e: e.activation(out=out_ap, in_=bank_a[0:64, 0:n],
                                                           func=AF.Identity, scale=scale),
                          r=[bka], w=[outk])
                    return
                t1, t1k = f32m.get()
                t2, t2k = f32m.get()
                S.add("vector", lambda e: e.tensor_tensor(out=t1[0:64, 0:n], in0=bank_a[0:64, 0:n],
                                                          in1=rC[0:64, 0:n], op=ALU.mult),
                      r=[bka, rCk], w=[t1k])
                S.add("vector", lambda e: e.tensor_tensor(out=t2[0:64, 0:n], in0=bank_b[0:64, 0:n],
                                                          in1=rS[0:64, 0:n], op=ALU.mult),
                      r=[bkb, rSk], w=[t2k])
                S.add("vector", lambda e: e.scalar_tensor_tensor(
                    out=out_ap, in0=t1[0:64, 0:n], scalar=scale, in1=t2[0:64, 0:n],
                    op0=ALU.mult, op1=ALU.add), r=[t1k, t2k], w=[outk])

            krt, krtk = krt_rot.get()
            bank_a, bka = fm_chain(C_KR, 64)
            if not isctx:
                bank_b, bkb = fm_chain(C_KRS, 64)
            else:
                bank_b, bkb = bank_a, bka
            rope_evac(bank_a, bka, bank_b, bkb, krt[0:64, 0:n], krtk, 1.0)
            dma("sync", kr_d[:, g0:g0 + n], krt[0:64, 0:n], [krtk], [("kr_d", gi)])

            for h in range(4):
                qn, qnk = qn_rot.get()
                qr, qrk = qr_rot.get()
                bank, bk = fm_bank()
                for c in range(2):
                    S.add("tensor", lambda e, c=c, h=h, bank=bank: e.matmul(
                        bank[:, 0:n], lhsT=wuq[:, c, h * 128:(h + 1) * 128], rhs=cqn[:, c, 0:n],
                        start=(c == 0), stop=(c == 1)), r=["wuq", (cqnk, c)], w=[bk])
                S.add("scalar", lambda e, h=h, bank=bank, qn=qn: e.activation(
                    out=qn[:, 0:n], in_=bank[:, 0:n], func=AF.Identity, scale=MLA_SCALE),
                    r=[bk], w=[qnk])
                dma("sync", qn_d[:, h, g0:g0 + n], qn[:, 0:n], [qnk], [("qn_d", gi, h)])
                banka, bka = fm_bank()
                for c in range(2):
                    S.add("tensor", lambda e, c=c, h=h, banka=banka: e.matmul(
                        banka[0:64, 0:n], lhsT=wuq[:, c, 512 + h * 128:512 + h * 128 + 64],
                        rhs=cqn[:, c, 0:n], start=(c == 0), stop=(c == 1)),
                        r=["wuq", (cqnk, c)], w=[bka])
                if not isctx:
                    bankb, bkb = fm_bank()
                    for c in range(2):
                        S.add("tensor", lambda e, c=c, h=h, bankb=bankb: e.matmul(
                            bankb[0:64, 0:n], lhsT=wuq[:, c, 512 + h * 128 + 64:512 + h * 128 + 128],
                            rhs=cqn[:, c, 0:n], start=(c == 0), stop=(c == 1)),
                            r=["wuq", (cqnk, c)], w=[bkb])
                    t1, t1k = f32m.get()
                    t2, t2k = f32m.get()
                    S.add("vector", lambda e, t1=t1, banka=banka: e.scalar_tensor_tensor(
                        out=t1[0:64, 0:n], in0=banka[0:64, 0:n], scalar=MLA_SCALE,
                        in1=rC[0:64, 0:n], op0=ALU.mult, op1=ALU.mult), r=[bka, rCk], w=[t1k])
                    S.add("vector", lambda e, t2=t2, bankb=bankb: e.scalar_tensor_tensor(
                        out=t2[0:64, 0:n], in0=bankb[0:64, 0:n], scalar=MLA_SCALE,
                        in1=rS[0:64, 0:n], op0=ALU.mult, op1=ALU.mult), r=[bkb, rSk], w=[t2k])
                    S.add("gpsimd", lambda e, t1=t1, t2=t2, h=h, qr=qr: e.tensor_tensor(
                        out=qr[0:64, 0:n], in0=t1[0:64, 0:n], in1=t2[0:64, 0:n], op=ALU.add),
                        r=[t1k, t2k], w=[qrk])
                else:
                    S.add("scalar", lambda e, h=h, banka=banka, qr=qr: e.activation(
                        out=qr[0:64, 0:n], in_=banka[0:64, 0:n], func=AF.Identity,
                        scale=MLA_SCALE), r=[bka], w=[qrk])
                dma("sync", qr_d[:, h, g0:g0 + n], qr[0:64, 0:n], [qrk], [("qr_d", gi, h)])
            kn, knk = kn_rot.get()
            for h in range(4):
                bank, bk = fm_bank()
                S.add("tensor", lambda e, h=h, bank=bank: e.matmul(
                    bank[:, 0:n], lhsT=wukv[:, h * 128:(h + 1) * 128], rhs=cqn[:, 2, 0:n],
                    start=True, stop=True), r=["wukv", (cqnk, 2)], w=[bk])
                S.add("vector", lambda e, h=h, bank=bank, kn=kn: e.tensor_copy(
                    out=kn[:, h, 0:n], in_=bank[:, 0:n]), r=[bk], w=[(knk, h)])
            dma("sync", kn_d[:, :, g0:g0 + n], kn[:, :, 0:n], [(knk, h) for h in range(4)],
                [("kn_d", gi)])
            vm, vmk = vm_rot.get()
            for j in range(ntl):
                bank, bk = fm_bank()
                S.add("tensor", lambda e, j=j, bank=bank: e.matmul(
                    bank, lhsT=cqn[:, 2, j * 128:(j + 1) * 128], rhs=wukv[:, 512:1024],
                    start=True, stop=True), r=["wukv", (cqnk, 2)], w=[bk])
                S.add("scalar", lambda e, j=j, bank=bank, vm=vm: e.copy(out=vm[:, j, :], in_=bank),
                      r=[bk], w=[(vmk, j)])
            dma("sync", vm_d[:, g0 // 128:g0 // 128 + ntl, :], vm[:, 0:ntl, :],
                [(vmk, j) for j in range(ntl)], [("vm_d", gi)])

            S.capture("A")
            qt, qtk = qt_rot.get()
            kt, ktk = kt_rot.get()

            def prep(mt, dr, q, qk, k_, kk, g, gk, esc):
                P, Pk = f32p.get()
                S.add("vector", lambda e: e.tensor_tensor_scan(
                    out=P[:, 0:n], data0=resetm[:, 0:n], data1=g[:, 0:n], initial=0.0,
                    op0=ALU.mult, op1=ALU.add), r=[gk, "resetm"], w=[Pk])
                P3 = P[:, 0:n].rearrange("p (c t) -> p c t", t=64)
                if dr == 0:
                    b, bk_ = P, Pk
                    mid = 31
                else:
                    b, bk_ = f32p.get()
                    S.add("vector", lambda e: e.tensor_tensor(out=b[:, 0:n], in0=g[:, 0:n],
                                                              in1=P[:, 0:n], op=ALU.subtract),
                          r=[gk, Pk], w=[bk_])
                    b3_ = b[:, 0:n].rearrange("p (c t) -> p c t", t=64)
                    S.add("vector", lambda e: e.tensor_tensor(
                        out=b3_, in0=b3_, in1=P3[:, :, 63:64].broadcast_to((128, nch, 64)),
                        op=ALU.add), r=[bk_, Pk], w=[bk_])
                    mid = 32
                b3 = b[:, 0:n].rearrange("p (c t) -> p c t", t=64)
                d, dk_ = f32p.get()
                d3 = d[:, 0:n].rearrange("p (c t) -> p c t", t=64)
                S.add("gpsimd", lambda e: e.tensor_tensor(
                    out=d3, in0=b3, in1=b3[:, :, mid:mid + 1].broadcast_to((128, nch, 64)),
                    op=ALU.subtract), r=[bk_], w=[dk_])
                import os as _os2
                _cl = 40.0 if _os2.environ.get("KCLAMP") else 80.0
                S.add("gpsimd", lambda e: e.tensor_scalar(out=d[:, 0:n], in0=d[:, 0:n], scalar1=_cl,
                                                           scalar2=-_cl, op0=ALU.min, op1=ALU.max),
                      r=[dk_], w=[dk_])
                Bc = P3[:, :, 63]
                rc = b3[:, :, mid]
                S.add("scalar", lambda e: e.activation(out=A_sb[:, mt, dr, ch0:ch0 + nch], in_=Bc,
                                                       func=AF.Exp, scale=esc),
                      r=[Pk], w=[("A", mt, dr, gi)])
                S.add("scalar", lambda e: e.activation(out=R_sb[:, mt, dr, ch0:ch0 + nch], in_=rc,
                                                       func=AF.Exp, scale=esc),
                      r=[bk_], w=[("R", mt, dr, gi)])
                S.add("vector", lambda e: e.tensor_tensor(out=tmpc[:, 0:nch], in0=Bc, in1=rc,
                                                          op=ALU.subtract),
                      r=[Pk, bk_, "tmpc", "tmpc2", "tmpc3"], w=["tmpc"])
                S.add("scalar", lambda e: e.activation(out=E_sb[:, mt, dr, ch0:ch0 + nch],
                                                       in_=tmpc[:, 0:nch], func=AF.Exp, scale=esc),
                      r=["tmpc"], w=[("E", mt, dr, gi)])
                e1, e1k = f32p.get()
                e2, e2k = f32p.get()
                S.add("scalar", lambda e: e.activation(out=e1[:, 0:n], in_=d[:, 0:n], func=AF.Exp,
                                                       scale=esc), r=[dk_], w=[e1k])
                S.add("scalar", lambda e: e.activation(out=e2[:, 0:n], in_=d[:, 0:n], func=AF.Exp,
                                                       scale=-esc), r=[dk_], w=[e2k])
                S.add("gpsimd", lambda e: e.tensor_tensor(out=qt[:, dr, mt, 0:n], in0=q[:, 0:n],
                                                          in1=e1[:, 0:n], op=ALU.mult),
                      r=[qk, e1k], w=[(qtk, dr, mt)])
                S.add("gpsimd", lambda e: e.tensor_tensor(out=kt[:, dr, mt, 0:n], in0=k_[:, 0:n],
                                                          in1=e2[:, 0:n], op=ALU.mult),
                      r=[kk, e2k], w=[(ktk, dr, mt)])

            for c in range(2):
                for dr in range(2):
                    bank, bk = fm_chain((C_ZFF if dr == 0 else C_ZFB) + c * 128, 128)
                    sg, sgk = f32p.get()
                    gt, gtk = f32p.get()
                    li = dr * 2 + c
                    S.add("scalar", lambda e, sg=sg, bank=bank: e.activation(
                        out=sg[:, 0:n], in_=bank[:, 0:n], func=AF.Exp, scale=-1.0), r=[bk], w=[sgk])
                    S.add("scalar", lambda e, sg=sg, gt=gt: e.activation(
                        out=gt[:, 0:n], in_=sg[:, 0:n], func=AF.Ln, bias=1.0), r=[sgk], w=[gtk])
                    S.add("scalar", lambda e, sg=sg, gt=gt: e.activation(
                        out=sg[:, 0:n], in_=gt[:, 0:n], func=AF.Exp, scale=-1.0), r=[gtk], w=[sgk])
                    S.add("vector", lambda e, sg=sg, gt=gt, li=li: e.tensor_scalar(
                        out=gt[:, 0:n], in0=sg[:, 0:n], scalar1=lbv[:, li, 3:4], scalar2=None,
                        op0=ALU.max), r=[sgk] + LBV_R, w=[gtk])
                    S.add("scalar", lambda e, gt=gt, li=li: e.activation(
                        out=gt[:, 0:n], in_=gt[:, 0:n], func=AF.Ln, scale=lbv[:, li, 1:2],
                        bias=lbv[:, li, 0:1]), r=[gtk] + LBV_R, w=[gtk])
                    S.add("vector", lambda e, sg=sg, li=li: e.tensor_scalar(
                        out=sg[:, 0:n], in0=sg[:, 0:n], scalar1=lbv[:, li, 2:3],
                        scalar2=lbv[:, li, 1:2], op0=ALU.mult, op1=ALU.add),
                        r=[sgk] + LBV_R, w=[sgk])
                    prep(c, dr, hq[c][0], hq[c][1], sg, sgk, gt, gtk, 1.0)
            bank, bk = fm_chain(C_GQ, 128)
            gq, gqk = ded["gq"]
            S.add("scalar", lambda e, gq=gq, bank=bank: e.activation(
                out=gq[:, 0:n], in_=bank[:, 0:n], func=AF.Identity, scale=32.0 ** -0.5),
                r=[bk], w=[gqk])
            bank, bk = fm_chain(C_GK, 128)
            gk_, gkk = ded["gk"]
            S.add("vector", lambda e, gk_=gk_, bank=bank: e.tensor_copy(out=gk_[:, 0:n],
                                                                        in_=bank[:, 0:n]),
                  r=[bk], w=[gkk])
            za, zak = za_rot.get()
            for dr in range(2):
                bank, bk = fm_chain(C_GAF + 16 * dr, 16)
                S.add("vector", lambda e, dr=dr, bank=bank, za=za: e.tensor_copy(
                    out=za[0:16, dr, 0:n], in_=bank[0:16, 0:n]), r=[bk], w=[(zak, dr)])
                bank2, bk2 = fm_bank()
                S.add("tensor", lambda e, dr=dr, bank2=bank2, za=za: e.matmul(
                    bank2[:, 0:n], lhsT=wa2[0:16, dr, :], rhs=za[0:16, dr, 0:n], start=True,
                    stop=True), r=["wa2", (zak, dr)], w=[bk2])
                gg, ggk = f32p.get()
                S.add("scalar", lambda e, dr=dr, bank2=bank2, gg=gg: e.activation(
                    out=gg[:, 0:n], in_=bank2[:, 0:n], func=AF.Exp, scale=-1.0,
                    bias=negb[:, dr:dr + 1]), r=[bk2, "negb"], w=[ggk])
                S.add("scalar", lambda e, gg=gg: e.activation(out=gg[:, 0:n], in_=gg[:, 0:n],
                                                               func=AF.Ln, bias=1.0), r=[ggk], w=[ggk])
                prep(2, dr, gq, gqk, gk_, gkk, gg, ggk, -1.0 / 16.0)
            QT_R = [(qtk, dr, mt) for dr in range(2) for mt in range(3)]
            KT_R = [(ktk, dr, mt) for dr in range(2) for mt in range(3)]
            if gi + 1 < min(len(GROUPS), a_groups):
                S.capture("L")
                emit_LN(gi + 1)
            S.flush_interleaved(["A", "B", "L"], blk=6)
            dma("sync", qt_d[:, :, :, g0:g0 + n], qt[:, :, :, 0:n], QT_R, [("qt_d", gi)])

            if a_stage < 4:
                continue
            for j in range(ntl):
                tk0 = j * 128
                ktm, ktmk = ktm_rot.get()
                for dr in range(2):
                    for mt in range(3):
                        slot = dr * 3 + mt
                        S.add("tensor", lambda e, dr=dr, mt=mt, slot=slot: e.transpose(
                            out=ps5b[:, slot * 128:(slot + 1) * 128],
                            in_=kt[:, dr, mt, tk0:tk0 + 128], identity=ident_b),
                            r=[(ktk, dr, mt), "cbf0"], w=["ps5"])
                S.add("vector", lambda e, ktm=ktm: e.tensor_copy(
                    out=ktm.rearrange("p a b c -> p (a b c)"), in_=ps5b[:, 0:768]),
                    r=["ps5"], w=[ktmk])
                oin, oink = oin_rot.get()
                Ut, Utk = U_rot.get()
                pTH, pTHk = pTH_rot.get()
                pTG, pTGk = pTG_rot.get()
                GB = (6, 2, 3, 4)

                def hg_st(dr, h):
                    hh, pair = h % 2, h // 2
                    bi = 6 if hh == 0 else 3
                    col = (dr * 2 + pair) * 128
                    S.add("tensor", lambda e: e.matmul(
                        ps[bi][:, col:col + 128], lhsT=kt[64 * hh:64 * hh + 64, dr, pair, tk0:tk0 + 128],
                        rhs=qt[64 * hh:64 * hh + 64, dr, pair, tk0:tk0 + 128], start=True, stop=True,
                        tile_position=(64 * hh, 0)),
                        r=[(ktk, dr, pair), (qtk, dr, pair)], w=[f"ps{bi}"])

                def gla_st(dr, h):
                    bi = GB[h]
                    col = dr * 128
                    S.add("tensor", lambda e: e.matmul(
                        ps[bi][:, col:col + 128], lhsT=kt[32 * h:32 * h + 32, dr, 2, tk0:tk0 + 128],
                        rhs=qt[32 * h:32 * h + 32, dr, 2, tk0:tk0 + 128], start=True, stop=True,
                        tile_position=(32 * h, 0)),
                        r=[(ktk, dr, 2), (qtk, dr, 2)], w=[f"ps{bi}"])

                def hg_evac(hh):
                    bi = 6 if hh == 0 else 3
                    S.add("vector", lambda e: e.copy_predicated(
                        out=pTH[:, hh, :, :].rearrange("p a c -> p (a c)"),
                        mask=maskH_u.rearrange("p a c -> p (a c)"),
                        data=ps[bi]),
                        r=[f"ps{bi}", ("mask4", 0), ("mask4", 1), (pTHk, hh)], w=[(pTHk, hh)])

                def gla_evac(h):
                    bi = GB[h]
                    S.add("vector", lambda e: e.copy_predicated(
                        out=pTG[:, h, :, :].rearrange("p a c -> p (a c)"),
                        mask=maskG_u.rearrange("p a c -> p (a c)"),
                        data=ps[bi][:, 0:256]),
                        r=[f"ps{bi}", "cf32", (pTGk, h)], w=[(pTGk, h)])

                for dr in range(2):
                    for h in range(4):
                        hg_st(dr, h)
                for dr in range(2):
                    gla_st(dr, 1)
                    gla_st(dr, 3)
                hg_evac(0)
                hg_evac(1)
                for dr in range(2):
                    gla_st(dr, 0)
                    gla_st(dr, 2)
                gla_evac(1)
                gla_evac(3)
                gla_evac(0)
                gla_evac(2)
                first_q = [True, True]
                n_oi = [0]
                for mix in range(2):
                    for dr in range(2):
                        for h in range(4):
                            hh = h % 2
                            ptile = mix * 2 + h // 2
                            vcol = mix * 256 + h * 64
                            st = first_q[hh]
                            first_q[hh] = False
                            n_oi[0] += 1
                            if mix == 0:
                                rhs_ap, rk = pTH[:, hh, dr * 2 + h // 2, :], (pTHk, hh)
                            else:
                                rhs_ap, rk = pTG[:, h, dr, :], (pTGk, h)
                            S.add("tensor", lambda e: e.matmul(
                                ps[7][64 * hh:64 * hh + 64, ptile * 128:(ptile + 1) * 128],
                                lhsT=vtm[:, j, vcol:vcol + 64], rhs=rhs_ap, start=st,
                                stop=(n_oi[0] == 16), skip_group_check=True,
                                tile_position=(0, 64 * hh)),
                                r=[rk, (vtmk, j)], w=["ps7"])
                for cc_ in range(2):
                    ub = ps[cc_]
                    ubk = f"ps{cc_}"
                    for dr in range(2):
                        for mt in range(3):
                            col = (dr * 3 + mt) * 64
                            for hx in range(2 if mt < 2 else 4):
                                if mt < 2:
                                    p0, dk, vcol = 64 * hx, 64, (2 * mt + hx) * 64
                                else:
                                    p0, dk, vcol = 32 * hx, 32, 256 + hx * 64
                                S.add("tensor", lambda e: e.matmul(
                                    ub[p0:p0 + dk, col:col + 64],
                                    lhsT=ktm[cc_ * 64:(cc_ + 1) * 64, dr, mt, p0:p0 + dk],
                                    rhs=vtm[cc_ * 64:(cc_ + 1) * 64, j, vcol:vcol + 64],
                                    start=True, stop=True, tile_position=(cc_ * 64, p0)),
                                    r=[ktmk, (vtmk, j)], w=[ubk])
                    cidx = ch0 + 2 * j + cc_
                    for dr in range(2):
                        S.add("vector", lambda e: e.tensor_tensor(
                            out=Ut[:, dr, :, cc_ * 64:(cc_ + 1) * 64],
                            in0=ub[:, dr * 192:(dr + 1) * 192].rearrange("p (a v) -> p a v", a=3, v=64),
                            in1=E_sb[:, :, dr, cidx:cidx + 1].broadcast_to((128, 3, 64)),
                            op=ALU.mult),
                            r=[ubk] + [("E", mt, dr, gi) for mt in range(3)], w=[(Utk, dr, cc_)])
                S.add("scalar", lambda e, oin=oin: e.copy(
                    out=oin.rearrange("p a b -> p (a b)"), in_=ps[7]), r=["ps7"], w=[oink])
                dma("sync", oin_d[:, :, g0 + tk0:g0 + tk0 + 128], oin, [oink], [("oin_d", gi, j)])
                cidx = ch0 + 2 * j
                dma("sync", U_d[:, :, :, cidx:cidx + 2, :],
                    Ut.rearrange("p a b (c v) -> p a b c v", c=2, v=64),
                    [(Utk, a_, b_) for a_ in range(2) for b_ in range(2)], [("U_d", gi, j)])
        S.barrier()
        if stop_after == "A":
            break

        ar.reset()
        U_sb = ar.alloc([128, 2, 3, NCH, 64], F32)
        snap = ar.alloc([128, 2, 3, NCH, 64], BF16)
        Sst = ar.alloc([128, 2, 3, 2, 64], F32)
        for dr in range(2):
            for mt in range(3):
                dma("sync", U_sb[:, dr, mt, :, :], U_d[:, dr, mt, :, :],
                    [("U_d", gi, j) for gi, (g0, n) in enumerate(GROUPS) for j in range(n // 128)],
                    [("U_sb", dr, mt)])
        S.add("vector", lambda e: e.memset(Sst.rearrange("p a b c d -> p (a b c d)"), 0.0),
              w=[("Sst", dr, mt, pp) for dr in range(2) for mt in range(3) for pp in range(2)])
        order = [list(range(NCH)), [3, 2, 1, 0] + list(range(NCH - 1, 3, -1))]
        AER = [(nm, mt, dr, gi) for nm in ("A", "E", "R") for mt in range(3) for dr in range(2)
               for gi in range(len(GROUPS))]
        for step in range(NCH):
            pp = step % 2
            for dr in range(2):
                c = order[dr][step]
                for mt in range(3):
                    S.add("scalar", lambda e, dr=dr, mt=mt, c=c, pp=pp: e.activation(
                        out=snap[:, dr, mt, c, :], in_=Sst[:, dr, mt, pp, :], func=AF.Identity,
                        scale=R_sb[:, mt, dr, c:c + 1]),
                        r=[("Sst", dr, mt, pp)] + (AER if step == 0 else []),
                        w=[("snap", dr, mt, c)])
                    if step < NCH - 1:
                        S.add("vector", lambda e, dr=dr, mt=mt, c=c, pp=pp: e.scalar_tensor_tensor(
                            out=Sst[:, dr, mt, 1 - pp, :], in0=Sst[:, dr, mt, pp, :],
                            scalar=A_sb[:, mt, dr, c:c + 1], in1=U_sb[:, dr, mt, c, :],
                            op0=ALU.mult, op1=ALU.add),
                            r=[("Sst", dr, mt, pp), ("U_sb", dr, mt)] + (AER if step == 0 else []),
                            w=[("Sst", dr, mt, 1 - pp)])
        dma("sync", snap_d, snap, [("snap", dr, mt, c) for dr in range(2) for mt in range(3)
                                   for c in range(NCH)], ["snap_d"])
        S.barrier()
        if stop_after == "S":
            break

        ar.reset()
        KnT = ar.alloc([128, 4, T], BF16)
        KrT = ar.alloc([128, T], BF16)
        Vsb = ar.alloc([128, NT, 512], BF16)
        wout = ar.alloc([128, 8, 1024], BF16)
        for h in range(4):
            dma("sync", KnT[:, h, :], kn_d[:, h, :], [("kn_d", gi) for gi in range(9)],
                [("KnT", h)])
        dma("sync", KrT[0:64, :], kr_d, [("kr_d", gi) for gi in range(9)], ["KrT"])
        for q4 in range(2):
            dma("sync", Vsb[:, q4 * 17:(q4 + 1) * 17, :], vm_d[:, q4 * 17:(q4 + 1) * 17, :],
                [("vm_d", gi) for gi in range(9)], [("Vsb", q4)])
        woutv = wout_d[l].rearrange("(k p) n -> p k n", p=128)
        for k in range(8):
            dma("gpsimd", wout[:, k, :], woutv[:, k, :], [], [("wout", k)])
        WOUT_R = [("wout", k) for k in range(8)]
        KV_R = [("KnT", h) for h in range(4)] + ["KrT", ("Vsb", 0), ("Vsb", 1)]
        qn_rot = Rot(ar, "cqn", 5, [128, 512], BF16)
        qr_rot = Rot(ar, "cqr", 5, [128, 512], BF16)
        sgm_rot = Rot(ar, "csgm", 1, [128, 4, 512], BF16)
        sgr_rot = Rot(ar, "csgr", 1, [128, 4, 512], BF16)
        oin_rot = Rot(ar, "coin", 1, [128, 4, 512], F32)
        qt_rot = Rot(ar, "cqt", 1, [128, 2, 3, 512], BF16)
        sn_rot = Rot(ar, "csn", 1, [128, 2, 3, 8, 64], BF16)
        pT_rot = Rot(ar, "cpT", 4, [128, 512], BF16)
        yT_rot = Rot(ar, "yT", 2, [128, 8, 512], BF16)
        f32p = Rot(ar, "cf32p", 4, [128, 512], F32)
        xt_rot = Rot(ar, "cxt", 1, [128, 1024], F32)
        zt_rot = Rot(ar, "czt", 2, [128, 1024], F32)
        sbank_i = [0]
        obank_i = [0]

        def outproj_steps(gi, g0, n, yT, yTk):
            mc = 1 if gi == 0 else 0
            YT_R = [(yTk, k) for k in range(8)]
            steps = []
            for j in range(n // 128):
                def step(j=j):
                    tok0 = g0 + j * 128
                    xt, xk = xt_rot.get()
                    zt, zk = zt_rot.get()
                    dma("sync", xt, xsrc[tok0:tok0 + 128, :], [(xsrc_key, tok0 // 128)], [xk])
                    for hf in range(2):
                        ob = ps[3 if hf == 0 else 7]
                        obk = "ps3" if hf == 0 else "ps7"
                        for k in range(8):
                            S.add("tensor", lambda e, k=k: e.matmul(
                                ob, lhsT=yT[:, k, j * 128:(j + 1) * 128],
                                rhs=wout[:, k, hf * 512:(hf + 1) * 512], start=(k == 0),
                                stop=(k == 7)), r=YT_R + WOUT_R, w=[obk])
                        S.add("vector", lambda e: e.tensor_tensor(
                            out=zt[:, hf * 512:(hf + 1) * 512], in0=ob,
                            in1=gb[:, mc, hf * 512:(hf + 1) * 512], op=ALU.mult),
                            r=[obk, ("gb", mc, hf)], w=[(zk, hf)])
                        S.add("vector", lambda e: e.scalar_tensor_tensor(
                            out=zt[:, hf * 512:(hf + 1) * 512], in0=xt[:, hf * 512:(hf + 1) * 512],
                            scalar=ALPHA, in1=zt[:, hf * 512:(hf + 1) * 512], op0=ALU.mult,
                            op1=ALU.add), r=[(zk, hf), xk], w=[(zk, hf)])
                    S.add("vector", lambda e: e.bn_stats(out=st6[:, 0, :], in_=zt[:, 0:512]),
                          r=[(zk, 0)], w=["st6a"])
                    S.add("vector", lambda e: e.bn_stats(out=st6[:, 1, :], in_=zt[:, 512:1024]),
                          r=[(zk, 1)], w=["st6b"])
                    S.add("vector", lambda e: e.bn_aggr(out=mv, in_=st6.rearrange("p a b -> p (a b)")),
                          r=["st6a", "st6b"], w=["mv"])
                    S.add("scalar", lambda e: e.activation(out=rstd, in_=mv[:, 1:2], func=AF.Ln,
                                                           bias=EPS), r=["mv"], w=["rstd"])
                    S.add("scalar", lambda e: e.activation(out=rstd, in_=rstd, func=AF.Exp,
                                                           scale=-0.5), r=["rstd"], w=["rstd"])
                    S.add("vector", lambda e: e.tensor_scalar(
                        out=zt, in0=zt, scalar1=mv[:, 0:1], scalar2=rstd, op0=ALU.subtract,
                        op1=ALU.mult), r=[(zk, 0), (zk, 1), "mv", "rstd"], w=[(zk, 0), (zk, 1)])
                    S.add("gpsimd", lambda e: e.tensor_tensor(
                        out=zt, in0=zt, in1=lnbc[:, 0, :], op=ALU.mult),
                        r=[(zk, 0), (zk, 1), "lnbc0"], w=[(zk, 0), (zk, 1)])
                    S.add("gpsimd", lambda e: e.tensor_tensor(
                        out=zt, in0=zt, in1=lnbc[:, 1, :], op=ALU.add),
                        r=[(zk, 0), (zk, 1), "lnbc1"], w=[(zk, 0), (zk, 1)])
                    if last:
                        dma("sync", out_d[tok0 - LCTX:tok0 - LCTX + 128, :], zt, [(zk, 0), (zk, 1)],
                            [("out", tok0 // 128)])
                    else:
                        dma("sync", x1_d[tok0:tok0 + 128, :], zt, [(zk, 0), (zk, 1)],
                            [("x1", tok0 // 128)])
                steps.append(step)
            return steps

        pending_tail = []
        for gi, (g0, n) in enumerate(GROUPS):
            isctx = gi == 0
            if isctx and last:
                continue
            ntl = n // 128
            nch = n // 64
            ch0 = g0 // 64
            kt_lo, kt_hi = (0, 2) if isctx else (0, NT)
            qh = []
            for h_ in range(4):
                qn_, qnk_ = qn_rot.get()
                qr_, qrk_ = qr_rot.get()
                dma("sync", qn_[:, 0:n], qn_d[:, h_, g0:g0 + n], [("qn_d", gi, h_)], [qnk_])
                dma("sync", qr_[0:64, 0:n], qr_d[:, h_, g0:g0 + n], [("qr_d", gi, h_)], [qrk_])
                qh.append((qn_, qnk_, qr_, qrk_))
            sgm, sgmk = sgm_rot.get()
            sgr, sgrk = sgr_rot.get()
            oin, oink = oin_rot.get()
            qt, qtk = qt_rot.get()
            sn, snk = sn_rot.get()
            yT, yTk = yT_rot.get()
            dma("sync", qt[:, :, :, 0:n], qt_d[:, :, :, g0:g0 + n], [("qt_d", gi)], [qtk])
            for dr in range(2):
                dma("sync", sn[:, dr, :, 0:nch, :], snap_d[:, dr, :, ch0:ch0 + nch, :],
                    ["snap_d"], [(snk, dr)])
            dma("sync", oin[:, :, 0:n], oin_d[:, :, g0:g0 + n],
                [("oin_d", gi, j) for j in range(ntl)], [oink])
            dma("sync", sgr[:, :, 0:n], sgr_d[:, :, g0:g0 + n], [("sgr_d", gi, c_) for c_ in range(4)], [sgrk])
            dma("sync", sgm[:, :, 0:n], sgm_d[:, :, g0:g0 + n], [("sgm_d", gi, c_) for c_ in range(4)], [sgmk])

            deferred = []
            for ptile in range(4):
                def ro_mm(ptile=ptile):
                    rb = (3, 7)
                    firsts = [True, True]
                    cnt = [0, 0]
                    for cidx in range(nch):
                        for dr in range(2):
                            for hh in range(2):
                                if ptile < 2:
                                    mt, p0, dk = ptile, 64 * hh, 64
                                else:
                                    mt, p0, dk = 2, 32 * (2 * (ptile - 2) + hh), 32
                                st = firsts[hh]
                                firsts[hh] = False
                                cnt[hh] += 1
                                S.add("tensor", lambda e: e.matmul(
                                    ps[rb[hh]][64 * hh:64 * hh + 64, cidx * 64:(cidx + 1) * 64],
                                    lhsT=sn[p0:p0 + dk, dr, mt, cidx, :],
                                    rhs=qt[p0:p0 + dk, dr, mt, cidx * 64:(cidx + 1) * 64],
                                    start=st, stop=(cnt[hh] == nch * 2), skip_group_check=True,
                                    tile_position=(p0, 64 * hh)),
                                    r=[(snk, dr), qtk], w=[f"ps{rb[hh]}"])
                    o, ok_ = f32p.get()
                    for hh in range(2):
                        S.add("vector", lambda e: e.tensor_tensor(
                            out=o[64 * hh:64 * hh + 64, 0:n], in0=ps[rb[hh]][64 * hh:64 * hh + 64, 0:n],
                            in1=oin[64 * hh:64 * hh + 64, ptile, 0:n], op=ALU.add),
                            r=[f"ps{rb[hh]}", oink] + ([ok_] if hh == 1 else []), w=[ok_])
                    sq, sqk = f32p.get()
                    S.add("scalar", lambda e: e.activation(out=sq[:, 0:n], in_=o[:, 0:n],
                                                           func=AF.Square), r=[ok_], w=[sqk])

                    def ro_ss():
                        S.add("tensor", lambda e: e.matmul(
                            ps[3][:, 0:n], lhsT=blk_f, rhs=sq[:, 0:n], start=True, stop=True),
                            r=[sqk, "cf32"], w=["ps3"])
                        S.add("scalar", lambda e: e.activation(
                            out=sq[:, 0:n], in_=ps[3][:, 0:n], func=AF.Ln, scale=1.0 / 64.0, bias=EPS),
                            r=["ps3"], w=[sqk])
                        S.add("scalar", lambda e: e.activation(out=sq[:, 0:n], in_=sq[:, 0:n],
                                                               func=AF.Exp, scale=-0.5),
                              r=[sqk], w=[sqk])
                        gcol = V(18 + l) if ptile < 2 else V(20 + l)
                        S.add("vector", lambda e: e.scalar_tensor_tensor(
                            out=o[:, 0:n], in0=o[:, 0:n], scalar=gcol, in1=sq[:, 0:n], op0=ALU.mult,
                            op1=ALU.mult), r=[ok_, sqk, "vecs"], w=[ok_])
                        ychunk = ptile if ptile < 2 else 4 + ptile
                        S.add("gpsimd", lambda e: e.tensor_tensor(
                            out=yT[:, ychunk, 0:n], in0=o[:, 0:n], in1=sgr[:, ptile, 0:n],
                            op=ALU.mult), r=[ok_, sgrk], w=[(yTk, ychunk)])
                    return ro_ss
                deferred.append(ro_mm)

            LA = 2
            nk = kt_hi - kt_lo
            iters = [(h, ki, kt_lo + ki) for h in range(4) for ki in range(nk)]
            hbanks = {}
            for h in range(4):
                par = obank_i[0] % 2
                obank_i[0] += 1
                hbanks[h] = (ps[4 + par], f"ps{4 + par}")
            pend = {}

            def emit_scores(it):
                h, ki, ktile = it
                sb = ps[sbank_i[0] % 3]
                sbk = f"ps{sbank_i[0] % 3}"
                sbank_i[0] += 1
                qn, qnk, qr, qrk = qh[h]
                S.add("tensor", lambda e: e.matmul(
                    sb[:, 0:n], lhsT=KnT[:, h, ktile * 128:(ktile + 1) * 128], rhs=qn[:, 0:n],
                    start=True, stop=False), r=[("KnT", h), qnk], w=[sbk])
                S.add("tensor", lambda e: e.matmul(
                    sb[:, 0:n], lhsT=KrT[0:64, ktile * 128:(ktile + 1) * 128],
                    rhs=qr[0:64, 0:n], start=False, stop=True), r=["KrT", qrk], w=[sbk])
                pT, pTk = pT_rot.get()
                S.add("scalar", lambda e: e.activation(
                    out=pT[:, 0:n], in_=sb[:, 0:n], func=AF.Exp), r=[sbk], w=[pTk])
                pend[it] = (pT, pTk)

            def emit_pv(it):
                h, ki, ktile = it
                pT, pTk = pend.pop(it)
                ob, obk = hbanks[h]
                S.add("tensor", lambda e: e.matmul(
                    ob[:, 0:n], lhsT=Vsb[:, ktile, h * 128:(h + 1) * 128], rhs=pT[:, 0:n],
                    start=(ki == 0), stop=(ki == nk - 1)),
                    r=[pTk, ("Vsb", ktile // 17)], w=[obk])
                S.add("tensor", lambda e: e.matmul(
                    ps[6][:, 0:n], lhsT=ones_b, rhs=pT[:, 0:n], start=(ki == 0),
                    stop=(ki == nk - 1)), r=[pTk, "cbf1"], w=["ps6"])
                if ki == nk - 1:
                    rd, rdk = f32p.get()
                    S.add("vector", lambda e: e.tensor_copy(out=rd[:, 0:n], in_=ps[6][:, 0:n]),
                          r=["ps6"], w=[rdk])
                    S.add("vector", lambda e: e.reciprocal(out=rd[:, 0:n], in_=rd[:, 0:n]),
                          r=[rdk], w=[rdk])
                    S.add("vector", lambda e: e.tensor_tensor(
                        out=rd[:, 0:n], in0=ob[:, 0:n], in1=rd[:, 0:n], op=ALU.mult),
                        r=[obk, rdk], w=[rdk])
                    S.add("gpsimd", lambda e: e.tensor_tensor(
                        out=yT[:, 2 + h, 0:n], in0=rd[:, 0:n], in1=sgm[:, h, 0:n], op=ALU.mult),
                        r=[rdk, sgmk], w=[(yTk, 2 + h)])

            extra = {}
            nit = len(iters)
            ro_pos = [0, 3, 6, 9]
            ss_list = []
            if nit < 40:
                for ro in deferred:
                    ro()()
            else:
                for k_, ro in enumerate(deferred):
                    extra.setdefault(ro_pos[k_], []).append(("ro", ro))
            tail_steps = pending_tail
            pending_tail = []
            if tail_steps:
                _T0 = 60
                gap = max(1, (nit - _T0 - 6) // len(tail_steps))
                for k_, stp in enumerate(tail_steps):
                    extra.setdefault(min(_T0 + k_ * gap, nit - 1), []).append(("tail", stp))
            for i in range(nit + LA):
                if i < nit:
                    emit_scores(iters[i])
                if i >= LA:
                    emit_pv(iters[i - LA])
                for kind, fn_ in list(extra.get(i, [])):
                    if kind == "ro":
                        extra.setdefault(i + 5, []).append(("ss", fn_()))
                    else:
                        fn_()
            for ss in ss_list:
                ss()
            pending_tail = outproj_steps(gi, g0, n, yT, yTk)
        for stp in pending_tail:
            stp()
        S.barrier()

    S.finalize(nc, stack)
    blk = stack.enter_context(nc.Block())
    S.emit(blk)
    stack.close()
    return nc


def _perm_w_in():
    o = dict(hq=0, hff=256, hfb=512, hi=768, hgate=1024, mcq=1280, mckv=1536, mkr=1664,
             mgate=1728, gq=2240, gk=2368, gv=2496, gaf=2752, gab=2768, ggate=2784)
    r = np.arange
    p = np.arange(64)
    a, hf, f = p // 32, (p // 16) % 2, p % 16
    partner = a * 32 + (1 - hf) * 16 + f
    idx = np.concatenate([
        r(o["hq"], o["hq"] + 256), r(o["hff"], o["hff"] + 256), r(o["hfb"], o["hfb"] + 256),
        r(o["hgate"], o["hgate"] + 256), r(o["mcq"], o["mcq"] + 256), r(o["mckv"], o["mckv"] + 128),
        r(o["mkr"], o["mkr"] + 64), o["mkr"] + partner, r(o["mgate"], o["mgate"] + 512),
        r(o["gq"], o["gq"] + 128), r(o["gk"], o["gk"] + 128), r(o["ggate"], o["ggate"] + 256),
        r(o["gaf"], o["gaf"] + 16), r(o["gab"], o["gab"] + 16), r(o["hi"], o["hi"] + 256),
        r(o["gv"], o["gv"] + 256)])
    assert idx.size == WCOLS
    return idx, partner


def _consts():
    pos = np.arange(NLAT)
    pos_r = (pos // 64).astype(np.float32)
    pos_c = (pos % 64).astype(np.float32)
    inv = (1.0 / (np.float32(10000.0) ** (np.arange(16, dtype=np.float32) / np.float32(16)))
           ).astype(np.float32)
    p = np.arange(64)
    a, hf, f = p // 32, (p // 16) % 2, p % 16
    posa = np.where(a[:, None] == 0, pos_r[None, :], pos_c[None, :]).astype(np.float32)
    ang = (posa * inv[f][:, None]).astype(np.float32)
    ropeC = np.cos(ang).astype(np.float32)
    sgn = np.where(hf == 0, -1.0, 1.0).astype(np.float32)
    ropeS = (np.sin(ang).astype(np.float32) * sgn[:, None]).astype(np.float32)
    cf = np.zeros((128, 6, 128), np.float32)
    i = np.arange(128)
    cf[:, 0, :] = np.eye(128, dtype=np.float32)
    cf[:, 1, :] = 1.0
    cf[:, 2, :] = (i[:, None] // 64 == i[None, :] // 64)
    same = (i[:, None] // 64 == i[None, :] // 64)
    cf[:, 3, :] = same & (i[:, None] <= i[None, :])
    cf[:, 4, :] = same & (i[:, None] >= i[None, :])
    resetm = np.ones((128, 512), np.float32)
    resetm[:, ::64] = 0.0
    return ropeC, ropeS, cf, resetm


_NC_CACHE = {}


def _prep_shared(inp):
    idx, partner = _perm_w_in()
    w_in_p = np.ascontiguousarray(inp["w_in"][:, :, idx])
    wuq = inp["mla_w_uq"].reshape(2, 256, 4, 192)
    rope = wuq[..., 128:]
    w_uq_p = np.concatenate(
        [wuq[..., :128].reshape(2, 256, 512),
         np.concatenate([rope, rope[..., partner]], axis=-1).reshape(2, 256, 512)], axis=-1)
    wukv = inp["mla_w_ukv"].reshape(2, 128, 4, 256)
    w_ukv_p = np.concatenate([wukv[..., :128].reshape(2, 128, 512),
                              wukv[..., 128:].reshape(2, 128, 512)], axis=-1)
    vecs = np.zeros((128, NVEC), np.float32)
    p = np.arange(128)
    lbl = inp["hg_lb_logits"]
    for l in range(2):
        for d in range(2):
            for c in range(2):
                vecs[:, l * 4 + d * 2 + c] = lbl[l, d, c * 128 + p]
        for c in range(2):
            vecs[:, 8 + l * 2 + c] = inp["mla_q_norm_g"][l, c * 128 + p]
        vecs[:, 12 + l] = inp["mla_kv_norm_g"][l, p]
        for d in range(2):
            vecs[:, 14 + l * 2 + d] = inp["gla_b_a"][l, d, p]
        vecs[:, 18 + l] = inp["hg_norm_g"][l, p % 64]
        vecs[:, 20 + l] = inp["gla_norm_g"][l, p % 64]
        for ci in range(24):
            vecs[:, 22 + 24 * l + ci] = inp["b_mod"][l, ci * 128 + p]
    ropeC, ropeS, cf, resetm = _consts()
    return dict(w_mod=np.ascontiguousarray(inp["w_mod"]), b_mod=np.ascontiguousarray(inp["b_mod"]),
                w_in=w_in_p, w_out=np.ascontiguousarray(inp["w_out"]),
                w_uq=np.ascontiguousarray(w_uq_p), w_ukv=np.ascontiguousarray(w_ukv_p),
                w_a2=np.ascontiguousarray(inp["gla_w_a2"]), ln_g=np.ascontiguousarray(inp["ln_g"]),
                ln_b=np.ascontiguousarray(inp["ln_b"]), vecs=vecs, ropeC=ropeC, ropeS=ropeS,
                cf32=cf, resetm=resetm)


def _per_core(inp, b):
    xin = np.ascontiguousarray(np.concatenate([inp["ctx"][b], inp["x"][b]], axis=0))
    cc = np.stack([inp["c"][b].reshape(8, 128).T, inp["c_ctx"].reshape(8, 128).T], axis=-1)
    return dict(xin=xin, cc=np.ascontiguousarray(cc.astype(np.float32)))


def kernel(**inputs):
    inp = {k: np.asarray(v, dtype=np.float32) for k, v in inputs.items()}
    if "nc" not in _NC_CACHE:
        _NC_CACHE["nc"] = build_program()
    nc = _NC_CACHE["nc"]
    shared = _prep_shared(inp)
    in_maps = []
    for b in range(8):
        m = dict(shared)
        m.update(_per_core(inp, b))
        in_maps.append(m)
    res = run_bass_kernel_spmd(nc, in_maps, core_ids=list(range(8)))
    out = np.stack([np.asarray(r["out"], dtype=np.float32) for r in res.results], axis=0)
    return out
```
